# Optimizing a Trainium2 kernel written in Bass

```python
import jax, jax.numpy as jnp
from jax import lax
import numpy as np

D_MODEL = 1024
BATCH = 16
SEQ = 256
DEPTH = 4
DEC_BATCH = 2
DEC_SEQ = 2048
PAST_LEN = 256

GRID_W = 64
EPS = 1e-6
NEG_INF = -1e30
NA_HEADS = 8
NA_HEAD_DIM = 64
NA_WIDTH = NA_HEADS * NA_HEAD_DIM
NA_WIN_H = 8
NA_WIN_W = 16
NA_BAND_W = 2 * NA_WIN_W
LRU_WIDTH = 512
LRU_BLOCKS = 8
LRU_BLOCK = LRU_WIDTH // LRU_BLOCKS
LRU_CONV = 4
LRU_C = 8.0
CONV_WIDTH = 512
CONV_K = 3
SPLIT_SIZES = (NA_WIDTH,) * 4 + (LRU_WIDTH,) * 2 + (CONV_WIDTH,) * 4 + (D_MODEL,) * 3
IN_COLS = sum(SPLIT_SIZES)
SPLIT_IDX = tuple(np.cumsum(SPLIT_SIZES)[:-1].tolist())

kernel_name = 'hybrid_dit_natten_rglru_shortconv_step'


def _rmsnorm(x, g):
    x32 = x.astype(jnp.float32)
    y = x32 * lax.rsqrt(jnp.mean(x32 * x32, axis=-1, keepdims=True) + EPS)
    return y.astype(x.dtype) * g


def _dwconv(x, w, left, right):
    K = w.shape[0]
    L = x.shape[1]
    xp = jnp.pad(x, ((0, 0), (left, right), (0, 0)))
    y = w[0] * xp[:, 0:L]
    for j in range(1, K):
        y = y + w[j] * xp[:, j:j + L]
    return y


def _lin_combine(e1, e2):
    a1, b1 = e1
    a2, b2 = e2
    return a1 * a2, a2 * b1 + b2


def _rglru(u, wa, ba, wx, bx, lam, h0, reverse):
    bsz, L, W = u.shape
    ub = u.reshape(bsz, L, LRU_BLOCKS, LRU_BLOCK)
    r = jax.nn.sigmoid(jnp.einsum('blnj,njk->blnk', ub, wa).reshape(bsz, L, W) + ba)
    i = jax.nn.sigmoid(jnp.einsum('blnj,njk->blnk', ub, wx).reshape(bsz, L, W) + bx)
    log_a = (-LRU_C * r.astype(jnp.float32)) * jax.nn.softplus(-lam.astype(jnp.float32))
    a = jnp.exp(log_a)
    b = jnp.sqrt(-jnp.expm1(2.0 * log_a)) * (i * u).astype(jnp.float32)
    a_cum, h = lax.associative_scan(_lin_combine, (a, b), axis=1, reverse=reverse)
    if h0 is not None:
        h = h + a_cum * h0.astype(jnp.float32)[:, None, :]
    h_last = h[:, 0] if reverse else h[:, -1]
    return h.astype(u.dtype), h_last.astype(u.dtype)


def _context_attention(q, k, v):
    s = jnp.einsum('bqhd,bkhd->bhqk', q, k).astype(jnp.float32) * (NA_HEAD_DIM ** -0.5)
    p = jax.nn.softmax(s, axis=-1).astype(v.dtype)
    o = jnp.einsum('bhqk,bkhd->bqhd', p, v)
    return o.reshape(q.shape[0], q.shape[1], NA_WIDTH)


def _neighbourhood_attention(q, k, v, ctx_k, ctx_v, rpb):
    bsz, L, H, Dh = q.shape
    rows = L // GRID_W
    kh = min(NA_WIN_H, rows)
    kw, kb = NA_WIN_W, NA_BAND_W
    ncb = GRID_W // kw
    r = np.arange(rows)
    row_idx = np.clip(r - kh // 2, 0, rows - kh)[:, None] + np.arange(kh)[None, :]
    win_start = np.clip(np.arange(GRID_W) - kw // 2, 0, GRID_W - kw)
    q_col = (np.arange(ncb) * kw)[:, None] + np.arange(kw)[None, :]
    band_idx = np.clip(np.arange(ncb) * kw - kw // 2, 0, GRID_W - kb)[:, None] + np.arange(kb)[None, :]
    qs = win_start[q_col][:, :, None]
    col_valid = (band_idx[:, None, :] >= qs) & (band_idx[:, None, :] < qs + kw)
    row_off = row_idx - r[:, None] + NA_WIN_H - 1
    col_off = np.clip(band_idx[:, None, :] - q_col[:, :, None], 1 - kw, kw - 1) + kw - 1
    bias = rpb[:, row_off[:, None, None, :, None], col_off[None, :, :, None, :]].astype(jnp.float32)
    bias = jnp.where(col_valid[None, None, :, :, None, :], bias, NEG_INF)
    bias = jnp.transpose(bias, (1, 2, 0, 3, 4, 5)).reshape(rows, ncb, H, kw, kh * kb)
    ri = row_idx[:, None, :, None]
    ci = band_idx[None, :, None, :]
    k_band = k.reshape(bsz, rows, GRID_W, H, Dh)[:, ri, ci]
    v_band = v.reshape(bsz, rows, GRID_W, H, Dh)[:, ri, ci]
    q_blk = q.reshape(bsz, rows, ncb, kw, H, Dh)
    scale = Dh ** -0.5
    s_loc = jnp.einsum('brnqhd,brnikhd->brnhqik', q_blk, k_band).reshape(bsz, rows, ncb, H, kw, kh * kb)
    s_ctx = jnp.einsum('brnqhd,bchd->brnhqc', q_blk, ctx_k)
    s = jnp.concatenate([s_loc.astype(jnp.float32) * scale + bias,
                         s_ctx.astype(jnp.float32) * scale], axis=-1)
    p = jax.nn.softmax(s, axis=-1).astype(v.dtype)
    p_loc = p[..., :kh * kb].reshape(bsz, rows, ncb, H, kw, kh, kb)
    p_ctx = p[..., kh * kb:]
    o = (jnp.einsum('brnhqik,brnikhd->brnqhd', p_loc, v_band)
         + jnp.einsum('brnhqc,bchd->brnqhd', p_ctx, ctx_v))
    return o.reshape(bsz, L, H * Dh)


def _layer(x, cond, norm_g, w_mod, b_mod, w_in, na_rpb, lru_conv_w, lru_conv_b,
           lru_wa, lru_ba, lru_wx, lru_bx, lru_lam, conv_w, w_br_na, w_br_lru,
           w_br_conv, w_out, ctx_k=None, ctx_v=None, ctx_h=None):
    bsz, L, _ = x.shape
    shift, scale, gate = jnp.split(jax.nn.silu(cond) @ w_mod + b_mod, 3, axis=-1)
    xm = _rmsnorm(x, norm_g) * (1.0 + scale[:, None]) + shift[:, None]
    (q, k, v, g_na, u, g_lru, c_b, c_c, c_h, g_conv,
     m_na, m_lru, m_conv) = jnp.split(xm @ w_in, SPLIT_IDX, axis=-1)
    heads = (bsz, L, NA_HEADS, NA_HEAD_DIM)
    q, k, v = q.reshape(heads), k.reshape(heads), v.reshape(heads)
    is_ctx = ctx_k is None
    if is_ctx:
        o_na = _context_attention(q, k, v)
        h0_f, h0_b = None, None
    else:
        o_na = _neighbourhood_attention(q, k, v, ctx_k, ctx_v, na_rpb)
        h0_f, h0_b = ctx_h[:, 0], ctx_h[:, 1]
    u_f = _dwconv(u, lru_conv_w[0], LRU_CONV - 1, 0) + lru_conv_b[0]
    u_b = _dwconv(u, lru_conv_w[1], 0, LRU_CONV - 1) + lru_conv_b[1]
    h_f, hT_f = _rglru(u_f, lru_wa[0], lru_ba[0], lru_wx[0], lru_bx[0], lru_lam[0], h0_f, False)
    h_b, hT_b = _rglru(u_b, lru_wa[1], lru_ba[1], lru_wx[1], lru_bx[1], lru_lam[1], h0_b, True)
    o_conv = c_b * _dwconv(c_c * c_h, conv_w, CONV_K // 2, CONV_K - 1 - CONV_K // 2)
    merged = (jax.nn.sigmoid(m_na) * ((o_na * jax.nn.silu(g_na)) @ w_br_na)
              + jax.nn.sigmoid(m_lru) * (((h_f + h_b) * jax.nn.silu(g_lru)) @ w_br_lru)
              + jax.nn.sigmoid(m_conv) * ((o_conv * jax.nn.silu(g_conv)) @ w_br_conv))
    x = x + gate[:, None] * (merged @ w_out)
    if is_ctx:
        return x, k, v, jnp.stack([hT_f, hT_b], axis=1)
    return x


def setup_inputs(seed: int = 0) -> dict:
    key = jax.random.key(seed)
    ks = jax.random.split(key, 26)

    def nrm(i, shape, s):
        return jax.random.normal(ks[i], shape, jnp.float32) * s

    d = D_MODEL
    u = jax.random.uniform(ks[25], (DEPTH, 2, LRU_WIDTH), jnp.float32, 0.9, 0.999)
    a0 = u ** (1.0 / LRU_C)
    return {
        'x_prompt': nrm(0, (BATCH, SEQ, d), 1.0),
        'x_sample': nrm(1, (DEC_BATCH, DEC_SEQ, d), 1.0),
        'cache_k': nrm(2, (DEC_BATCH, DEPTH, PAST_LEN, NA_HEADS, NA_HEAD_DIM), 1.0),
        'cache_v': nrm(3, (DEC_BATCH, DEPTH, PAST_LEN, NA_HEADS, NA_HEAD_DIM), 1.0),
        'state_lru': nrm(4, (DEC_BATCH, DEPTH, 2, LRU_WIDTH), 0.5),
        'c': nrm(5, (DEC_BATCH, d), 1.0),
        'c_ctx': nrm(6, (d,), 1.0),
        'norm_g': 1.0 + nrm(7, (DEPTH, d), 0.02),
        'w_mod': nrm(8, (DEPTH, d, 3 * d), 0.5 * d ** -0.5),
        'b_mod': nrm(9, (DEPTH, 3 * d), 0.01),
        'w_in': nrm(10, (DEPTH, d, IN_COLS), d ** -0.5),
        'na_rpb': nrm(11, (DEPTH, NA_HEADS, 2 * NA_WIN_H - 1, 2 * NA_WIN_W - 1), 0.1),
        'lru_conv_w': nrm(12, (DEPTH, 2, LRU_CONV, LRU_WIDTH), LRU_CONV ** -0.5),
        'lru_conv_b': nrm(13, (DEPTH, 2, LRU_WIDTH), 0.01),
        'lru_wa': nrm(14, (DEPTH, 2, LRU_BLOCKS, LRU_BLOCK, LRU_BLOCK), LRU_BLOCK ** -0.5),
        'lru_ba': nrm(15, (DEPTH, 2, LRU_WIDTH), 0.01),
        'lru_wx': nrm(16, (DEPTH, 2, LRU_BLOCKS, LRU_BLOCK, LRU_BLOCK), LRU_BLOCK ** -0.5),
        'lru_bx': nrm(17, (DEPTH, 2, LRU_WIDTH), 0.01),
        'lru_lam': jnp.log(a0) - jnp.log1p(-a0),
        'conv_w': nrm(18, (DEPTH, CONV_K, CONV_WIDTH), CONV_K ** -0.5),
        'w_br_na': nrm(19, (DEPTH, NA_WIDTH, d), NA_WIDTH ** -0.5),
        'w_br_lru': nrm(20, (DEPTH, LRU_WIDTH, d), LRU_WIDTH ** -0.5),
        'w_br_conv': nrm(21, (DEPTH, CONV_WIDTH, d), CONV_WIDTH ** -0.5),
        'w_out': nrm(22, (DEPTH, d, d), d ** -0.5),
        'final_g': 1.0 + nrm(23, (d,), 0.02),
    }


def reference(x_prompt, x_sample, cache_k, cache_v, state_lru, c, c_ctx, norm_g, w_mod, b_mod,
              w_in, na_rpb, lru_conv_w, lru_conv_b, lru_wa, lru_ba, lru_wx, lru_bx, lru_lam,
              conv_w, w_br_na, w_br_lru, w_br_conv, w_out, final_g):
    xp, xs = x_prompt, x_sample
    cond_ctx = c_ctx[None, :]
    new_k, new_v, new_h = [], [], []
    for l in range(DEPTH):
        lw = (norm_g[l], w_mod[l], b_mod[l], w_in[l], na_rpb[l], lru_conv_w[l], lru_conv_b[l],
              lru_wa[l], lru_ba[l], lru_wx[l], lru_bx[l], lru_lam[l], conv_w[l],
              w_br_na[l], w_br_lru[l], w_br_conv[l], w_out[l])
        xp, k_l, v_l, h_l = _layer(xp, cond_ctx, *lw)
        new_k.append(k_l)
        new_v.append(v_l)
        new_h.append(h_l)
        xs = _layer(xs, c, *lw, cache_k[:, l], cache_v[:, l], state_lru[:, l])
    y_prompt = _rmsnorm(xp, final_g)
    y_sample = _rmsnorm(xs, final_g)
    new_cache_k = jnp.stack(new_k, axis=1)
    new_cache_v = jnp.stack(new_v, axis=1)
    new_state_lru = jnp.stack(new_h, axis=1)
    return (y_prompt, y_sample, new_cache_k, new_cache_v, new_state_lru)
```

```python
import numpy as np
import concourse.bass as bass
import concourse.mybir as mybir
from concourse.bass_utils import run_bass_kernel_spmd
from contextlib import ExitStack

F32 = mybir.dt.float32
BF16 = mybir.dt.bfloat16
AF = mybir.ActivationFunctionType
ALU = mybir.AluOpType

DEPTH = 4
D = 1024
TS = 2048
TP = 512
TMAX = 2048
NEG = -30000.0
PAGE = 512
NDS = 40
NW = 6
DBG = {"phases": "SP", "layers": DEPTH, "stages": "MNABCO", "att": 9, "merge": 1, "kvout": 2}

VOFF = {}
_o = 0
for _n, _sz in [("cond_s", 8), ("cond_p", 8), ("final_g", 8), ("norm_g", 32), ("b_mod", 96),
                ("lcw", 128), ("lcb", 32), ("lba", 32), ("lbx", 32), ("lam", 32), ("cw", 48),
                ("st", 32), ("eps", 1), ("one", 1), ("quarter", 1)]:
    VOFF[_n] = _o
    _o += _sz
NV = _o


class Buf:
    def __init__(self, ap, space, off, esz, nel):
        self.ap = ap
        self.space = space
        self.off = off
        self.esz = esz
        self.nel = nel

    def r(self, lo=0, hi=None):
        if hi is None:
            hi = self.nel
        return (self.space, self.off + lo * self.esz, self.off + hi * self.esz)


class Builder:
    def __init__(self):
        self.nc = bass.Bass("TRN2", target_bir_lowering=False)
        self.es = ExitStack()
        nc = self.nc
        self.engs = {"pe": nc.tensor, "act": nc.scalar, "dve": nc.vector, "pool": nc.gpsimd, "sp": nc.sync}
        self.sems = []
        self.semval = []
        self.esem = {}
        for e in self.engs:
            self.esem[e] = self._newsem("e_" + e)
        self.dsems = {q: [self._newsem(f"d{q}{i}") for i in range(NDS // 2)] for q in ("sp", "pool")}
        self.dnext = {"sp": 0, "pool": 0}
        self.waited = {e: {} for e in self.engs}
        self.pages = {}
        self.ninstr = 0

    def _newsem(self, name):
        s = self.es.enter_context(self.nc.semaphore(name))
        self.sems.append(s)
        self.semval.append(0)
        return len(self.sems) - 1

    @staticmethod
    def _pg(reg):
        sp, lo, hi = reg
        if sp == "ps":
            return [(sp, p) for p in range(lo // 2048, (hi - 1) // 2048 + 1)]
        return [(sp, p) for p in range(lo // PAGE, (hi - 1) // PAGE + 1)]

    @staticmethod
    def _split(reads, writes):
        r2 = [r for r in reads if r[0] != "ps"]
        w2 = list(writes) + [r for r in reads if r[0] == "ps"]
        return r2, w2

    def _waits(self, engine, reads, writes, extra=()):
        need = {}

        def add(sv):
            if sv is None:
                return
            s, v = sv
            if need.get(s, 0) < v:
                need[s] = v
        for r in reads:
            for p in self._pg(r):
                st = self.pages.get(p)
                if st:
                    add(st[0])
        for w in writes:
            for p in self._pg(w):
                st = self.pages.get(p)
                if st:
                    add(st[0])
                    for s, v in st[1].items():
                        add((s, v))
        for sv in extra:
            add(sv)
        wd = self.waited[engine]
        own = self.esem[engine]
        eng = self.engs[engine]
        for s, v in need.items():
            if engine == "pe" and s == own:
                continue
            if wd.get(s, 0) >= v:
                continue
            eng.wait_ge(self.sems[s], v)
            wd[s] = v
            self.ninstr += 1

    def _update(self, s, v, reads, writes):
        for r in reads:
            for p in self._pg(r):
                st = self.pages.get(p)
                if st is None:
                    st = [None, {}]
                    self.pages[p] = st
                st[1][s] = v
        for w in writes:
            for p in self._pg(w):
                self.pages[p] = [(s, v), {}]

    def op(self, engine, fn, reads, writes):
        reads, writes = self._split(reads, writes)
        self._waits(engine, reads, writes)
        ins = fn()
        s = self.esem[engine]
        self.semval[s] += 1
        ins.then_inc(self.sems[s], 1)
        self._update(s, self.semval[s], reads, writes)
        self.ninstr += 1

    def group(self, engine, fns, reads, writes):
        reads, writes = self._split(reads, writes)
        self._waits(engine, reads, writes)
        ins = None
        for fn in fns:
            ins = fn()
            self.ninstr += 1
        s = self.esem[engine]
        self.semval[s] += 1
        ins.then_inc(self.sems[s], 1)
        self._update(s, self.semval[s], reads, writes)

    def dma(self, queue, pairs, reads, writes):
        d = self.dsems[queue][self.dnext[queue]]
        self.dnext[queue] = (self.dnext[queue] + 1) % (NDS // 2)
        extra = [(d, self.semval[d])] if self.semval[d] > 0 else []
        self._waits(queue, reads, writes, extra)
        eng = self.engs[queue]
        for out, in_ in pairs:
            ins = eng.dma_start(out=out, in_=in_)
            self.semval[d] += 16
            ins.then_inc(self.sems[d], 16)
            self.ninstr += 1
        self._update(d, self.semval[d], reads, writes)

    def finish(self):
        sp = self.engs["sp"]
        for s in range(len(self.sems)):
            if self.semval[s] > 0 and self.waited["sp"].get(s, 0) < self.semval[s]:
                sp.wait_ge(self.sems[s], self.semval[s])

    def alloc_arena(self):
        nc = self.nc
        self.ARENA_BYTES = 211968
        self.arena = self.es.enter_context(nc.sbuf_tensor("arena", [128, self.ARENA_BYTES // 4], F32))
        self.psb = [self.es.enter_context(nc.psum_tensor(f"psb{i}", [128, 512], F32)) for i in range(8)]

    def view(self, off, nel, dt, shape=None):
        esz = 4 if dt == F32 else 2
        nb = nel * esz
        assert off % 4 == 0 and nb % 4 == 0, (off, nel)
        assert off + nb <= self.ARENA_BYTES, (off, nb)
        ap = self.arena[:, off // 4:(off + nb) // 4]
        if dt != F32:
            ap = ap.bitcast(dt)
        if shape is not None:
            if len(shape) == 2:
                ap = ap.rearrange("p (a b) -> p a b", a=shape[0], b=shape[1])
            elif len(shape) == 3:
                ap = ap.rearrange("p (a b c) -> p a b c", a=shape[0], b=shape[1], c=shape[2])
        return Buf(ap, "sb", off, esz, nel)

    def bank(self, i, lo=0, hi=512, dt=F32):
        ap = self.psb[i][:, lo:hi]
        nel = hi - lo
        esz = 4
        if dt != F32:
            ap = ap.bitcast(dt)
            nel *= 2
            esz = 2
        return Buf(ap, "ps", i * 2048 + lo * 4, esz, nel)

    def build(self):
        nc = self.nc
        dr = {}

        def din(name, shape):
            dr[name] = nc.dram_tensor(name, list(shape), F32, kind="ExternalInput").ap()

        def dout(name, shape):
            dr[name] = nc.dram_tensor(name, list(shape), F32, kind="ExternalOutput").ap()
        din("xs", [TS, D])
        din("xp", [TP, D])
        din("ck", [DEPTH, 256, 512])
        din("cv", [DEPTH, 256, 512])
        din("vecs", [128, NV])
        din("w_mod", [DEPTH, D, 3 * D])
        din("w_in", [DEPTH, D, 8192])
        din("w_br", [DEPTH, 3, 512, D])
        din("w_out", [DEPTH, D, D])
        din("lwa", [DEPTH, 2, 8, 64, 64])
        din("lwx", [DEPTH, 2, 8, 64, 64])
        din("tab", [DEPTH, 2, 8, 128, 1024])
        din("ident", [128, 128])
        dout("ys", [TS, D])
        dout("yp", [TP, D])
        dout("nk", [2, DEPTH, 256, 512])
        dout("nv", [2, DEPTH, 256, 512])
        dout("nh", [128, 64])
        self.dr = dr

        self.alloc_arena()
        o = 0
        self.XT = self.view(o, 8 * TMAX, F32, (8, TMAX)); o += 8 * TMAX * 4
        self.XM = self.view(o, 8 * TMAX, BF16, (8, TMAX)); o += 8 * TMAX * 2
        self.MG = self.view(o, 8 * TMAX, BF16, (8, TMAX)); o += 8 * TMAX * 2
        self.FT = self.view(o, 4 * TMAX, BF16, (4, TMAX)); o += 4 * TMAX * 2
        self.IDF = self.view(o, 128, F32); o += 512
        self.IDB = self.view(o, 128, BF16); o += 512
        self.ONESB = self.view(o, 128, BF16); o += 512
        self.VEC = self.view(o, 512, F32); o += 2048
        self.SC = self.view(o, 8, BF16); o += 512
        self.MODS = []
        for i in range(2):
            self.MODS.append({"MODV": self.view(o, 24, F32), "AV": self.view(o + 128, 8, F32), "CST": self.view(o + 192, 8, F32),
                              "HBV": self.view(o + 256, 16, F32)})
            o += 512
        self.modbank = None
        self.NH = self.view(o, 64, F32); o += 512
        self.RC = [self.view(o + i * 512, 1, F32) for i in range(2)]; o += 1024
        self.QZ = [self.view(o + i * 512, 256, BF16) for i in range(2)]; o += 1024
        self.BD = [[self.view(o + (d * 2 + g) * 512, 128, BF16) for g in range(2)] for d in range(2)]; o += 2048
        self.WS = [self.view(o + i * 2048, 1024, BF16, (8, 128)) for i in range(NW)]; o += NW * 2048
        self.wnext = 0
        self.SCR = o
        self.SCR_BYTES = self.ARENA_BYTES - o
        assert self.SCR_BYTES >= 41984, self.SCR_BYTES
        self.rot = {}

        V = self.VEC
        self.dma("sp", [(V.ap[:, 0:NV], dr["vecs"][:, :])], [], [V.r()])
        self.dma("sp", [(self.IDF.ap, dr["ident"][:, :])], [], [self.IDF.r()])
        self.dma("pool", [(self.IDB.ap, dr["ident"][:, :])], [], [self.IDB.r()])
        self.op("dve", lambda: nc.vector.memset(self.ONESB.ap, 1.0), [], [self.ONESB.r()])
        self.op("dve", lambda: nc.vector.memset(self.NH.ap, 0.0), [], [self.NH.r()])
        for i in range(2):
            self.op("dve", lambda i=i: nc.vector.memset(self.QZ[i].ap, 0.0), [], [self.QZ[i].r()])
        for d in range(2):
            for g in range(2):
                b = self.BD[d][g]
                self.op("dve", lambda b=b: nc.vector.memset(b.ap, 0.0), [], [b.r()])

        segs_s = [(0, 2048)]
        segs_p = [(0, 256), (256, 256)]
        if "S" in DBG["phases"]:
            self.phase("S", dr["xs"], dr["ys"], TS, segs_s, VOFF["cond_s"])
        if "P" in DBG["phases"]:
            self.phase("P", dr["xp"], dr["yp"], TP, segs_p, VOFF["cond_p"])
        self.dma("sp", [(dr["nh"][:, :], self.NH.ap)], [self.NH.r()], [])
        self.finish()
        return nc

    def vcol(self, name, idx, n=1):
        o = VOFF[name] + idx
        return self.VEC.ap[:, o:o + n]

    BANKMAPS = {
        "att": {"proj": [0, 1], "s": [2, 3], "po": [4, 5], "x": [6, 7]},
        "wide": {"proj": [0, 1, 2, 3, 4, 5], "x": [6, 7]},
        "lru": {"proj": [0, 1], "x": [2, 3, 4, 5, 6], "m": [7]},
        "norm": {"proj": [0, 1, 2, 3], "x": [4, 5, 6, 7]},
    }

    def psrot(self, cls):
        banks = self.BANKMAPS[getattr(self, "bmap", "norm")][cls]
        i = self.rot.get(cls, 0)
        self.rot[cls] = i + 1
        return banks[i % len(banks)]

    def wload(self, src, nk):
        w = self.WS[self.wnext]
        self.wnext = (self.wnext + 1) % NW
        self.dma("pool", [(w.ap[:, 0:nk, :], src)], [], [w.r(0, nk * 128)])
        return w

    def win(self, l, col):
        return self.dr["w_in"][l].rearrange("(k p) n -> p k n", p=128)[:, :, col:col + 128]

    def proj_fm(self, w, nk, src, t0, n, bank_i):
        nc = self.nc
        b = self.bank(bank_i, 0, n)
        fns = []
        reads = [w.r(0, nk * 128)]
        for k in range(nk):
            fns.append(lambda k=k: nc.tensor.matmul(b.ap, lhsT=w.ap[:, k, :], rhs=src.ap[:, k, t0:t0 + n],
                                                    start=(k == 0), stop=(k == nk - 1)))
            reads.append(src.r(k * TMAX + t0, k * TMAX + t0 + n))
        self.group("pe", fns, reads, [b.r()])
        return b

    def scr(self, off, nel, dt, shape=None):
        return self.view(self.SCR + off, nel, dt, shape)

    def phase(self, ph, xdram, ydram, T, segs, cond_off):
        nc = self.nc
        NB = T // 512
        NTT = T // 128
        self.T = T
        self.ph = ph
        self.bmap = "norm"
        XIN = [self.scr(i * 4096, 1024, F32) for i in range(2)]
        for tt in range(NTT):
            xin = XIN[tt % 2]
            self.dma("sp", [(xin.ap, xdram[tt * 128:(tt + 1) * 128, :])], [], [xin.r()])
            for half in range(2):
                bi = self.psrot("proj")
                b = self.bank(bi)
                fns = [lambda c=c, b=b, xin=xin: nc.tensor.transpose(
                    b.ap[:, (c % 4) * 128:(c % 4 + 1) * 128], xin.ap[:, c * 128:(c + 1) * 128], self.IDF.ap)
                    for c in range(half * 4, half * 4 + 4)]
                self.group("pe", fns, [xin.r(), self.IDF.r()], [b.r()])
                out = self.XT.ap[:, half * 4:half * 4 + 4, tt * 128:(tt + 1) * 128]
                in_ = b.ap.rearrange("p (c t) -> p c t", c=4)
                wr = [self.XT.r(c * TMAX + tt * 128, c * TMAX + (tt + 1) * 128) for c in range(half * 4, half * 4 + 4)]
                eng = "dve" if half == 0 else "act"
                if eng == "dve":
                    self.op("dve", lambda out=out, in_=in_: nc.vector.tensor_copy(out=out, in_=in_), [b.r()], wr)
                else:
                    self.op("act", lambda out=out, in_=in_: nc.scalar.copy(out=out, in_=in_), [b.r()], wr)
        for l in range(DBG["layers"]):
            self.layer(l, T, segs, cond_off)
        self.final(ydram, T)

    def rstd_block(self, blk, off):
        nc = self.nc
        SQ = [self.scr(off + i * 1024, 512, BF16) for i in range(2)]
        TMP = self.scr(off + 2048, 512, F32)
        RS = self.scr(off + 4096, 512, F32)
        t0 = blk * 512
        bi = self.psrot("x")
        b = self.bank(bi)
        for c in range(8):
            sq = SQ[c % 2]
            xr = self.XT.r(c * TMAX + t0, c * TMAX + t0 + 512)
            self.op("act", lambda c=c, sq=sq: nc.scalar.activation(out=sq.ap, in_=self.XT.ap[:, c, t0:t0 + 512], func=AF.Square),
                    [xr], [sq.r()])
            self.group("pe", [lambda c=c, sq=sq: nc.tensor.matmul(b.ap, lhsT=self.ONESB.ap, rhs=sq.ap, start=(c == 0), stop=(c == 7))],
                       [sq.r(), self.ONESB.r()], [b.r()])
        self.op("act", lambda: nc.scalar.activation(out=TMP.ap, in_=b.ap, func=AF.Sqrt, scale=1.0 / D, bias=self.vcol("eps", 0)),
                [b.r(), self.VEC.r()], [TMP.r()])
        self.op("dve", lambda: nc.vector.reciprocal(out=RS.ap, in_=TMP.ap), [TMP.r()], [RS.r()])
        return RS

    def mod_part(self, key, cond_off, slot, part):
        nc = self.nc
        dr = self.dr
        V = self.VEC
        l = key[1]
        M = self.MODS[slot]
        if part == 0:
            self.op("act", lambda: nc.scalar.activation(out=self.SC.ap, in_=V.ap[:, cond_off:cond_off + 8], func=AF.Silu),
                    [V.r()], [self.SC.r()])
        bm = self.bank(7, 0, 24)
        wmod = dr["w_mod"][l].rearrange("(k p) n -> p k n", p=128)
        for j in range(part * 6, part * 6 + 6):
            w = self.wload(wmod[:, :, j * 128:(j + 1) * 128], 8)
            fns = [lambda k=k, w=w, j=j: nc.tensor.matmul(bm.ap[:, j:j + 1], lhsT=w.ap[:, k, :], rhs=self.SC.ap[:, k:k + 1],
                                                          start=(k == 0), stop=(k == 7)) for k in range(8)]
            self.group("pe", fns, [w.r(), self.SC.r()], [bm.r()])
        if part < 3:
            return
        MODV, AV, CST, HBV = M["MODV"], M["AV"], M["CST"], M["HBV"]
        ob = VOFF["b_mod"] + l * 24
        self.op("dve", lambda: nc.vector.tensor_tensor(out=MODV.ap, in0=bm.ap, in1=V.ap[:, ob:ob + 24], op=ALU.add),
                [bm.r(), V.r()], [MODV.r()])
        og = VOFF["norm_g"] + l * 8
        self.op("dve", lambda: nc.vector.scalar_tensor_tensor(out=AV.ap, in0=MODV.ap[:, 8:16], scalar=1.0,
                                                              in1=V.ap[:, og:og + 8], op0=ALU.add, op1=ALU.mult),
                [MODV.r(), V.r()], [AV.r()])
        self.op("dve", lambda: nc.vector.tensor_scalar(out=MODV.ap[:, 16:24], in0=MODV.ap[:, 16:24], scalar1=0.5, scalar2=None, op0=ALU.mult),
                [MODV.r()], [MODV.r()])
        ol = VOFF["lam"] + l * 8
        self.op("act", lambda: nc.scalar.activation(out=CST.ap, in_=V.ap[:, ol:ol + 8], func=AF.Exp, scale=-1.0),
                [V.r()], [CST.r()])
        self.op("act", lambda: nc.scalar.activation(out=CST.ap, in_=CST.ap, func=AF.Ln, bias=self.vcol("one", 0), scale=1.0),
                [CST.r(), V.r()], [CST.r()])
        self.op("dve", lambda: nc.vector.tensor_scalar(out=CST.ap, in0=CST.ap, scalar1=-4.0, scalar2=None, op0=ALU.mult),
                [CST.r()], [CST.r()])
        for gi, nm in enumerate(["lba", "lbx"]):
            ov = VOFF[nm] + l * 8
            self.op("dve", lambda gi=gi, ov=ov: nc.vector.tensor_scalar(out=HBV.ap[:, gi * 8:gi * 8 + 8], in0=V.ap[:, ov:ov + 8], scalar1=0.5,
                                                                       scalar2=None, op0=ALU.mult), [V.r()], [HBV.r()])
        self.mod_ready = key

    def layer(self, l, T, segs, cond_off):
        nc = self.nc
        dr = self.dr
        NB = T // 512
        NTT = T // 128
        V = self.VEC
        sample = (self.ph == "S")

        key = (self.ph, l)
        slot = l % 2
        if getattr(self, "mod_ready", None) != key:
            for part in range(4):
                self.mod_part(key, cond_off, slot, part)
        M = self.MODS[slot]
        self.M = M
        if l + 1 < DBG["layers"]:
            self.next_mod = ((self.ph, l + 1), cond_off, (l + 1) % 2)
        elif self.ph == "S" and "P" in DBG["phases"] and DBG["layers"] > 0:
            self.next_mod = (("P", 0), VOFF["cond_p"], 0)
        else:
            self.next_mod = None

        self.bmap = "norm"
        T1 = [self.scr(8192 + i * 2048, 512, F32) for i in range(2)]
        for blk in range(NB):
            RS = self.rstd_block(blk, 0)
            t0 = blk * 512
            for c in range(8):
                t1 = T1[c % 2]
                xr = self.XT.r(c * TMAX + t0, c * TMAX + t0 + 512)
                self.op("dve", lambda c=c, t1=t1: nc.vector.scalar_tensor_tensor(
                    out=t1.ap, in0=self.XT.ap[:, c, t0:t0 + 512], scalar=M["AV"].ap[:, c:c + 1], in1=RS.ap,
                    op0=ALU.mult, op1=ALU.mult), [xr, M["AV"].r(), RS.r()], [t1.r()])
                self.op("act", lambda c=c, t1=t1: nc.scalar.activation(
                    out=self.XM.ap[:, c, t0:t0 + 512], in_=t1.ap, func=AF.Identity, bias=M["MODV"].ap[:, c:c + 1], scale=1.0),
                    [t1.r(), M["MODV"].r()], [self.XM.r(c * TMAX + t0, c * TMAX + t0 + 512)])

        st_ = DBG["stages"]
        self.mg_init = False
        if "A" in st_:
            self.bmap = "att"
            self.branch_attention(l, T, segs)
            self.merge(l, 0, T)
        if "B" in st_:
            self.bmap = "lru"
            self.branch_lru(l, T, segs)
            self.merge(l, 1, T)
        if "C" in st_:
            self.bmap = "wide"
            self.branch_conv(l, T, segs)
            self.merge(l, 2, T)
        if "O" not in st_:
            return
        self.bmap = "wide"
        wout = dr["w_out"][l].rearrange("(k p) n -> p k n", p=128)
        for j in range(8):
            w = self.wload(wout[:, :, j * 128:(j + 1) * 128], 8)
            for blk in range(NB):
                t0 = blk * 512
                b = self.proj_fm(w, 8, self.MG, t0, 512, self.psrot("proj"))
                xr = self.XT.r(j * TMAX + t0, j * TMAX + t0 + 512)
                self.op("dve", lambda j=j, b=b, t0=t0: nc.vector.scalar_tensor_tensor(
                    out=self.XT.ap[:, j, t0:t0 + 512], in0=b.ap, scalar=M["MODV"].ap[:, 16 + j:17 + j],
                    in1=self.XT.ap[:, j, t0:t0 + 512], op0=ALU.mult, op1=ALU.add),
                    [b.r(), M["MODV"].r(), xr], [xr])

    def merge(self, l, br, T):
        if not DBG["merge"]:
            return
        self.bmap = "wide"
        nc = self.nc
        dr = self.dr
        NB = T // 512
        SGM = [self.scr(i * 2048, 512, F32) for i in range(2)]
        TMPM = [self.scr(4096 + i * 2048, 512, F32) for i in range(2)]
        wbr = dr["w_br"][l, br].rearrange("(k p) n -> p k n", p=128)
        n = 0
        for j in range(8):
            w1 = self.wload(wbr[:, :, j * 128:(j + 1) * 128], 4)
            w2 = self.wload(self.win(l, 5120 + br * 1024 + j * 128), 8)
            for blk in range(NB):
                t0 = blk * 512
                b1 = self.proj_fm(w1, 4, self.FT, t0, 512, self.psrot("proj"))
                b2 = self.proj_fm(w2, 8, self.XM, t0, 512, self.psrot("proj"))
                sg = SGM[n % 2]
                tm = TMPM[n % 2]
                n += 1
                self.op("act", lambda sg=sg, b2=b2: nc.scalar.activation(out=sg.ap, in_=b2.ap, func=AF.Tanh, scale=0.5),
                        [b2.r()], [sg.r()])
                mr = self.MG.r(j * TMAX + t0, j * TMAX + t0 + 512)
                mg = self.MG.ap[:, j, t0:t0 + 512]
                if not self.mg_init:
                    self.op("dve", lambda sg=sg, b1=b1, mg=mg: nc.vector.scalar_tensor_tensor(out=mg, in0=sg.ap, scalar=1.0, in1=b1.ap,
                                                                                            op0=ALU.add, op1=ALU.mult),
                            [b1.r(), sg.r()], [mr])
                else:
                    self.op("dve", lambda sg=sg, b1=b1, tm=tm: nc.vector.scalar_tensor_tensor(out=tm.ap, in0=sg.ap, scalar=1.0, in1=b1.ap,
                                                                                            op0=ALU.add, op1=ALU.mult),
                            [b1.r(), sg.r()], [tm.r()])
                    self.op("dve", lambda tm=tm, mg=mg: nc.vector.tensor_tensor(out=mg, in0=mg, in1=tm.ap, op=ALU.add),
                            [tm.r(), mr], [mr])
        self.mg_init = True

    def branch_attention(self, l, T, segs):
        nc = self.nc
        dr = self.dr
        NB = T // 512
        NTT = T // 128
        sample = (self.ph == "S")
        o = 0
        QT = self.scr(o, TMAX, BF16); o += 4096
        KT = self.scr(o, TMAX, BF16); o += 4096
        SG = self.scr(o, TMAX, BF16); o += 4096
        OTOK = self.scr(o, 16 * 128, BF16, (16, 128)); o += 4096
        VAUG = self.scr(o, 16 * 130, BF16, (16, 2, 65)); o += 4608
        ES = [self.scr(o + i * 4096, 8 * 256, BF16, (8, 256)) for i in range(2)]
        CKS = self.scr(o, 2 * 512, F32, (2, 512))
        o += 8192
        TAB = self.scr(o, 2 * 2 * 1024, BF16, (2, 2, 1024)); o += 8192
        KCT = self.scr(o, 4 * 256, BF16, (4, 256)); o += 2048
        VC = self.scr(o, 2 * 8 * 65, BF16, (2, 8, 65)); o += 2560
        assert o <= self.SCR_BYTES, o
        KVO = [[self.scr(TAB.off - self.SCR + kv * 2048, 512, F32, (4, 128)) for kv in range(2)]]

        if sample:
            self.dma("sp", [(CKS.ap, dr["ck"][l].rearrange("(u p) f -> p u f", p=128))], [], [CKS.r()])
            for u in range(2):
                bi = self.psrot("proj")
                b = self.bank(bi)
                fns = [lambda hp=hp, u=u, b=b: nc.tensor.transpose(b.ap[:, hp * 128:(hp + 1) * 128],
                                                                   CKS.ap[:, u, hp * 128:(hp + 1) * 128], self.IDF.ap) for hp in range(4)]
                self.group("pe", fns, [CKS.r(), self.IDF.r()], [b.r()])
                self.op("dve", lambda u=u, b=b: nc.vector.tensor_copy(out=KCT.ap[:, :, u * 128:(u + 1) * 128],
                                                                      in_=b.ap.rearrange("p (h t) -> p h t", h=4)),
                        [b.r()], [KCT.r()])
            self.op("dve", lambda: nc.vector.memset(VC.ap[:, :, :, 64:65], 1.0), [], [VC.r()])
            self.dma("pool", [(VC.ap[:, u, :, 0:64], dr["cv"][l][u * 128:(u + 1) * 128, :].rearrange("p (h d) -> p h d", d=64)) for u in range(2)],
                     [], [VC.r()])
        self.op("dve", lambda: nc.vector.memset(VAUG.ap[:, :, :, 64:65], 1.0), [], [VAUG.r()])

        for hp in range(4):
            wq = self.wload(self.win(l, hp * 128), 8)
            wk = self.wload(self.win(l, 512 + hp * 128), 8)
            wv = self.wload(self.win(l, 1024 + hp * 128), 8)
            wg = self.wload(self.win(l, 1536 + hp * 128), 8)
            if sample:
                self.dma("pool", [(TAB.ap[:, ty], dr["tab"][l, ty, 2 * hp:2 * hp + 2].rearrange("h p n -> p h n")) for ty in range(2)],
                         [], [TAB.r()])
            for blk in range(NB):
                t0 = blk * 512
                b = self.proj_fm(wq, 8, self.XM, t0, 512, self.psrot("proj"))
                self.op("act", lambda b=b, t0=t0: nc.scalar.activation(out=QT.ap[:, t0:t0 + 512], in_=b.ap, func=AF.Copy, scale=0.125),
                        [b.r()], [QT.r(t0, t0 + 512)])
                b = self.proj_fm(wk, 8, self.XM, t0, 512, self.psrot("proj"))
                self.op("dve", lambda b=b, t0=t0: nc.vector.tensor_copy(out=KT.ap[:, t0:t0 + 512], in_=b.ap),
                        [b.r()], [KT.r(t0, t0 + 512)])
                b = self.proj_fm(wg, 8, self.XM, t0, 512, self.psrot("proj"))
                self.op("act", lambda b=b, t0=t0: nc.scalar.activation(out=SG.ap[:, t0:t0 + 512], in_=b.ap, func=AF.Silu),
                        [b.r()], [SG.r(t0, t0 + 512)])
            for g4 in range(NTT // 4 if DBG["att"] >= 1 else 0):
                for kv in ([1] if sample else [0, 1]):
                    w = wv if kv == 1 else wk
                    bi = self.psrot("x")
                    b = self.bank(bi)
                    fns = []
                    reads = [w.r()]
                    for i in range(4):
                        tt = g4 * 4 + i
                        for k in range(8):
                            fns.append(lambda i=i, tt=tt, k=k, w=w, b=b: nc.tensor.matmul(
                                b.ap[:, i * 128:(i + 1) * 128], lhsT=self.XM.ap[:, k, tt * 128:(tt + 1) * 128], rhs=w.ap[:, k, :],
                                start=(k == 0), stop=(k == 7)))
                    for k in range(8):
                        reads.append(self.XM.r(k * TMAX + g4 * 512, k * TMAX + g4 * 512 + 512))
                    self.group("pe", fns, reads, [b.r()])
                    if kv == 1:
                        self.op("dve", lambda b=b, g4=g4: nc.vector.tensor_copy(
                            out=VAUG.ap[:, g4 * 4:g4 * 4 + 4, :, 0:64], in_=b.ap.rearrange("p (t h d) -> p t h d", t=4, h=2)),
                            [b.r()], [VAUG.r(g4 * 4 * 130, (g4 * 4 + 4) * 130)])
                    if not sample and DBG["kvout"]:
                        st = KVO[0][kv]
                        self.op("act", lambda b=b, st=st: nc.scalar.copy(out=st.ap, in_=b.ap.rearrange("p (t c) -> p t c", t=4)),
                                [b.r()], [st.r()])
                        dn = dr["nk" if kv == 0 else "nv"]
                        if DBG["kvout"] >= 2:
                            self.dma("sp", [(dn[s_, l, :, hp * 128:(hp + 1) * 128].rearrange("(u p) c -> p u c", p=128), st.ap[:, 2 * s_:2 * s_ + 2, :])
                                            for s_ in range(2)], [st.r()], [])
            items = []
            if sample:
                for a in range(8):
                    if a == 0:
                        krs = [0, 2, 4, 6]
                    elif a == 7:
                        krs = [24, 26, 28, 30]
                    else:
                        krs = list(range(4 * a - 4, 4 * a + 7, 2))
                    ty = 1 if a in (0, 7) else 0
                    chunks = [("loc", kr // 2, ty, 8 - (kr - 4 * a)) for kr in krs] + [("ctx", 0), ("ctx", 1)]
                    for hh in range(2):
                        items.append((a * 256, hh, chunks))
            else:
                for s in range(2):
                    chunks = [("loc", 2 * s, None, None), ("loc", 2 * s + 1, None, None)]
                    for hh in range(2):
                        items.append((s * 256, hh, chunks))

            def emit_qk(it, es):
                q0, hh, chunks = it
                hb = hh * 64
                nch = len(chunks)
                qz = self.QZ[hh]
                self.op("dve", lambda qz=qz, hb=hb, q0=q0: nc.vector.tensor_copy(out=qz.ap[hb:hb + 64, :], in_=QT.ap[hb:hb + 64, q0:q0 + 256]),
                        [QT.r(q0, q0 + 256)], [qz.r()])
                for pair in range(0, nch, 2):
                    n2 = min(2, nch - pair)
                    bi = self.psrot("s")
                    b = self.bank(bi, 0, n2 * 256)
                    fns = []
                    reads = [qz.r()]
                    for i in range(n2):
                        ch = chunks[pair + i]
                        oap = b.ap[:, i * 256:(i + 1) * 256]
                        if ch[0] == "loc":
                            tt = ch[1]
                            kap = KT.ap[:, tt * 128:(tt + 1) * 128]
                            reads.append(KT.r(tt * 128, (tt + 1) * 128))
                            hasb = ch[2] is not None
                        else:
                            kap = KCT.ap[:, hp, ch[1] * 128:(ch[1] + 1) * 128]
                            reads.append(KCT.r())
                            hasb = False
                        fns.append(lambda oap=oap, kap=kap, hasb=hasb: nc.tensor.matmul(
                            oap, lhsT=kap, rhs=qz.ap, start=True, stop=(not hasb)))
                        if hasb:
                            ty, s = ch[2], ch[3]
                            bap = TAB.ap[:, ty, hh, s * 64:s * 64 + 256]
                            reads.append(TAB.r())
                            fns.append(lambda oap=oap, bap=bap: nc.tensor.matmul(oap, lhsT=self.IDB.ap, rhs=bap, start=False, stop=True))
                    reads.append(self.IDB.r())
                    self.group("pe", fns, reads, [b.r()])
                    self.op("act", lambda b=b, pair=pair, n2=n2, es=es: nc.scalar.activation(
                        out=es.ap[:, pair:pair + n2, :], in_=b.ap.rearrange("p (c q) -> p c q", c=n2), func=AF.Exp),
                        [b.r()], [es.r(pair * 256, (pair + n2) * 256)])

            def emit_pv(it, es, idx):
                q0, hh, chunks = it
                hb = hh * 64
                nch = len(chunks)
                for half in range(2):
                    slot = self.rot.get("poslot", 0)
                    self.rot["poslot"] = slot + 1
                    po = self.bank(self.psrot("po"), 0, 65)
                    fns = []
                    reads = [es.r(0, nch * 256)]
                    for ci, ch in enumerate(chunks):
                        if ch[0] == "loc":
                            vap = VAUG.ap[:, ch[1], hh, :]
                            reads.append(VAUG.r(ch[1] * 130, (ch[1] + 1) * 130))
                        else:
                            vap = VC.ap[:, ch[1], 2 * hp + hh, :]
                            reads.append(VC.r())
                        fns.append(lambda ci=ci, vap=vap, po=po, half=half: nc.tensor.matmul(
                            po.ap, lhsT=es.ap[:, ci, half * 128:(half + 1) * 128], rhs=vap, start=(ci == 0), stop=(ci == nch - 1)))
                    self.group("pe", fns, reads, [po.r()])
                    rc = self.RC[slot % 2]
                    self.op("dve", lambda po=po, rc=rc: nc.vector.reciprocal(out=rc.ap, in_=po.ap[:, 64:65]), [po.r()], [rc.r()])
                    tt = (q0 + half * 128) // 128
                    self.op("dve", lambda po=po, rc=rc, tt=tt: nc.vector.tensor_scalar(
                        out=OTOK.ap[:, tt, hb:hb + 64], in0=po.ap[:, 0:64], scalar1=rc.ap, scalar2=None, op0=ALU.mult),
                        [po.r(), rc.r()], [OTOK.r(tt * 128, (tt + 1) * 128)])
                if hh == 1:
                    tt0 = q0 // 128
                    bt = self.bank(self.psrot("x"), 0, 128, BF16)
                    fns = [lambda i=i: nc.tensor.transpose(bt.ap[:, i * 128:(i + 1) * 128], OTOK.ap[:, tt0 + i, :], self.IDB.ap)
                           for i in range(2)]
                    self.group("pe", fns, [OTOK.r(tt0 * 128, (tt0 + 2) * 128), self.IDB.r()], [bt.r()])
                    self.op("dve", lambda: nc.vector.tensor_tensor(out=self.FT.ap[:, hp, q0:q0 + 256], in0=bt.ap,
                                                                   in1=SG.ap[:, q0:q0 + 256], op=ALU.mult),
                            [bt.r(), SG.r(q0, q0 + 256)], [self.FT.r(hp * TMAX + q0, hp * TMAX + q0 + 256)])

            if DBG["att"] < 2:
                continue
            emit_qk(items[0], ES[0])
            for i in range(len(items) if DBG["att"] >= 3 else 0):
                if i + 1 < len(items):
                    emit_qk(items[i + 1], ES[(i + 1) % 2])
                emit_pv(items[i], ES[i % 2], i)

    def branch_lru(self, l, T, segs):
        nc = self.nc
        dr = self.dr
        NB = T // 512
        sample = (self.ph == "S")
        V = self.VEC
        M = self.M
        L = segs[0][1]
        SB = min(512, L)
        nsb = L // SB
        o = 0
        U = self.scr(o, len(segs) * (L + 6), BF16); o += 4608
        HS = self.scr(o, TMAX, F32); o += 8192
        SG = self.scr(o, TMAX, BF16); o += 4096
        sets = []
        for i in range(2):
            d_ = {}
            for nm in ["UC", "R", "I", "G"]:
                d_[nm] = self.scr(o, 512, F32); o += 2048
            d_["UCB"] = self.scr(o, 512, BF16); o += 1024
            sets.append(d_)
        DG = [[self.scr(o + (d * 4 + j) * 256, 128, BF16) for j in range(4)] for d in range(2)]; o += 2048
        CAR = self.RC
        assert o <= self.SCR_BYTES, o
        for c in range(4):
            wu = self.wload(self.win(l, 2048 + c * 128), 8)
            wg = self.wload(self.win(l, 2560 + c * 128), 8)
            for d in range(2):
                for g, nm in enumerate(["lwa", "lwx"]):
                    bd = self.BD[d][g]
                    self.dma("pool", [(bd.ap[0:64, 0:64], dr[nm][l, d, 2 * c]), (bd.ap[64:128, 64:128], dr[nm][l, d, 2 * c + 1])],
                             [], [bd.r()])
            for d in range(2):
                for j in range(4):
                    wc = VOFF["lcw"] + ((l * 2 + d) * 4 + j) * 4 + c
                    self.op("dve", lambda d=d, j=j, wc=wc: nc.vector.tensor_scalar(out=DG[d][j].ap, in0=self.IDB.ap, scalar1=V.ap[:, wc:wc + 1],
                                                                                 scalar2=None, op0=ALU.mult),
                            [self.IDB.r(), V.r()], [DG[d][j].r()])
            for si in range(len(segs)):
                base = si * (L + 6)
                self.op("dve", lambda base=base: nc.vector.memset(U.ap[:, base:base + 3], 0.0), [], [U.r(base, base + 3)])
                self.op("dve", lambda base=base: nc.vector.memset(U.ap[:, base + 3 + L:base + 6 + L], 0.0), [],
                        [U.r(base + 3 + L, base + 6 + L)])
            for blk in range(NB):
                t0 = blk * 512
                b = self.proj_fm(wu, 8, self.XM, t0, 512, self.psrot("proj"))
                for si, (s0, sl) in enumerate(segs):
                    lo = max(s0, t0)
                    hi = min(s0 + sl, t0 + 512)
                    if lo >= hi:
                        continue
                    uo = si * (L + 6) + 3 + (lo - s0)
                    self.op("dve", lambda b=b, lo=lo, hi=hi, uo=uo: nc.vector.tensor_copy(out=U.ap[:, uo:uo + hi - lo], in_=b.ap[:, lo - t0:hi - t0]),
                            [b.r()], [U.r(uo, uo + hi - lo)])
                b = self.proj_fm(wg, 8, self.XM, t0, 512, self.psrot("proj"))
                self.op("act", lambda b=b, t0=t0: nc.scalar.activation(out=SG.ap[:, t0:t0 + 512], in_=b.ap, func=AF.Silu),
                        [b.r()], [SG.r(t0, t0 + 512)])
            if self.next_mod is not None:
                self.mod_part(self.next_mod[0], self.next_mod[1], self.next_mod[2], c)
            for si, (s0, sl) in enumerate(segs):
                ubase = si * (L + 6) + 3
                written = [False] * nsb
                has_carry = [sample, sample]
                if sample:
                    for d in range(2):
                        so_ = VOFF["st"] + (l * 2 + d) * 4 + c
                        self.op("dve", lambda d=d, so_=so_: nc.vector.tensor_copy(out=CAR[d].ap, in_=V.ap[:, so_:so_ + 1]),
                                [V.r()], [CAR[d].r()])
                for i in range(nsb):
                    subs = [(0, i), (1, nsb - 1 - i)]
                    for d, sb in subs:
                        S = sets[d]
                        UC, R, I, G, UCB = S["UC"], S["R"], S["I"], S["G"], S["UCB"]
                        wof = VOFF["lcw"] + ((l * 2 + d) * 4) * 4 + c
                        bof = VOFF["lcb"] + (l * 2 + d) * 4 + c
                        hc = M["CST"].ap[:, d * 4 + c:d * 4 + c + 1]
                        tl = sb * SB
                        shift = -3 if d == 0 else 0
                        ur = U.r(ubase + tl - 3, ubase + tl + SB + 3)
                        u0 = ubase + tl + shift
                        bi = self.psrot("x")
                        bc = self.bank(bi, 0, SB)
                        fns = [lambda j=j, bc=bc, u0=u0, d=d: nc.tensor.matmul(bc.ap, lhsT=DG[d][j].ap, rhs=U.ap[:, u0 + j:u0 + j + SB],
                                                                              start=(j == 0), stop=(j == 3)) for j in range(4)]
                        self.group("pe", fns, [ur] + [DG[d][j].r() for j in range(4)], [bc.r()])
                        self.op("act", lambda bc=bc, UCB=UCB, bof=bof: nc.scalar.activation(out=UCB.ap[:, 0:SB], in_=bc.ap, func=AF.Identity,
                                                                                          bias=V.ap[:, bof:bof + 1], scale=1.0),
                                [bc.r(), V.r()], [UCB.r(0, SB)])
                        for g, dst in enumerate([R, I]):
                            bi = self.psrot("x")
                            b = self.bank(bi, 0, SB)
                            bd = self.BD[d][g]
                            hb = M["HBV"].ap[:, g * 8 + d * 4 + c:g * 8 + d * 4 + c + 1]
                            self.group("pe", [lambda b=b, bd=bd, UCB=UCB: nc.tensor.matmul(b.ap, lhsT=bd.ap, rhs=UCB.ap[:, 0:SB], start=True, stop=True)],
                                       [bd.r(), UCB.r(0, SB)], [b.r()])
                            self.op("act", lambda b=b, dst=dst, hb=hb: nc.scalar.activation(out=dst.ap[:, 0:SB], in_=b.ap, func=AF.Tanh,
                                                                                          bias=hb, scale=0.5),
                                    [b.r(), M["HBV"].r()], [dst.r(0, SB)])
                        self.op("act", lambda R=R, hc=hc: nc.scalar.activation(out=R.ap[:, 0:SB], in_=R.ap[:, 0:SB], func=AF.Exp, scale=hc, bias=hc),
                                [R.r(0, SB), M["CST"].r()], [R.r(0, SB)])
                        self.op("dve", lambda R=R, G=G: nc.vector.tensor_tensor(out=G.ap[:, 0:SB], in0=R.ap[:, 0:SB], in1=R.ap[:, 0:SB], op=ALU.mult),
                                [R.r(0, SB)], [G.r(0, SB)])
                    for d, sb in subs:
                        G = sets[d]["G"]
                        self.op("act", lambda G=G: nc.scalar.activation(out=G.ap[:, 0:SB], in_=G.ap[:, 0:SB], func=AF.Sqrt, scale=-0.25,
                                                                        bias=self.vcol("quarter", 0)),
                                [G.r(0, SB), V.r()], [G.r(0, SB)])
                    for d, sb in subs:
                        S = sets[d]
                        UC, R, I, G = S["UC"], S["R"], S["I"], S["G"]
                        H = UC
                        tl = sb * SB
                        tg = s0 + tl
                        UCB = S["UCB"]
                        self.op("dve", lambda I=I, UCB=UCB: nc.vector.scalar_tensor_tensor(out=I.ap[:, 0:SB], in0=I.ap[:, 0:SB], scalar=1.0,
                                                                                          in1=UCB.ap[:, 0:SB], op0=ALU.add, op1=ALU.mult),
                                [I.r(0, SB), UCB.r(0, SB)], [I.r(0, SB)])
                        self.op("dve", lambda I=I, G=G: nc.vector.tensor_tensor(out=I.ap[:, 0:SB], in0=I.ap[:, 0:SB], in1=G.ap[:, 0:SB], op=ALU.mult),
                                [I.r(0, SB), G.r(0, SB)], [I.r(0, SB)])
                        if d == 0:
                            out_ap, a_ap, b_ap = H.ap[:, 0:SB], R.ap[:, 0:SB], I.ap[:, 0:SB]
                            last_col = H.ap[:, SB - 1:SB]
                        else:
                            out_ap, a_ap, b_ap = H.ap[:, 0:SB][:, ::-1], R.ap[:, 0:SB][:, ::-1], I.ap[:, 0:SB][:, ::-1]
                            last_col = H.ap[:, 0:1]
                        init = CAR[d].ap if has_carry[d] else 0.0
                        rds = [R.r(0, SB), I.r(0, SB)] + ([CAR[d].r()] if has_carry[d] else [])
                        self.op("dve", lambda out_ap=out_ap, a_ap=a_ap, b_ap=b_ap, init=init: nc.vector.tensor_tensor_scan(
                            out=out_ap, data0=a_ap, data1=b_ap, initial=init, op0=ALU.mult, op1=ALU.add), rds, [H.r(0, SB)])
                        is_last = (i == nsb - 1)
                        if not is_last:
                            self.op("dve", lambda d=d, last_col=last_col: nc.vector.tensor_copy(out=CAR[d].ap, in_=last_col),
                                    [H.r(0, SB)], [CAR[d].r()])
                            has_carry[d] = True
                        elif not sample:
                            col = ((si * DEPTH + l) * 2 + d) * 4 + c
                            self.op("dve", lambda col=col, last_col=last_col: nc.vector.tensor_copy(out=self.NH.ap[:, col:col + 1], in_=last_col),
                                    [H.r(0, SB)], [self.NH.r(col, col + 1)])
                        if not written[sb]:
                            written[sb] = True
                            self.op("act", lambda H=H, tg=tg: nc.scalar.copy(out=HS.ap[:, tg:tg + SB], in_=H.ap[:, 0:SB]),
                                    [H.r(0, SB)], [HS.r(tg, tg + SB)])
                        else:
                            self.op("dve", lambda H=H, G=G, tg=tg: nc.vector.tensor_tensor(out=G.ap[:, 0:SB], in0=H.ap[:, 0:SB], in1=HS.ap[:, tg:tg + SB], op=ALU.add),
                                    [H.r(0, SB), HS.r(tg, tg + SB)], [G.r(0, SB)])
                            self.op("dve", lambda G=G, tg=tg, c=c: nc.vector.tensor_tensor(out=self.FT.ap[:, c, tg:tg + SB], in0=G.ap[:, 0:SB],
                                                                                         in1=SG.ap[:, tg:tg + SB], op=ALU.mult),
                                    [G.r(0, SB), SG.r(tg, tg + SB)], [self.FT.r(c * TMAX + tg, c * TMAX + tg + SB)])

    def branch_conv(self, l, T, segs):
        nc = self.nc
        NB = T // 512
        V = self.VEC
        L = segs[0][1]
        SB = min(512, L)
        nsb = L // SB
        o = 0
        PR = self.scr(o, len(segs) * (L + 2), F32); o += 8704
        CB = self.scr(o, TMAX, F32); o += 8192
        SG = self.scr(o, TMAX, BF16); o += 4096
        CCT = [self.scr(o + i * 2048, 512, F32) for i in range(2)]; o += 4096
        CV = [self.scr(o + i * 2048, 512, F32) for i in range(2)]; o += 4096
        assert o <= self.SCR_BYTES
        n = 0
        for c in range(4):
            wcb = self.wload(self.win(l, 3072 + c * 128), 8)
            wcc = self.wload(self.win(l, 3584 + c * 128), 8)
            wch = self.wload(self.win(l, 4096 + c * 128), 8)
            wgc = self.wload(self.win(l, 4608 + c * 128), 8)
            for si in range(len(segs)):
                base = si * (L + 2)
                self.op("dve", lambda base=base: nc.vector.memset(PR.ap[:, base:base + 1], 0.0), [], [PR.r(base, base + 1)])
                self.op("dve", lambda base=base: nc.vector.memset(PR.ap[:, base + 1 + L:base + 2 + L], 0.0), [], [PR.r(base + 1 + L, base + 2 + L)])
            for blk in range(NB):
                t0 = blk * 512
                b1 = self.proj_fm(wcc, 8, self.XM, t0, 512, self.psrot("proj"))
                cct = CCT[blk % 2]
                self.op("act", lambda b1=b1, cct=cct: nc.scalar.copy(out=cct.ap, in_=b1.ap), [b1.r()], [cct.r()])
                b2 = self.proj_fm(wch, 8, self.XM, t0, 512, self.psrot("proj"))
                for si, (s0, sl) in enumerate(segs):
                    lo = max(s0, t0)
                    hi = min(s0 + sl, t0 + 512)
                    if lo >= hi:
                        continue
                    po_ = si * (L + 2) + 1 + (lo - s0)
                    self.op("dve", lambda b2=b2, cct=cct, lo=lo, hi=hi, po_=po_: nc.vector.tensor_tensor(
                        out=PR.ap[:, po_:po_ + hi - lo], in0=b2.ap[:, lo - t0:hi - t0], in1=cct.ap[:, lo - t0:hi - t0], op=ALU.mult),
                        [b2.r(), cct.r()], [PR.r(po_, po_ + hi - lo)])
                b3 = self.proj_fm(wcb, 8, self.XM, t0, 512, self.psrot("proj"))
                self.op("act", lambda b3=b3, t0=t0: nc.scalar.copy(out=CB.ap[:, t0:t0 + 512], in_=b3.ap), [b3.r()], [CB.r(t0, t0 + 512)])
                b4 = self.proj_fm(wgc, 8, self.XM, t0, 512, self.psrot("proj"))
                self.op("act", lambda b4=b4, t0=t0: nc.scalar.activation(out=SG.ap[:, t0:t0 + 512], in_=b4.ap, func=AF.Silu),
                        [b4.r()], [SG.r(t0, t0 + 512)])
            wof = VOFF["cw"] + (l * 3) * 4 + c
            for si, (s0, sl) in enumerate(segs):
                pbase = si * (L + 2) + 1
                for sb in range(nsb):
                    cv = CV[n % 2]
                    n += 1
                    tl = sb * SB
                    tg = s0 + tl
                    p0 = pbase + tl - 1
                    pr = PR.r(p0, p0 + SB + 2)
                    self.op("dve", lambda p0=p0, cv=cv: nc.vector.tensor_scalar(
                        out=cv.ap[:, 0:SB], in0=PR.ap[:, p0:p0 + SB], scalar1=V.ap[:, wof:wof + 1], scalar2=None, op0=ALU.mult),
                        [pr, V.r()], [cv.r(0, SB)])
                    for j in range(1, 3):
                        self.op("dve", lambda p0=p0, cv=cv, j=j: nc.vector.scalar_tensor_tensor(
                            out=cv.ap[:, 0:SB], in0=PR.ap[:, p0 + j:p0 + j + SB], scalar=V.ap[:, wof + 4 * j:wof + 4 * j + 1],
                            in1=cv.ap[:, 0:SB], op0=ALU.mult, op1=ALU.add), [pr, V.r(), cv.r(0, SB)], [cv.r(0, SB)])
                    self.op("dve", lambda cv=cv, tg=tg: nc.vector.tensor_tensor(out=cv.ap[:, 0:SB], in0=cv.ap[:, 0:SB], in1=CB.ap[:, tg:tg + SB], op=ALU.mult),
                            [cv.r(0, SB), CB.r(tg, tg + SB)], [cv.r(0, SB)])
                    self.op("dve", lambda cv=cv, tg=tg, c=c: nc.vector.tensor_tensor(out=self.FT.ap[:, c, tg:tg + SB], in0=cv.ap[:, 0:SB],
                                                                                   in1=SG.ap[:, tg:tg + SB], op=ALU.mult),
                            [cv.r(0, SB), SG.r(tg, tg + SB)], [self.FT.r(c * TMAX + tg, c * TMAX + tg + SB)])

    def final(self, ydram, T):
        nc = self.nc
        self.bmap = "norm"
        NB = T // 512
        V = self.VEC
        YF = self.scr(8192, 8 * 512, F32, (8, 512))
        YO = [self.scr(8192 + 16384 + i * 4096, 1024, F32) for i in range(2)]
        n = 0
        for blk in range(NB):
            RS = self.rstd_block(blk, 0)
            t0 = blk * 512
            for c in range(8):
                fg = VOFF["final_g"] + c
                self.op("dve", lambda c=c, fg=fg: nc.vector.scalar_tensor_tensor(
                    out=YF.ap[:, c, :], in0=self.XT.ap[:, c, t0:t0 + 512], scalar=V.ap[:, fg:fg + 1], in1=RS.ap, op0=ALU.mult, op1=ALU.mult),
                    [self.XT.r(c * TMAX + t0, c * TMAX + t0 + 512), V.r(), RS.r()], [YF.r(c * 512, (c + 1) * 512)])
            for ti in range(4):
                yo = YO[n % 2]
                n += 1
                for half in range(2):
                    bi = self.psrot("proj")
                    b = self.bank(bi)
                    fns = [lambda c=c, b=b, ti=ti: nc.tensor.transpose(b.ap[:, (c % 4) * 128:(c % 4 + 1) * 128],
                                                                       YF.ap[:, c, ti * 128:(ti + 1) * 128], self.IDF.ap)
                           for c in range(half * 4, half * 4 + 4)]
                    self.group("pe", fns, [YF.r(), self.IDF.r()], [b.r()])
                    if half == 0:
                        self.op("dve", lambda b=b, yo=yo: nc.vector.tensor_copy(out=yo.ap[:, 0:512], in_=b.ap), [b.r()], [yo.r(0, 512)])
                    else:
                        self.op("act", lambda b=b, yo=yo: nc.scalar.copy(out=yo.ap[:, 512:1024], in_=b.ap), [b.r()], [yo.r(512, 1024)])
                tok = t0 + ti * 128
                self.dma("sp", [(ydram[tok:tok + 128, :], yo.ap)], [yo.r()], [])


def _pvec(v):
    v = np.asarray(v, np.float32)
    return np.ascontiguousarray(v.reshape(-1, 128).T)


def _build_tab(rpb):
    H = rpb.shape[0]
    qc = np.arange(64)
    kc = np.arange(64)
    ws = np.clip(qc - 8, 0, 48)
    valid = (kc[:, None] >= ws[None, :]) & (kc[:, None] < ws[None, :] + 16)
    cidx = np.clip(kc[:, None] - qc[None, :] + 15, 0, 30)
    out = np.full((2, H, 128, 16, 64), NEG, np.float32)
    for ty in range(2):
        for krl in range(2):
            for j in range(16):
                e = j - 1 - krl
                if e < 0 or e > 14:
                    continue
                if ty == 0 and not (4 <= e <= 11):
                    continue
                d = 14 - e
                g = rpb[:, d, :][:, cidx]
                out[ty, :, krl * 64:(krl + 1) * 64, j, :] = np.where(valid[None], g, np.float32(NEG))
    return out.reshape(2, H, 128, 1024)


_NC_CACHE = {}


def kernel(x_prompt, x_sample, cache_k, cache_v, state_lru, c, c_ctx, norm_g, w_mod, b_mod,
           w_in, na_rpb, lru_conv_w, lru_conv_b, lru_wa, lru_ba, lru_wx, lru_bx, lru_lam,
           conv_w, w_br_na, w_br_lru, w_br_conv, w_out, final_g):
    f32 = lambda a: np.ascontiguousarray(np.asarray(a, dtype=np.float32))
    x_prompt, x_sample, cache_k, cache_v = f32(x_prompt), f32(x_sample), f32(cache_k), f32(cache_v)
    state_lru, c, c_ctx = f32(state_lru), f32(c), f32(c_ctx)
    w_mod, w_in, w_out = f32(w_mod), f32(w_in), f32(w_out)
    w_br = np.ascontiguousarray(np.stack([f32(w_br_na), f32(w_br_lru), f32(w_br_conv)], axis=1))
    lwa, lwx = f32(lru_wa), f32(lru_wx)
    na_rpb = f32(na_rpb)
    tab = np.ascontiguousarray(np.stack([_build_tab(na_rpb[l]) for l in range(DEPTH)], axis=0))
    ident = np.eye(128, dtype=np.float32)

    base = np.zeros((128, NV), np.float32)
    base[:, VOFF["cond_p"]:VOFF["cond_p"] + 8] = _pvec(c_ctx)
    base[:, VOFF["final_g"]:VOFF["final_g"] + 8] = _pvec(final_g)
    for l in range(DEPTH):
        base[:, VOFF["norm_g"] + l * 8:VOFF["norm_g"] + l * 8 + 8] = _pvec(norm_g[l])
        base[:, VOFF["b_mod"] + l * 24:VOFF["b_mod"] + l * 24 + 24] = _pvec(b_mod[l])
        for d in range(2):
            for j in range(4):
                o = VOFF["lcw"] + ((l * 2 + d) * 4 + j) * 4
                base[:, o:o + 4] = _pvec(np.asarray(lru_conv_w)[l, d, j])
            for nm, arr in [("lcb", lru_conv_b), ("lba", lru_ba), ("lbx", lru_bx), ("lam", lru_lam)]:
                o = VOFF[nm] + (l * 2 + d) * 4
                base[:, o:o + 4] = _pvec(np.asarray(arr)[l, d])
        for j in range(3):
            o = VOFF["cw"] + (l * 3 + j) * 4
            base[:, o:o + 4] = _pvec(np.asarray(conv_w)[l, j])
    base[:, VOFF["eps"]] = 1e-6
    base[:, VOFF["one"]] = 1.0
    base[:, VOFF["quarter"]] = 0.25

    in_maps = []
    for core in range(8):
        b = core % 2
        vec = base.copy()
        vec[:, VOFF["cond_s"]:VOFF["cond_s"] + 8] = _pvec(c[b])
        for l in range(DEPTH):
            for d in range(2):
                o = VOFF["st"] + (l * 2 + d) * 4
                vec[:, o:o + 4] = _pvec(state_lru[b, l, d])
        in_maps.append({
            "xs": x_sample[b],
            "xp": np.ascontiguousarray(x_prompt[2 * core:2 * core + 2].reshape(TP, D)),
            "ck": np.ascontiguousarray(cache_k[b].reshape(DEPTH, 256, 512)),
            "cv": np.ascontiguousarray(cache_v[b].reshape(DEPTH, 256, 512)),
            "vecs": vec,
            "w_mod": w_mod, "w_in": w_in, "w_br": w_br, "w_out": w_out,
            "lwa": lwa, "lwx": lwx, "tab": tab, "ident": ident,
        })
    if "nc" not in _NC_CACHE:
        _NC_CACHE["nc"] = Builder().build()
    nc = _NC_CACHE["nc"]
    res = run_bass_kernel_spmd(nc, in_maps, core_ids=list(range(8)))
    rs = res.results
    y_prompt = np.concatenate([rs[i]["yp"].reshape(2, 256, D) for i in range(8)], axis=0)
    y_sample = np.stack([rs[0]["ys"], rs[1]["ys"]], axis=0)
    nk = np.concatenate([rs[i]["nk"].reshape(2, DEPTH, 256, 8, 64) for i in range(8)], axis=0)
    nv = np.concatenate([rs[i]["nv"].reshape(2, DEPTH, 256, 8, 64) for i in range(8)], axis=0)
    nhs = []
    for i in range(8):
        h = rs[i]["nh"].reshape(128, 2, DEPTH, 2, 4)
        nhs.append(np.transpose(h, (1, 2, 3, 4, 0)).reshape(2, DEPTH, 2, 512))
    nh = np.concatenate(nhs, axis=0)
    return (y_prompt.astype(np.float32), y_sample.astype(np.float32), nk.astype(np.float32),
            nv.astype(np.float32), nh.astype(np.float32))
```

```python
import numpy as np
import concourse.bass as bass
import concourse.mybir as mybir
from concourse.bass_utils import run_bass_kernel_spmd
from contextlib import ExitStack

F32 = mybir.dt.float32
BF16 = mybir.dt.bfloat16
AF = mybir.ActivationFunctionType
ALU = mybir.AluOpType

DEPTH = 4
D = 1024
TS = 2048
TP = 512
TMAX = 2048
NEG = -30000.0
PAGE = 512
NDS = 40
NW = 6
DBG = {"phases": "SP", "layers": DEPTH, "stages": "MNABCO", "att": 9, "merge": 1, "kvout": 2}

VOFF = {}
_o = 0
for _n, _sz in [("cond_s", 8), ("cond_p", 8), ("final_g", 8), ("norm_g", 32), ("b_mod", 96),
                ("lcw", 128), ("lcb", 32), ("lba", 32), ("lbx", 32), ("lam", 32), ("cw", 48),
                ("st", 32), ("eps", 1), ("one", 1), ("quarter", 1)]:
    VOFF[_n] = _o
    _o += _sz
NV = _o


class Buf:
    def __init__(self, ap, space, off, esz, nel):
        self.ap = ap
        self.space = space
        self.off = off
        self.esz = esz
        self.nel = nel

    def r(self, lo=0, hi=None):
        if hi is None:
            hi = self.nel
        return (self.space, self.off + lo * self.esz, self.off + hi * self.esz)


class Builder:
    def __init__(self):
        self.nc = bass.Bass("TRN2", target_bir_lowering=False)
        self.es = ExitStack()
        nc = self.nc
        self.engs = {"pe": nc.tensor, "act": nc.scalar, "dve": nc.vector, "pool": nc.gpsimd, "sp": nc.sync}
        self.sems = []
        self.semval = []
        self.esem = {}
        for e in self.engs:
            self.esem[e] = self._newsem("e_" + e)
        self.dsems = {q: [self._newsem(f"d{q}{i}") for i in range(NDS // 2)] for q in ("sp", "pool")}
        self.dnext = {"sp": 0, "pool": 0}
        self.waited = {e: {} for e in self.engs}
        self.pages = {}
        self.ninstr = 0

    def _newsem(self, name):
        s = self.es.enter_context(self.nc.semaphore(name))
        self.sems.append(s)
        self.semval.append(0)
        return len(self.sems) - 1

    @staticmethod
    def _pg(reg):
        sp, lo, hi = reg
        if sp == "ps":
            return [(sp, p) for p in range(lo // 2048, (hi - 1) // 2048 + 1)]
        return [(sp, p) for p in range(lo // PAGE, (hi - 1) // PAGE + 1)]

    @staticmethod
    def _split(reads, writes):
        r2 = [r for r in reads if r[0] != "ps"]
        w2 = list(writes) + [r for r in reads if r[0] == "ps"]
        return r2, w2

    def _waits(self, engine, reads, writes, extra=()):
        need = {}

        def add(sv):
            if sv is None:
                return
            s, v = sv
            if need.get(s, 0) < v:
                need[s] = v
        for r in reads:
            for p in self._pg(r):
                st = self.pages.get(p)
                if st:
                    add(st[0])
        for w in writes:
            for p in self._pg(w):
                st = self.pages.get(p)
                if st:
                    add(st[0])
                    for s, v in st[1].items():
                        add((s, v))
        for sv in extra:
            add(sv)
        wd = self.waited[engine]
        own = self.esem[engine]
        eng = self.engs[engine]
        for s, v in need.items():
            if engine == "pe" and s == own:
                continue
            if wd.get(s, 0) >= v:
                continue
            eng.wait_ge(self.sems[s], v)
            wd[s] = v
            self.ninstr += 1

    def _update(self, s, v, reads, writes):
        for r in reads:
            for p in self._pg(r):
                st = self.pages.get(p)
                if st is None:
                    st = [None, {}]
                    self.pages[p] = st
                st[1][s] = v
        for w in writes:
            for p in self._pg(w):
                self.pages[p] = [(s, v), {}]

    def op(self, engine, fn, reads, writes):
        reads, writes = self._split(reads, writes)
        self._waits(engine, reads, writes)
        ins = fn()
        s = self.esem[engine]
        self.semval[s] += 1
        ins.then_inc(self.sems[s], 1)
        self._update(s, self.semval[s], reads, writes)
        self.ninstr += 1

    def group(self, engine, fns, reads, writes):
        reads, writes = self._split(reads, writes)
        self._waits(engine, reads, writes)
        ins = None
        for fn in fns:
            ins = fn()
            self.ninstr += 1
        s = self.esem[engine]
        self.semval[s] += 1
        ins.then_inc(self.sems[s], 1)
        self._update(s, self.semval[s], reads, writes)

    def dma(self, queue, pairs, reads, writes):
        d = self.dsems[queue][self.dnext[queue]]
        self.dnext[queue] = (self.dnext[queue] + 1) % (NDS // 2)
        extra = [(d, self.semval[d])] if self.semval[d] > 0 else []
        self._waits(queue, reads, writes, extra)
        eng = self.engs[queue]
        for out, in_ in pairs:
            ins = eng.dma_start(out=out, in_=in_)
            self.semval[d] += 16
            ins.then_inc(self.sems[d], 16)
            self.ninstr += 1
        self._update(d, self.semval[d], reads, writes)

    def finish(self):
        sp = self.engs["sp"]
        for s in range(len(self.sems)):
            if self.semval[s] > 0 and self.waited["sp"].get(s, 0) < self.semval[s]:
                sp.wait_ge(self.sems[s], self.semval[s])

    def alloc_arena(self):
        nc = self.nc
        self.ARENA_BYTES = 211968
        self.arena = self.es.enter_context(nc.sbuf_tensor("arena", [128, self.ARENA_BYTES // 4], F32))
        self.psb = [self.es.enter_context(nc.psum_tensor(f"psb{i}", [128, 512], F32)) for i in range(8)]

    def view(self, off, nel, dt, shape=None):
        esz = 4 if dt == F32 else 2
        nb = nel * esz
        assert off % 4 == 0 and nb % 4 == 0, (off, nel)
        assert off + nb <= self.ARENA_BYTES, (off, nb)
        ap = self.arena[:, off // 4:(off + nb) // 4]
        if dt != F32:
            ap = ap.bitcast(dt)
        if shape is not None:
            if len(shape) == 2:
                ap = ap.rearrange("p (a b) -> p a b", a=shape[0], b=shape[1])
            elif len(shape) == 3:
                ap = ap.rearrange("p (a b c) -> p a b c", a=shape[0], b=shape[1], c=shape[2])
        return Buf(ap, "sb", off, esz, nel)

    def bank(self, i, lo=0, hi=512, dt=F32):
        ap = self.psb[i][:, lo:hi]
        nel = hi - lo
        esz = 4
        if dt != F32:
            ap = ap.bitcast(dt)
            nel *= 2
            esz = 2
        return Buf(ap, "ps", i * 2048 + lo * 4, esz, nel)

    def build(self):
        nc = self.nc
        dr = {}

        def din(name, shape):
            dr[name] = nc.dram_tensor(name, list(shape), F32, kind="ExternalInput").ap()

        def dout(name, shape):
            dr[name] = nc.dram_tensor(name, list(shape), F32, kind="ExternalOutput").ap()
        din("xs", [TS, D])
        din("xp", [TP, D])
        din("ck", [DEPTH, 256, 512])
        din("cv", [DEPTH, 256, 512])
        din("vecs", [128, NV])
        din("w_mod", [DEPTH, D, 3 * D])
        din("w_in", [DEPTH, D, 8192])
        din("w_br", [DEPTH, 3, 512, D])
        din("w_out", [DEPTH, D, D])
        din("lwa", [DEPTH, 2, 8, 64, 64])
        din("lwx", [DEPTH, 2, 8, 64, 64])
        din("tab", [DEPTH, 2, 8, 128, 1024])
        din("ident", [128, 128])
        dout("ys", [TS, D])
        dout("yp", [TP, D])
        dout("nk", [2, DEPTH, 256, 512])
        dout("nv", [2, DEPTH, 256, 512])
        dout("nh", [128, 64])
        self.dr = dr

        self.alloc_arena()
        o = 0
        self.XT = self.view(o, 8 * TMAX, F32, (8, TMAX)); o += 8 * TMAX * 4
        self.XM = self.view(o, 8 * TMAX, BF16, (8, TMAX)); o += 8 * TMAX * 2
        self.MG = self.view(o, 8 * TMAX, BF16, (8, TMAX)); o += 8 * TMAX * 2
        self.FT = self.view(o, 4 * TMAX, BF16, (4, TMAX)); o += 4 * TMAX * 2
        self.IDF = self.view(o, 128, F32); o += 512
        self.IDB = self.view(o, 128, BF16); o += 512
        self.ONESB = self.view(o, 128, BF16); o += 512
        self.VEC = self.view(o, 512, F32); o += 2048
        self.SC = self.view(o, 8, BF16); o += 512
        self.MODS = []
        for i in range(2):
            self.MODS.append({"MODV": self.view(o, 24, F32), "AV": self.view(o + 128, 8, F32), "CST": self.view(o + 192, 8, F32),
                              "HBV": self.view(o + 256, 16, F32)})
            o += 512
        self.modbank = None
        self.NH = self.view(o, 64, F32); o += 512
        self.RC = [self.view(o + i * 512, 1, F32) for i in range(2)]; o += 1024
        self.QZ = [self.view(o + i * 512, 256, BF16) for i in range(2)]; o += 1024
        self.BD = [[self.view(o + (d * 2 + g) * 512, 128, BF16) for g in range(2)] for d in range(2)]; o += 2048
        self.WS = [self.view(o + i * 2048, 1024, BF16, (8, 128)) for i in range(NW)]; o += NW * 2048
        self.wnext = 0
        self.SCR = o
        self.SCR_BYTES = self.ARENA_BYTES - o
        assert self.SCR_BYTES >= 41984, self.SCR_BYTES
        self.rot = {}

        V = self.VEC
        self.dma("sp", [(V.ap[:, 0:NV], dr["vecs"][:, :])], [], [V.r()])
        self.dma("sp", [(self.IDF.ap, dr["ident"][:, :])], [], [self.IDF.r()])
        self.dma("pool", [(self.IDB.ap, dr["ident"][:, :])], [], [self.IDB.r()])
        self.op("dve", lambda: nc.vector.memset(self.ONESB.ap, 1.0), [], [self.ONESB.r()])
        self.op("dve", lambda: nc.vector.memset(self.NH.ap, 0.0), [], [self.NH.r()])
        for i in range(2):
            self.op("dve", lambda i=i: nc.vector.memset(self.QZ[i].ap, 0.0), [], [self.QZ[i].r()])
        for d in range(2):
            for g in range(2):
                b = self.BD[d][g]
                self.op("dve", lambda b=b: nc.vector.memset(b.ap, 0.0), [], [b.r()])

        segs_s = [(0, 2048)]
        segs_p = [(0, 256), (256, 256)]
        if "S" in DBG["phases"]:
            self.phase("S", dr["xs"], dr["ys"], TS, segs_s, VOFF["cond_s"])
        if "P" in DBG["phases"]:
            self.phase("P", dr["xp"], dr["yp"], TP, segs_p, VOFF["cond_p"])
        self.dma("sp", [(dr["nh"][:, :], self.NH.ap)], [self.NH.r()], [])
        self.finish()
        return nc

    def vcol(self, name, idx, n=1):
        o = VOFF[name] + idx
        return self.VEC.ap[:, o:o + n]

    BANKMAPS = {
        "att": {"proj": [0, 1], "s": [2, 3], "po": [4, 5], "x": [6, 7]},
        "attcore": {"proj": [0, 1], "s": [0, 1, 2, 3], "po": [4, 5], "x": [6, 7]},
        "wide": {"proj": [0, 1, 2, 3, 4, 5], "x": [6, 7]},
        "lru": {"proj": [0, 1], "x": [2, 3, 4, 5, 6], "m": [7]},
        "norm": {"proj": [0, 1, 2, 3], "x": [4, 5, 6, 7]},
    }

    def psrot(self, cls):
        banks = self.BANKMAPS[getattr(self, "bmap", "norm")][cls]
        i = self.rot.get(cls, 0)
        self.rot[cls] = i + 1
        return banks[i % len(banks)]

    def wload(self, src, nk):
        w = self.WS[self.wnext]
        self.wnext = (self.wnext + 1) % NW
        self.dma("pool", [(w.ap[:, 0:nk, :], src)], [], [w.r(0, nk * 128)])
        return w

    def win(self, l, col):
        return self.dr["w_in"][l].rearrange("(k p) n -> p k n", p=128)[:, :, col:col + 128]

    def proj_fm(self, w, nk, src, t0, n, bank_i):
        nc = self.nc
        b = self.bank(bank_i, 0, n)
        fns = []
        reads = [w.r(0, nk * 128)]
        for k in range(nk):
            fns.append(lambda k=k: nc.tensor.matmul(b.ap, lhsT=w.ap[:, k, :], rhs=src.ap[:, k, t0:t0 + n],
                                                    start=(k == 0), stop=(k == nk - 1)))
            reads.append(src.r(k * TMAX + t0, k * TMAX + t0 + n))
        self.group("pe", fns, reads, [b.r()])
        return b

    def scr(self, off, nel, dt, shape=None):
        return self.view(self.SCR + off, nel, dt, shape)

    def phase(self, ph, xdram, ydram, T, segs, cond_off):
        nc = self.nc
        NB = T // 512
        NTT = T // 128
        self.T = T
        self.ph = ph
        self.bmap = "norm"
        XIN = [self.scr(i * 4096, 1024, F32) for i in range(2)]
        for tt in range(NTT):
            xin = XIN[tt % 2]
            self.dma("sp", [(xin.ap, xdram[tt * 128:(tt + 1) * 128, :])], [], [xin.r()])
            for half in range(2):
                bi = self.psrot("proj")
                b = self.bank(bi)
                fns = [lambda c=c, b=b, xin=xin: nc.tensor.transpose(
                    b.ap[:, (c % 4) * 128:(c % 4 + 1) * 128], xin.ap[:, c * 128:(c + 1) * 128], self.IDF.ap)
                    for c in range(half * 4, half * 4 + 4)]
                self.group("pe", fns, [xin.r(), self.IDF.r()], [b.r()])
                out = self.XT.ap[:, half * 4:half * 4 + 4, tt * 128:(tt + 1) * 128]
                in_ = b.ap.rearrange("p (c t) -> p c t", c=4)
                wr = [self.XT.r(c * TMAX + tt * 128, c * TMAX + (tt + 1) * 128) for c in range(half * 4, half * 4 + 4)]
                eng = "dve" if half == 0 else "act"
                if eng == "dve":
                    self.op("dve", lambda out=out, in_=in_: nc.vector.tensor_copy(out=out, in_=in_), [b.r()], wr)
                else:
                    self.op("act", lambda out=out, in_=in_: nc.scalar.copy(out=out, in_=in_), [b.r()], wr)
        for l in range(DBG["layers"]):
            self.layer(l, T, segs, cond_off)
        self.final(ydram, T)

    def rstd_block(self, blk, off):
        nc = self.nc
        SQ = [self.scr(off + i * 1024, 512, BF16) for i in range(2)] + [self.scr(off + 6144 + i * 1024, 512, BF16) for i in range(2)]
        TMP = self.scr(off + 2048, 512, F32)
        RS = self.scr(off + 4096, 512, F32)
        t0 = blk * 512
        bi = self.psrot("x")
        b = self.bank(bi)
        for c in range(8):
            sq = SQ[c % 4]
            xr = self.XT.r(c * TMAX + t0, c * TMAX + t0 + 512)
            if c % 2 == 0:
                self.op("act", lambda c=c, sq=sq: nc.scalar.activation(out=sq.ap, in_=self.XT.ap[:, c, t0:t0 + 512], func=AF.Square),
                        [xr], [sq.r()])
            else:
                self.op("pool", lambda c=c, sq=sq: nc.gpsimd.tensor_tensor(out=sq.ap, in0=self.XT.ap[:, c, t0:t0 + 512],
                                                                          in1=self.XT.ap[:, c, t0:t0 + 512], op=ALU.mult),
                        [xr], [sq.r()])
            self.group("pe", [lambda c=c, sq=sq: nc.tensor.matmul(b.ap, lhsT=self.ONESB.ap, rhs=sq.ap, start=(c == 0), stop=(c == 7))],
                       [sq.r(), self.ONESB.r()], [b.r()])
        self.op("act", lambda: nc.scalar.activation(out=TMP.ap, in_=b.ap, func=AF.Sqrt, scale=1.0 / D, bias=self.vcol("eps", 0)),
                [b.r(), self.VEC.r()], [TMP.r()])
        self.op("dve", lambda: nc.vector.reciprocal(out=RS.ap, in_=TMP.ap), [TMP.r()], [RS.r()])
        return RS

    def mod_part(self, key, cond_off, slot, part):
        nc = self.nc
        dr = self.dr
        V = self.VEC
        l = key[1]
        M = self.MODS[slot]
        if part == 0:
            self.op("act", lambda: nc.scalar.activation(out=self.SC.ap, in_=V.ap[:, cond_off:cond_off + 8], func=AF.Silu),
                    [V.r()], [self.SC.r()])
        bm = self.bank(7, 0, 24)
        wmod = dr["w_mod"][l].rearrange("(k p) n -> p k n", p=128)
        for j in range(part * 6, part * 6 + 6):
            w = self.wload(wmod[:, :, j * 128:(j + 1) * 128], 8)
            fns = [lambda k=k, w=w, j=j: nc.tensor.matmul(bm.ap[:, j:j + 1], lhsT=w.ap[:, k, :], rhs=self.SC.ap[:, k:k + 1],
                                                          start=(k == 0), stop=(k == 7)) for k in range(8)]
            self.group("pe", fns, [w.r(), self.SC.r()], [bm.r()])
        if part < 3:
            return
        MODV, AV, CST, HBV = M["MODV"], M["AV"], M["CST"], M["HBV"]
        ob = VOFF["b_mod"] + l * 24
        self.op("dve", lambda: nc.vector.tensor_tensor(out=MODV.ap, in0=bm.ap, in1=V.ap[:, ob:ob + 24], op=ALU.add),
                [bm.r(), V.r()], [MODV.r()])
        og = VOFF["norm_g"] + l * 8
        self.op("dve", lambda: nc.vector.scalar_tensor_tensor(out=AV.ap, in0=MODV.ap[:, 8:16], scalar=1.0,
                                                              in1=V.ap[:, og:og + 8], op0=ALU.add, op1=ALU.mult),
                [MODV.r(), V.r()], [AV.r()])
        self.op("dve", lambda: nc.vector.tensor_scalar(out=MODV.ap[:, 16:24], in0=MODV.ap[:, 16:24], scalar1=0.5, scalar2=None, op0=ALU.mult),
                [MODV.r()], [MODV.r()])
        ol = VOFF["lam"] + l * 8
        self.op("act", lambda: nc.scalar.activation(out=CST.ap, in_=V.ap[:, ol:ol + 8], func=AF.Exp, scale=-1.0),
                [V.r()], [CST.r()])
        self.op("act", lambda: nc.scalar.activation(out=CST.ap, in_=CST.ap, func=AF.Ln, bias=self.vcol("one", 0), scale=1.0),
                [CST.r(), V.r()], [CST.r()])
        self.op("dve", lambda: nc.vector.tensor_scalar(out=CST.ap, in0=CST.ap, scalar1=-4.0, scalar2=None, op0=ALU.mult),
                [CST.r()], [CST.r()])
        for gi, nm in enumerate(["lba", "lbx"]):
            ov = VOFF[nm] + l * 8
            self.op("dve", lambda gi=gi, ov=ov: nc.vector.tensor_scalar(out=HBV.ap[:, gi * 8:gi * 8 + 8], in0=V.ap[:, ov:ov + 8], scalar1=0.5,
                                                                       scalar2=None, op0=ALU.mult), [V.r()], [HBV.r()])
        self.mod_ready = key

    def layer(self, l, T, segs, cond_off):
        nc = self.nc
        dr = self.dr
        NB = T // 512
        NTT = T // 128
        V = self.VEC
        sample = (self.ph == "S")

        key = (self.ph, l)
        slot = l % 2
        if getattr(self, "mod_ready", None) != key:
            for part in range(4):
                self.mod_part(key, cond_off, slot, part)
        M = self.MODS[slot]
        self.M = M
        if l + 1 < DBG["layers"]:
            self.next_mod = ((self.ph, l + 1), cond_off, (l + 1) % 2)
        elif self.ph == "S" and "P" in DBG["phases"] and DBG["layers"] > 0:
            self.next_mod = (("P", 0), VOFF["cond_p"], 0)
        else:
            self.next_mod = None

        self.bmap = "norm"
        T1 = [self.scr(8192 + i * 2048, 512, F32) for i in range(2)]
        for blk in range(NB):
            RS = self.rstd_block(blk, 0)
            t0 = blk * 512
            for c in range(8):
                t1 = T1[c % 2]
                xr = self.XT.r(c * TMAX + t0, c * TMAX + t0 + 512)
                self.op("dve", lambda c=c, t1=t1: nc.vector.scalar_tensor_tensor(
                    out=t1.ap, in0=self.XT.ap[:, c, t0:t0 + 512], scalar=M["AV"].ap[:, c:c + 1], in1=RS.ap,
                    op0=ALU.mult, op1=ALU.mult), [xr, M["AV"].r(), RS.r()], [t1.r()])
                self.op("act", lambda c=c, t1=t1: nc.scalar.activation(
                    out=self.XM.ap[:, c, t0:t0 + 512], in_=t1.ap, func=AF.Identity, bias=M["MODV"].ap[:, c:c + 1], scale=1.0),
                    [t1.r(), M["MODV"].r()], [self.XM.r(c * TMAX + t0, c * TMAX + t0 + 512)])

        st_ = DBG["stages"]
        self.mg_init = False
        if "A" in st_:
            self.bmap = "att"
            self.branch_attention(l, T, segs)
            self.merge(l, 0, T)
        if "B" in st_:
            self.bmap = "lru"
            self.branch_lru(l, T, segs)
            self.merge(l, 1, T)
        if "C" in st_:
            self.bmap = "wide"
            self.branch_conv(l, T, segs)
            self.merge(l, 2, T)
        if "O" not in st_:
            return
        self.bmap = "wide"
        wout = dr["w_out"][l].rearrange("(k p) n -> p k n", p=128)
        for j in range(8):
            w = self.wload(wout[:, :, j * 128:(j + 1) * 128], 8)
            for blk in range(NB):
                t0 = blk * 512
                b = self.proj_fm(w, 8, self.MG, t0, 512, self.psrot("proj"))
                xr = self.XT.r(j * TMAX + t0, j * TMAX + t0 + 512)
                self.op("dve", lambda j=j, b=b, t0=t0: nc.vector.scalar_tensor_tensor(
                    out=self.XT.ap[:, j, t0:t0 + 512], in0=b.ap, scalar=M["MODV"].ap[:, 16 + j:17 + j],
                    in1=self.XT.ap[:, j, t0:t0 + 512], op0=ALU.mult, op1=ALU.add),
                    [b.r(), M["MODV"].r(), xr], [xr])

    def merge(self, l, br, T):
        if not DBG["merge"]:
            return
        self.bmap = "wide"
        nc = self.nc
        dr = self.dr
        NB = T // 512
        SGM = [self.scr(i * 2048, 512, F32) for i in range(2)]
        TMPM = [self.scr(4096 + i * 2048, 512, F32) for i in range(2)]
        wbr = dr["w_br"][l, br].rearrange("(k p) n -> p k n", p=128)
        n = 0
        for j in range(8):
            w1 = self.wload(wbr[:, :, j * 128:(j + 1) * 128], 4)
            w2 = self.wload(self.win(l, 5120 + br * 1024 + j * 128), 8)
            for blk in range(NB):
                t0 = blk * 512
                b1 = self.proj_fm(w1, 4, self.FT, t0, 512, self.psrot("proj"))
                b2 = self.proj_fm(w2, 8, self.XM, t0, 512, self.psrot("proj"))
                sg = SGM[n % 2]
                tm = TMPM[n % 2]
                n += 1
                self.op("act", lambda sg=sg, b2=b2: nc.scalar.activation(out=sg.ap, in_=b2.ap, func=AF.Tanh, scale=0.5),
                        [b2.r()], [sg.r()])
                mr = self.MG.r(j * TMAX + t0, j * TMAX + t0 + 512)
                mg = self.MG.ap[:, j, t0:t0 + 512]
                if not self.mg_init:
                    self.op("dve", lambda sg=sg, b1=b1, mg=mg: nc.vector.scalar_tensor_tensor(out=mg, in0=sg.ap, scalar=1.0, in1=b1.ap,
                                                                                            op0=ALU.add, op1=ALU.mult),
                            [b1.r(), sg.r()], [mr])
                else:
                    self.op("dve", lambda sg=sg, b1=b1, tm=tm: nc.vector.scalar_tensor_tensor(out=tm.ap, in0=sg.ap, scalar=1.0, in1=b1.ap,
                                                                                            op0=ALU.add, op1=ALU.mult),
                            [b1.r(), sg.r()], [tm.r()])
                    self.op("dve", lambda tm=tm, mg=mg: nc.vector.tensor_tensor(out=mg, in0=mg, in1=tm.ap, op=ALU.add),
                            [tm.r(), mr], [mr])
        self.mg_init = True

    def branch_attention(self, l, T, segs):
        nc = self.nc
        dr = self.dr
        NB = T // 512
        NTT = T // 128
        sample = (self.ph == "S")
        o = 0
        QT = self.scr(o, TMAX, BF16); o += 4096
        KT = self.scr(o, TMAX, BF16); o += 4096
        SG = self.scr(o, TMAX, BF16); o += 4096
        OTOK = self.scr(o, 16 * 128, BF16, (16, 128)); o += 4096
        VAUG = self.scr(o, 16 * 130, BF16, (16, 2, 65)); o += 4608
        ES = [self.scr(o + i * 4096, 8 * 256, BF16, (8, 256)) for i in range(2)]
        CKS = self.scr(o, 2 * 512, F32, (2, 512))
        o += 8192
        TAB = self.scr(o, 2 * 2 * 1024, BF16, (2, 2, 1024)); o += 8192
        KCT = self.scr(o, 4 * 256, BF16, (4, 256)); o += 2048
        VC = self.scr(o, 2 * 8 * 65, BF16, (2, 8, 65)); o += 2560
        assert o <= self.SCR_BYTES, o
        KVO = [[self.scr(TAB.off - self.SCR + kv * 2048, 512, F32, (4, 128)) for kv in range(2)]]

        if sample:
            self.dma("sp", [(CKS.ap, dr["ck"][l].rearrange("(u p) f -> p u f", p=128))], [], [CKS.r()])
            for u in range(2):
                bi = self.psrot("proj")
                b = self.bank(bi)
                fns = [lambda hp=hp, u=u, b=b: nc.tensor.transpose(b.ap[:, hp * 128:(hp + 1) * 128],
                                                                   CKS.ap[:, u, hp * 128:(hp + 1) * 128], self.IDF.ap) for hp in range(4)]
                self.group("pe", fns, [CKS.r(), self.IDF.r()], [b.r()])
                self.op("dve", lambda u=u, b=b: nc.vector.tensor_copy(out=KCT.ap[:, :, u * 128:(u + 1) * 128],
                                                                      in_=b.ap.rearrange("p (h t) -> p h t", h=4)),
                        [b.r()], [KCT.r()])
            self.op("dve", lambda: nc.vector.memset(VC.ap[:, :, :, 64:65], 1.0), [], [VC.r()])
            self.dma("pool", [(VC.ap[:, u, :, 0:64], dr["cv"][l][u * 128:(u + 1) * 128, :].rearrange("p (h d) -> p h d", d=64)) for u in range(2)],
                     [], [VC.r()])
        self.op("dve", lambda: nc.vector.memset(VAUG.ap[:, :, :, 64:65], 1.0), [], [VAUG.r()])

        for hp in range(4):
            self.bmap = "att"
            wq = self.wload(self.win(l, hp * 128), 8)
            wk = self.wload(self.win(l, 512 + hp * 128), 8)
            wv = self.wload(self.win(l, 1024 + hp * 128), 8)
            wg = self.wload(self.win(l, 1536 + hp * 128), 8)
            if sample:
                self.dma("pool", [(TAB.ap[:, ty], dr["tab"][l, ty, 2 * hp:2 * hp + 2].rearrange("h p n -> p h n")) for ty in range(2)],
                         [], [TAB.r()])
            for blk in range(NB):
                t0 = blk * 512
                b = self.proj_fm(wq, 8, self.XM, t0, 512, self.psrot("proj"))
                self.op("act", lambda b=b, t0=t0: nc.scalar.activation(out=QT.ap[:, t0:t0 + 512], in_=b.ap, func=AF.Copy, scale=0.125),
                        [b.r()], [QT.r(t0, t0 + 512)])
                b = self.proj_fm(wk, 8, self.XM, t0, 512, self.psrot("proj"))
                self.op("dve", lambda b=b, t0=t0: nc.vector.tensor_copy(out=KT.ap[:, t0:t0 + 512], in_=b.ap),
                        [b.r()], [KT.r(t0, t0 + 512)])
                b = self.proj_fm(wg, 8, self.XM, t0, 512, self.psrot("proj"))
                self.op("act", lambda b=b, t0=t0: nc.scalar.activation(out=SG.ap[:, t0:t0 + 512], in_=b.ap, func=AF.Silu),
                        [b.r()], [SG.r(t0, t0 + 512)])
            for g4 in range(NTT // 4 if DBG["att"] >= 1 else 0):
                for kv in ([1] if sample else [0, 1]):
                    w = wv if kv == 1 else wk
                    bi = self.psrot("x")
                    b = self.bank(bi)
                    fns = []
                    reads = [w.r()]
                    for i in range(4):
                        tt = g4 * 4 + i
                        for k in range(8):
                            fns.append(lambda i=i, tt=tt, k=k, w=w, b=b: nc.tensor.matmul(
                                b.ap[:, i * 128:(i + 1) * 128], lhsT=self.XM.ap[:, k, tt * 128:(tt + 1) * 128], rhs=w.ap[:, k, :],
                                start=(k == 0), stop=(k == 7)))
                    for k in range(8):
                        reads.append(self.XM.r(k * TMAX + g4 * 512, k * TMAX + g4 * 512 + 512))
                    self.group("pe", fns, reads, [b.r()])
                    if kv == 1:
                        self.op("dve", lambda b=b, g4=g4: nc.vector.tensor_copy(
                            out=VAUG.ap[:, g4 * 4:g4 * 4 + 4, :, 0:64], in_=b.ap.rearrange("p (t h d) -> p t h d", t=4, h=2)),
                            [b.r()], [VAUG.r(g4 * 4 * 130, (g4 * 4 + 4) * 130)])
                    if not sample and DBG["kvout"]:
                        st = KVO[0][kv]
                        self.op("act", lambda b=b, st=st: nc.scalar.copy(out=st.ap, in_=b.ap.rearrange("p (t c) -> p t c", t=4)),
                                [b.r()], [st.r()])
                        dn = dr["nk" if kv == 0 else "nv"]
                        if DBG["kvout"] >= 2:
                            self.dma("sp", [(dn[s_, l, :, hp * 128:(hp + 1) * 128].rearrange("(u p) c -> p u c", p=128), st.ap[:, 2 * s_:2 * s_ + 2, :])
                                            for s_ in range(2)], [st.r()], [])
            items = []
            if sample:
                for a in range(8):
                    if a == 0:
                        krs = [0, 2, 4, 6]
                    elif a == 7:
                        krs = [24, 26, 28, 30]
                    else:
                        krs = list(range(4 * a - 4, 4 * a + 7, 2))
                    ty = 1 if a in (0, 7) else 0
                    chunks = [("loc", kr // 2, ty, 8 - (kr - 4 * a)) for kr in krs] + [("ctx", 0), ("ctx", 1)]
                    for hh in range(2):
                        items.append((a * 256, hh, chunks))
            else:
                for s in range(2):
                    chunks = [("loc", 2 * s, None, None), ("loc", 2 * s + 1, None, None)]
                    for hh in range(2):
                        items.append((s * 256, hh, chunks))

            def emit_qk(it, es):
                q0, hh, chunks = it
                hb = hh * 64
                nch = len(chunks)
                qz = self.QZ[hh]
                self.op("dve", lambda qz=qz, hb=hb, q0=q0: nc.vector.tensor_copy(out=qz.ap[hb:hb + 64, :], in_=QT.ap[hb:hb + 64, q0:q0 + 256]),
                        [QT.r(q0, q0 + 256)], [qz.r()])
                for pair in range(0, nch, 2):
                    n2 = min(2, nch - pair)
                    bi = self.psrot("s")
                    b = self.bank(bi, 0, n2 * 256)
                    fns = []
                    reads = [qz.r()]
                    for i in range(n2):
                        ch = chunks[pair + i]
                        oap = b.ap[:, i * 256:(i + 1) * 256]
                        if ch[0] == "loc":
                            tt = ch[1]
                            kap = KT.ap[:, tt * 128:(tt + 1) * 128]
                            reads.append(KT.r(tt * 128, (tt + 1) * 128))
                            hasb = ch[2] is not None
                        else:
                            kap = KCT.ap[:, hp, ch[1] * 128:(ch[1] + 1) * 128]
                            reads.append(KCT.r())
                            hasb = False
                        fns.append(lambda oap=oap, kap=kap, hasb=hasb: nc.tensor.matmul(
                            oap, lhsT=kap, rhs=qz.ap, start=True, stop=(not hasb)))
                        if hasb:
                            ty, s = ch[2], ch[3]
                            bap = TAB.ap[:, ty, hh, s * 64:s * 64 + 256]
                            reads.append(TAB.r())
                            fns.append(lambda oap=oap, bap=bap: nc.tensor.matmul(oap, lhsT=self.IDB.ap, rhs=bap, start=False, stop=True))
                    reads.append(self.IDB.r())
                    self.group("pe", fns, reads, [b.r()])
                    self.op("act", lambda b=b, pair=pair, n2=n2, es=es: nc.scalar.activation(
                        out=es.ap[:, pair:pair + n2, :], in_=b.ap.rearrange("p (c q) -> p c q", c=n2), func=AF.Exp),
                        [b.r()], [es.r(pair * 256, (pair + n2) * 256)])

            def emit_pv(it, es, idx):
                q0, hh, chunks = it
                hb = hh * 64
                nch = len(chunks)
                for half in range(2):
                    slot = self.rot.get("poslot", 0)
                    self.rot["poslot"] = slot + 1
                    po = self.bank(self.psrot("po"), 0, 65)
                    fns = []
                    reads = [es.r(0, nch * 256)]
                    for ci, ch in enumerate(chunks):
                        if ch[0] == "loc":
                            vap = VAUG.ap[:, ch[1], hh, :]
                            reads.append(VAUG.r(ch[1] * 130, (ch[1] + 1) * 130))
                        else:
                            vap = VC.ap[:, ch[1], 2 * hp + hh, :]
                            reads.append(VC.r())
                        fns.append(lambda ci=ci, vap=vap, po=po, half=half: nc.tensor.matmul(
                            po.ap, lhsT=es.ap[:, ci, half * 128:(half + 1) * 128], rhs=vap, start=(ci == 0), stop=(ci == nch - 1)))
                    self.group("pe", fns, reads, [po.r()])
                    rc = self.RC[slot % 2]
                    self.op("dve", lambda po=po, rc=rc: nc.vector.reciprocal(out=rc.ap, in_=po.ap[:, 64:65]), [po.r()], [rc.r()])
                    tt = (q0 + half * 128) // 128
                    self.op("dve", lambda po=po, rc=rc, tt=tt: nc.vector.tensor_scalar(
                        out=OTOK.ap[:, tt, hb:hb + 64], in0=po.ap[:, 0:64], scalar1=rc.ap, scalar2=None, op0=ALU.mult),
                        [po.r(), rc.r()], [OTOK.r(tt * 128, (tt + 1) * 128)])
                if hh == 1:
                    tt0 = q0 // 128
                    bt = self.bank(self.psrot("x"), 0, 128, BF16)
                    fns = [lambda i=i: nc.tensor.transpose(bt.ap[:, i * 128:(i + 1) * 128], OTOK.ap[:, tt0 + i, :], self.IDB.ap)
                           for i in range(2)]
                    self.group("pe", fns, [OTOK.r(tt0 * 128, (tt0 + 2) * 128), self.IDB.r()], [bt.r()])
                    self.op("dve", lambda: nc.vector.tensor_tensor(out=self.FT.ap[:, hp, q0:q0 + 256], in0=bt.ap,
                                                                   in1=SG.ap[:, q0:q0 + 256], op=ALU.mult),
                            [bt.r(), SG.r(q0, q0 + 256)], [self.FT.r(hp * TMAX + q0, hp * TMAX + q0 + 256)])

            if DBG["att"] < 2:
                continue
            self.bmap = "attcore"
            emit_qk(items[0], ES[0])
            for i in range(len(items) if DBG["att"] >= 3 else 0):
                if i + 1 < len(items):
                    emit_qk(items[i + 1], ES[(i + 1) % 2])
                emit_pv(items[i], ES[i % 2], i)

    def branch_lru(self, l, T, segs):
        nc = self.nc
        dr = self.dr
        NB = T // 512
        sample = (self.ph == "S")
        V = self.VEC
        M = self.M
        L = segs[0][1]
        SB = min(512, L)
        nsb = L // SB
        o = 0
        U = self.scr(o, len(segs) * (L + 6), BF16); o += 4608
        HS = self.scr(o, TMAX, F32); o += 8192
        SG = self.scr(o, TMAX, BF16); o += 4096
        sets = []
        for i in range(2):
            d_ = {}
            for nm in ["UC", "R", "I", "G"]:
                d_[nm] = self.scr(o, 512, F32); o += 2048
            d_["UCB"] = self.scr(o, 512, BF16); o += 1024
            sets.append(d_)
        DG = [[self.scr(o + (d * 4 + j) * 256, 128, BF16) for j in range(4)] for d in range(2)]; o += 2048
        CAR = self.RC
        assert o <= self.SCR_BYTES, o
        for c in range(4):
            wu = self.wload(self.win(l, 2048 + c * 128), 8)
            wg = self.wload(self.win(l, 2560 + c * 128), 8)
            for d in range(2):
                for g, nm in enumerate(["lwa", "lwx"]):
                    bd = self.BD[d][g]
                    self.dma("pool", [(bd.ap[0:64, 0:64], dr[nm][l, d, 2 * c]), (bd.ap[64:128, 64:128], dr[nm][l, d, 2 * c + 1])],
                             [], [bd.r()])
            for d in range(2):
                for j in range(4):
                    wc = VOFF["lcw"] + ((l * 2 + d) * 4 + j) * 4 + c
                    self.op("dve", lambda d=d, j=j, wc=wc: nc.vector.tensor_scalar(out=DG[d][j].ap, in0=self.IDB.ap, scalar1=V.ap[:, wc:wc + 1],
                                                                                 scalar2=None, op0=ALU.mult),
                            [self.IDB.r(), V.r()], [DG[d][j].r()])
            for si in range(len(segs)):
                base = si * (L + 6)
                self.op("dve", lambda base=base: nc.vector.memset(U.ap[:, base:base + 3], 0.0), [], [U.r(base, base + 3)])
                self.op("dve", lambda base=base: nc.vector.memset(U.ap[:, base + 3 + L:base + 6 + L], 0.0), [],
                        [U.r(base + 3 + L, base + 6 + L)])
            for blk in range(NB):
                t0 = blk * 512
                b = self.proj_fm(wu, 8, self.XM, t0, 512, self.psrot("proj"))
                for si, (s0, sl) in enumerate(segs):
                    lo = max(s0, t0)
                    hi = min(s0 + sl, t0 + 512)
                    if lo >= hi:
                        continue
                    uo = si * (L + 6) + 3 + (lo - s0)
                    self.op("dve", lambda b=b, lo=lo, hi=hi, uo=uo: nc.vector.tensor_copy(out=U.ap[:, uo:uo + hi - lo], in_=b.ap[:, lo - t0:hi - t0]),
                            [b.r()], [U.r(uo, uo + hi - lo)])
                b = self.proj_fm(wg, 8, self.XM, t0, 512, self.psrot("proj"))
                self.op("act", lambda b=b, t0=t0: nc.scalar.activation(out=SG.ap[:, t0:t0 + 512], in_=b.ap, func=AF.Silu),
                        [b.r()], [SG.r(t0, t0 + 512)])
            if self.next_mod is not None:
                self.mod_part(self.next_mod[0], self.next_mod[1], self.next_mod[2], c)
            for si, (s0, sl) in enumerate(segs):
                ubase = si * (L + 6) + 3
                written = [False] * nsb
                has_carry = [sample, sample]
                if sample:
                    for d in range(2):
                        so_ = VOFF["st"] + (l * 2 + d) * 4 + c
                        self.op("dve", lambda d=d, so_=so_: nc.vector.tensor_copy(out=CAR[d].ap, in_=V.ap[:, so_:so_ + 1]),
                                [V.r()], [CAR[d].r()])
                for i in range(nsb):
                    subs = [(0, i), (1, nsb - 1 - i)]
                    for d, sb in subs:
                        S = sets[d]
                        UC, R, I, G, UCB = S["UC"], S["R"], S["I"], S["G"], S["UCB"]
                        wof = VOFF["lcw"] + ((l * 2 + d) * 4) * 4 + c
                        bof = VOFF["lcb"] + (l * 2 + d) * 4 + c
                        hc = M["CST"].ap[:, d * 4 + c:d * 4 + c + 1]
                        tl = sb * SB
                        shift = -3 if d == 0 else 0
                        ur = U.r(ubase + tl - 3, ubase + tl + SB + 3)
                        u0 = ubase + tl + shift
                        bi = self.psrot("x")
                        bc = self.bank(bi, 0, SB)
                        fns = [lambda j=j, bc=bc, u0=u0, d=d: nc.tensor.matmul(bc.ap, lhsT=DG[d][j].ap, rhs=U.ap[:, u0 + j:u0 + j + SB],
                                                                              start=(j == 0), stop=(j == 3)) for j in range(4)]
                        self.group("pe", fns, [ur] + [DG[d][j].r() for j in range(4)], [bc.r()])
                        self.op("act", lambda bc=bc, UCB=UCB, bof=bof: nc.scalar.activation(out=UCB.ap[:, 0:SB], in_=bc.ap, func=AF.Identity,
                                                                                          bias=V.ap[:, bof:bof + 1], scale=1.0),
                                [bc.r(), V.r()], [UCB.r(0, SB)])
                        for g, dst in enumerate([R, I]):
                            bi = self.psrot("x")
                            b = self.bank(bi, 0, SB)
                            bd = self.BD[d][g]
                            hb = M["HBV"].ap[:, g * 8 + d * 4 + c:g * 8 + d * 4 + c + 1]
                            self.group("pe", [lambda b=b, bd=bd, UCB=UCB: nc.tensor.matmul(b.ap, lhsT=bd.ap, rhs=UCB.ap[:, 0:SB], start=True, stop=True)],
                                       [bd.r(), UCB.r(0, SB)], [b.r()])
                            self.op("act", lambda b=b, dst=dst, hb=hb: nc.scalar.activation(out=dst.ap[:, 0:SB], in_=b.ap, func=AF.Tanh,
                                                                                          bias=hb, scale=0.5),
                                    [b.r(), M["HBV"].r()], [dst.r(0, SB)])
                        self.op("act", lambda R=R, hc=hc: nc.scalar.activation(out=R.ap[:, 0:SB], in_=R.ap[:, 0:SB], func=AF.Exp, scale=hc, bias=hc),
                                [R.r(0, SB), M["CST"].r()], [R.r(0, SB)])
                        self.op("dve", lambda R=R, G=G: nc.vector.tensor_tensor(out=G.ap[:, 0:SB], in0=R.ap[:, 0:SB], in1=R.ap[:, 0:SB], op=ALU.mult),
                                [R.r(0, SB)], [G.r(0, SB)])
                    for d, sb in subs:
                        G = sets[d]["G"]
                        self.op("act", lambda G=G: nc.scalar.activation(out=G.ap[:, 0:SB], in_=G.ap[:, 0:SB], func=AF.Sqrt, scale=-0.25,
                                                                        bias=self.vcol("quarter", 0)),
                                [G.r(0, SB), V.r()], [G.r(0, SB)])
                    for d, sb in subs:
                        S = sets[d]
                        UC, R, I, G = S["UC"], S["R"], S["I"], S["G"]
                        H = UC
                        tl = sb * SB
                        tg = s0 + tl
                        UCB = S["UCB"]
                        self.op("dve", lambda I=I, UCB=UCB: nc.vector.scalar_tensor_tensor(out=I.ap[:, 0:SB], in0=I.ap[:, 0:SB], scalar=1.0,
                                                                                          in1=UCB.ap[:, 0:SB], op0=ALU.add, op1=ALU.mult),
                                [I.r(0, SB), UCB.r(0, SB)], [I.r(0, SB)])
                        self.op("dve", lambda I=I, G=G: nc.vector.tensor_tensor(out=I.ap[:, 0:SB], in0=I.ap[:, 0:SB], in1=G.ap[:, 0:SB], op=ALU.mult),
                                [I.r(0, SB), G.r(0, SB)], [I.r(0, SB)])
                        if d == 0:
                            out_ap, a_ap, b_ap = H.ap[:, 0:SB], R.ap[:, 0:SB], I.ap[:, 0:SB]
                            last_col = H.ap[:, SB - 1:SB]
                        else:
                            out_ap, a_ap, b_ap = H.ap[:, 0:SB][:, ::-1], R.ap[:, 0:SB][:, ::-1], I.ap[:, 0:SB][:, ::-1]
                            last_col = H.ap[:, 0:1]
                        init = CAR[d].ap if has_carry[d] else 0.0
                        rds = [R.r(0, SB), I.r(0, SB)] + ([CAR[d].r()] if has_carry[d] else [])
                        self.op("dve", lambda out_ap=out_ap, a_ap=a_ap, b_ap=b_ap, init=init: nc.vector.tensor_tensor_scan(
                            out=out_ap, data0=a_ap, data1=b_ap, initial=init, op0=ALU.mult, op1=ALU.add), rds, [H.r(0, SB)])
                        is_last = (i == nsb - 1)
                        if not is_last:
                            self.op("dve", lambda d=d, last_col=last_col: nc.vector.tensor_copy(out=CAR[d].ap, in_=last_col),
                                    [H.r(0, SB)], [CAR[d].r()])
                            has_carry[d] = True
                        elif not sample:
                            col = ((si * DEPTH + l) * 2 + d) * 4 + c
                            self.op("dve", lambda col=col, last_col=last_col: nc.vector.tensor_copy(out=self.NH.ap[:, col:col + 1], in_=last_col),
                                    [H.r(0, SB)], [self.NH.r(col, col + 1)])
                        if not written[sb]:
                            written[sb] = True
                            self.op("act", lambda H=H, tg=tg: nc.scalar.copy(out=HS.ap[:, tg:tg + SB], in_=H.ap[:, 0:SB]),
                                    [H.r(0, SB)], [HS.r(tg, tg + SB)])
                        else:
                            self.op("dve", lambda H=H, G=G, tg=tg: nc.vector.tensor_tensor(out=G.ap[:, 0:SB], in0=H.ap[:, 0:SB], in1=HS.ap[:, tg:tg + SB], op=ALU.add),
                                    [H.r(0, SB), HS.r(tg, tg + SB)], [G.r(0, SB)])
                            self.op("dve", lambda G=G, tg=tg, c=c: nc.vector.tensor_tensor(out=self.FT.ap[:, c, tg:tg + SB], in0=G.ap[:, 0:SB],
                                                                                         in1=SG.ap[:, tg:tg + SB], op=ALU.mult),
                                    [G.r(0, SB), SG.r(tg, tg + SB)], [self.FT.r(c * TMAX + tg, c * TMAX + tg + SB)])

    def branch_conv(self, l, T, segs):
        nc = self.nc
        NB = T // 512
        V = self.VEC
        L = segs[0][1]
        SB = min(512, L)
        nsb = L // SB
        o = 0
        PR = self.scr(o, len(segs) * (L + 2), F32); o += 8704
        CB = self.scr(o, TMAX, F32); o += 8192
        SG = self.scr(o, TMAX, BF16); o += 4096
        CCT = [self.scr(o + i * 2048, 512, F32) for i in range(2)]; o += 4096
        CV = [self.scr(o + i * 2048, 512, F32) for i in range(2)]; o += 4096
        assert o <= self.SCR_BYTES
        n = 0
        for c in range(4):
            wcb = self.wload(self.win(l, 3072 + c * 128), 8)
            wcc = self.wload(self.win(l, 3584 + c * 128), 8)
            wch = self.wload(self.win(l, 4096 + c * 128), 8)
            wgc = self.wload(self.win(l, 4608 + c * 128), 8)
            for si in range(len(segs)):
                base = si * (L + 2)
                self.op("dve", lambda base=base: nc.vector.memset(PR.ap[:, base:base + 1], 0.0), [], [PR.r(base, base + 1)])
                self.op("dve", lambda base=base: nc.vector.memset(PR.ap[:, base + 1 + L:base + 2 + L], 0.0), [], [PR.r(base + 1 + L, base + 2 + L)])
            for blk in range(NB):
                t0 = blk * 512
                b1 = self.proj_fm(wcc, 8, self.XM, t0, 512, self.psrot("proj"))
                cct = CCT[blk % 2]
                self.op("act", lambda b1=b1, cct=cct: nc.scalar.copy(out=cct.ap, in_=b1.ap), [b1.r()], [cct.r()])
                b2 = self.proj_fm(wch, 8, self.XM, t0, 512, self.psrot("proj"))
                for si, (s0, sl) in enumerate(segs):
                    lo = max(s0, t0)
                    hi = min(s0 + sl, t0 + 512)
                    if lo >= hi:
                        continue
                    po_ = si * (L + 2) + 1 + (lo - s0)
                    self.op("dve", lambda b2=b2, cct=cct, lo=lo, hi=hi, po_=po_: nc.vector.tensor_tensor(
                        out=PR.ap[:, po_:po_ + hi - lo], in0=b2.ap[:, lo - t0:hi - t0], in1=cct.ap[:, lo - t0:hi - t0], op=ALU.mult),
                        [b2.r(), cct.r()], [PR.r(po_, po_ + hi - lo)])
                b3 = self.proj_fm(wcb, 8, self.XM, t0, 512, self.psrot("proj"))
                self.op("act", lambda b3=b3, t0=t0: nc.scalar.copy(out=CB.ap[:, t0:t0 + 512], in_=b3.ap), [b3.r()], [CB.r(t0, t0 + 512)])
                b4 = self.proj_fm(wgc, 8, self.XM, t0, 512, self.psrot("proj"))
                self.op("act", lambda b4=b4, t0=t0: nc.scalar.activation(out=SG.ap[:, t0:t0 + 512], in_=b4.ap, func=AF.Silu),
                        [b4.r()], [SG.r(t0, t0 + 512)])
            wof = VOFF["cw"] + (l * 3) * 4 + c
            for si, (s0, sl) in enumerate(segs):
                pbase = si * (L + 2) + 1
                for sb in range(nsb):
                    cv = CV[n % 2]
                    n += 1
                    tl = sb * SB
                    tg = s0 + tl
                    p0 = pbase + tl - 1
                    pr = PR.r(p0, p0 + SB + 2)
                    self.op("dve", lambda p0=p0, cv=cv: nc.vector.tensor_scalar(
                        out=cv.ap[:, 0:SB], in0=PR.ap[:, p0:p0 + SB], scalar1=V.ap[:, wof:wof + 1], scalar2=None, op0=ALU.mult),
                        [pr, V.r()], [cv.r(0, SB)])
                    for j in range(1, 3):
                        self.op("dve", lambda p0=p0, cv=cv, j=j: nc.vector.scalar_tensor_tensor(
                            out=cv.ap[:, 0:SB], in0=PR.ap[:, p0 + j:p0 + j + SB], scalar=V.ap[:, wof + 4 * j:wof + 4 * j + 1],
                            in1=cv.ap[:, 0:SB], op0=ALU.mult, op1=ALU.add), [pr, V.r(), cv.r(0, SB)], [cv.r(0, SB)])
                    self.op("dve", lambda cv=cv, tg=tg: nc.vector.tensor_tensor(out=cv.ap[:, 0:SB], in0=cv.ap[:, 0:SB], in1=CB.ap[:, tg:tg + SB], op=ALU.mult),
                            [cv.r(0, SB), CB.r(tg, tg + SB)], [cv.r(0, SB)])
                    self.op("dve", lambda cv=cv, tg=tg, c=c: nc.vector.tensor_tensor(out=self.FT.ap[:, c, tg:tg + SB], in0=cv.ap[:, 0:SB],
                                                                                   in1=SG.ap[:, tg:tg + SB], op=ALU.mult),
                            [cv.r(0, SB), SG.r(tg, tg + SB)], [self.FT.r(c * TMAX + tg, c * TMAX + tg + SB)])

    def final(self, ydram, T):
        nc = self.nc
        self.bmap = "norm"
        NB = T // 512
        V = self.VEC
        YF = self.scr(8192, 8 * 512, F32, (8, 512))
        YO = [self.scr(8192 + 16384 + i * 4096, 1024, F32) for i in range(2)]
        n = 0
        for blk in range(NB):
            RS = self.rstd_block(blk, 0)
            t0 = blk * 512
            for c in range(8):
                fg = VOFF["final_g"] + c
                self.op("dve", lambda c=c, fg=fg: nc.vector.scalar_tensor_tensor(
                    out=YF.ap[:, c, :], in0=self.XT.ap[:, c, t0:t0 + 512], scalar=V.ap[:, fg:fg + 1], in1=RS.ap, op0=ALU.mult, op1=ALU.mult),
                    [self.XT.r(c * TMAX + t0, c * TMAX + t0 + 512), V.r(), RS.r()], [YF.r(c * 512, (c + 1) * 512)])
            for ti in range(4):
                yo = YO[n % 2]
                n += 1
                for half in range(2):
                    bi = self.psrot("proj")
                    b = self.bank(bi)
                    fns = [lambda c=c, b=b, ti=ti: nc.tensor.transpose(b.ap[:, (c % 4) * 128:(c % 4 + 1) * 128],
                                                                       YF.ap[:, c, ti * 128:(ti + 1) * 128], self.IDF.ap)
                           for c in range(half * 4, half * 4 + 4)]
                    self.group("pe", fns, [YF.r(), self.IDF.r()], [b.r()])
                    if half == 0:
                        self.op("dve", lambda b=b, yo=yo: nc.vector.tensor_copy(out=yo.ap[:, 0:512], in_=b.ap), [b.r()], [yo.r(0, 512)])
                    else:
                        self.op("act", lambda b=b, yo=yo: nc.scalar.copy(out=yo.ap[:, 512:1024], in_=b.ap), [b.r()], [yo.r(512, 1024)])
                tok = t0 + ti * 128
                self.dma("sp", [(ydram[tok:tok + 128, :], yo.ap)], [yo.r()], [])


def _pvec(v):
    v = np.asarray(v, np.float32)
    return np.ascontiguousarray(v.reshape(-1, 128).T)


def _build_tab(rpb):
    H = rpb.shape[0]
    qc = np.arange(64)
    kc = np.arange(64)
    ws = np.clip(qc - 8, 0, 48)
    valid = (kc[:, None] >= ws[None, :]) & (kc[:, None] < ws[None, :] + 16)
    cidx = np.clip(kc[:, None] - qc[None, :] + 15, 0, 30)
    out = np.full((2, H, 128, 16, 64), NEG, np.float32)
    for ty in range(2):
        for krl in range(2):
            for j in range(16):
                e = j - 1 - krl
                if e < 0 or e > 14:
                    continue
                if ty == 0 and not (4 <= e <= 11):
                    continue
                d = 14 - e
                g = rpb[:, d, :][:, cidx]
                out[ty, :, krl * 64:(krl + 1) * 64, j, :] = np.where(valid[None], g, np.float32(NEG))
    return out.reshape(2, H, 128, 1024)


_NC_CACHE = {}


def kernel(x_prompt, x_sample, cache_k, cache_v, state_lru, c, c_ctx, norm_g, w_mod, b_mod,
           w_in, na_rpb, lru_conv_w, lru_conv_b, lru_wa, lru_ba, lru_wx, lru_bx, lru_lam,
           conv_w, w_br_na, w_br_lru, w_br_conv, w_out, final_g):
    f32 = lambda a: np.ascontiguousarray(np.asarray(a, dtype=np.float32))
    x_prompt, x_sample, cache_k, cache_v = f32(x_prompt), f32(x_sample), f32(cache_k), f32(cache_v)
    state_lru, c, c_ctx = f32(state_lru), f32(c), f32(c_ctx)
    w_mod, w_in, w_out = f32(w_mod), f32(w_in), f32(w_out)
    w_br = np.ascontiguousarray(np.stack([f32(w_br_na), f32(w_br_lru), f32(w_br_conv)], axis=1))
    lwa, lwx = f32(lru_wa), f32(lru_wx)
    na_rpb = f32(na_rpb)
    tab = np.ascontiguousarray(np.stack([_build_tab(na_rpb[l]) for l in range(DEPTH)], axis=0))
    ident = np.eye(128, dtype=np.float32)

    base = np.zeros((128, NV), np.float32)
    base[:, VOFF["cond_p"]:VOFF["cond_p"] + 8] = _pvec(c_ctx)
    base[:, VOFF["final_g"]:VOFF["final_g"] + 8] = _pvec(final_g)
    for l in range(DEPTH):
        base[:, VOFF["norm_g"] + l * 8:VOFF["norm_g"] + l * 8 + 8] = _pvec(norm_g[l])
        base[:, VOFF["b_mod"] + l * 24:VOFF["b_mod"] + l * 24 + 24] = _pvec(b_mod[l])
        for d in range(2):
            for j in range(4):
                o = VOFF["lcw"] + ((l * 2 + d) * 4 + j) * 4
                base[:, o:o + 4] = _pvec(np.asarray(lru_conv_w)[l, d, j])
            for nm, arr in [("lcb", lru_conv_b), ("lba", lru_ba), ("lbx", lru_bx), ("lam", lru_lam)]:
                o = VOFF[nm] + (l * 2 + d) * 4
                base[:, o:o + 4] = _pvec(np.asarray(arr)[l, d])
        for j in range(3):
            o = VOFF["cw"] + (l * 3 + j) * 4
            base[:, o:o + 4] = _pvec(np.asarray(conv_w)[l, j])
    base[:, VOFF["eps"]] = 1e-6
    base[:, VOFF["one"]] = 1.0
    base[:, VOFF["quarter"]] = 0.25

    in_maps = []
    for core in range(8):
        b = core % 2
        vec = base.copy()
        vec[:, VOFF["cond_s"]:VOFF["cond_s"] + 8] = _pvec(c[b])
        for l in range(DEPTH):
            for d in range(2):
                o = VOFF["st"] + (l * 2 + d) * 4
                vec[:, o:o + 4] = _pvec(state_lru[b, l, d])
        in_maps.append({
            "xs": x_sample[b],
            "xp": np.ascontiguousarray(x_prompt[2 * core:2 * core + 2].reshape(TP, D)),
            "ck": np.ascontiguousarray(cache_k[b].reshape(DEPTH, 256, 512)),
            "cv": np.ascontiguousarray(cache_v[b].reshape(DEPTH, 256, 512)),
            "vecs": vec,
            "w_mod": w_mod, "w_in": w_in, "w_br": w_br, "w_out": w_out,
            "lwa": lwa, "lwx": lwx, "tab": tab, "ident": ident,
        })
    if "nc" not in _NC_CACHE:
        _NC_CACHE["nc"] = Builder().build()
    nc = _NC_CACHE["nc"]
    res = run_bass_kernel_spmd(nc, in_maps, core_ids=list(range(8)))
    rs = res.results
    y_prompt = np.concatenate([rs[i]["yp"].reshape(2, 256, D) for i in range(8)], axis=0)
    y_sample = np.stack([rs[0]["ys"], rs[1]["ys"]], axis=0)
    nk = np.concatenate([rs[i]["nk"].reshape(2, DEPTH, 256, 8, 64) for i in range(8)], axis=0)
    nv = np.concatenate([rs[i]["nv"].reshape(2, DEPTH, 256, 8, 64) for i in range(8)], axis=0)
    nhs = []
    for i in range(8):
        h = rs[i]["nh"].reshape(128, 2, DEPTH, 2, 4)
        nhs.append(np.transpose(h, (1, 2, 3, 4, 0)).reshape(2, DEPTH, 2, 512))
    nh = np.concatenate(nhs, axis=0)
    return (y_prompt.astype(np.float32), y_sample.astype(np.float32), nk.astype(np.float32),
            nv.astype(np.float32), nh.astype(np.float32))
```

```python
import numpy as np
import concourse.bass as bass
import concourse.mybir as mybir
from concourse.bass_utils import run_bass_kernel_spmd
from contextlib import ExitStack

F32 = mybir.dt.float32
BF16 = mybir.dt.bfloat16
AF = mybir.ActivationFunctionType
ALU = mybir.AluOpType

DEPTH = 4
D = 1024
TS = 2048
TP = 512
TMAX = 2048
NEG = -30000.0
PAGE = 512
NDS = 40
NW = 6
DBG = {"phases": "SP", "layers": DEPTH, "stages": "MNABCO", "att": 9, "merge": 1, "kvout": 2}

VOFF = {}
_o = 0
for _n, _sz in [("cond_s", 8), ("cond_p", 8), ("final_g", 8), ("norm_g", 32), ("b_mod", 96),
                ("lcw", 128), ("lcb", 32), ("lba", 32), ("lbx", 32), ("lam", 32), ("cw", 48),
                ("st", 32), ("eps", 1), ("one", 1), ("quarter", 1)]:
    VOFF[_n] = _o
    _o += _sz
NV = _o


class Buf:
    def __init__(self, ap, space, off, esz, nel):
        self.ap = ap
        self.space = space
        self.off = off
        self.esz = esz
        self.nel = nel

    def r(self, lo=0, hi=None):
        if hi is None:
            hi = self.nel
        return (self.space, self.off + lo * self.esz, self.off + hi * self.esz)


class Builder:
    def __init__(self):
        self.nc = bass.Bass("TRN2", target_bir_lowering=False)
        self.es = ExitStack()
        nc = self.nc
        self.engs = {"pe": nc.tensor, "act": nc.scalar, "dve": nc.vector, "pool": nc.gpsimd, "sp": nc.sync}
        self.sems = []
        self.semval = []
        self.esem = {}
        for e in self.engs:
            self.esem[e] = self._newsem("e_" + e)
        self.dsems = {q: [self._newsem(f"d{q}{i}") for i in range(NDS // 2)] for q in ("sp", "pool")}
        self.dnext = {"sp": 0, "pool": 0}
        self.waited = {e: {} for e in self.engs}
        self.pages = {}
        self.ninstr = 0

    def _newsem(self, name):
        s = self.es.enter_context(self.nc.semaphore(name))
        self.sems.append(s)
        self.semval.append(0)
        return len(self.sems) - 1

    @staticmethod
    def _pg(reg):
        sp, lo, hi = reg
        if sp == "ps":
            return [(sp, p) for p in range(lo // 2048, (hi - 1) // 2048 + 1)]
        return [(sp, p) for p in range(lo // PAGE, (hi - 1) // PAGE + 1)]

    @staticmethod
    def _split(reads, writes):
        r2 = [r for r in reads if r[0] != "ps"]
        w2 = list(writes) + [r for r in reads if r[0] == "ps"]
        return r2, w2

    def _waits(self, engine, reads, writes, extra=()):
        need = {}

        def add(sv):
            if sv is None:
                return
            s, v = sv
            if need.get(s, 0) < v:
                need[s] = v
        for r in reads:
            for p in self._pg(r):
                st = self.pages.get(p)
                if st:
                    add(st[0])
        for w in writes:
            for p in self._pg(w):
                st = self.pages.get(p)
                if st:
                    add(st[0])
                    for s, v in st[1].items():
                        add((s, v))
        for sv in extra:
            add(sv)
        wd = self.waited[engine]
        own = self.esem[engine]
        eng = self.engs[engine]
        for s, v in need.items():
            if engine == "pe" and s == own:
                continue
            if wd.get(s, 0) >= v:
                continue
            eng.wait_ge(self.sems[s], v)
            wd[s] = v
            self.ninstr += 1

    def _update(self, s, v, reads, writes):
        for r in reads:
            for p in self._pg(r):
                st = self.pages.get(p)
                if st is None:
                    st = [None, {}]
                    self.pages[p] = st
                st[1][s] = v
        for w in writes:
            for p in self._pg(w):
                self.pages[p] = [(s, v), {}]

    def op(self, engine, fn, reads, writes):
        reads, writes = self._split(reads, writes)
        self._waits(engine, reads, writes)
        ins = fn()
        s = self.esem[engine]
        self.semval[s] += 1
        ins.then_inc(self.sems[s], 1)
        self._update(s, self.semval[s], reads, writes)
        self.ninstr += 1

    def group(self, engine, fns, reads, writes):
        reads, writes = self._split(reads, writes)
        self._waits(engine, reads, writes)
        ins = None
        for fn in fns:
            ins = fn()
            self.ninstr += 1
        s = self.esem[engine]
        self.semval[s] += 1
        ins.then_inc(self.sems[s], 1)
        self._update(s, self.semval[s], reads, writes)

    def dma(self, queue, pairs, reads, writes):
        d = self.dsems[queue][self.dnext[queue]]
        self.dnext[queue] = (self.dnext[queue] + 1) % (NDS // 2)
        extra = [(d, self.semval[d])] if self.semval[d] > 0 else []
        self._waits(queue, reads, writes, extra)
        eng = self.engs[queue]
        for out, in_ in pairs:
            ins = eng.dma_start(out=out, in_=in_)
            self.semval[d] += 16
            ins.then_inc(self.sems[d], 16)
            self.ninstr += 1
        self._update(d, self.semval[d], reads, writes)

    def finish(self):
        sp = self.engs["sp"]
        for s in range(len(self.sems)):
            if self.semval[s] > 0 and self.waited["sp"].get(s, 0) < self.semval[s]:
                sp.wait_ge(self.sems[s], self.semval[s])

    def alloc_arena(self):
        nc = self.nc
        self.ARENA_BYTES = 211968
        self.arena = self.es.enter_context(nc.sbuf_tensor("arena", [128, self.ARENA_BYTES // 4], F32))
        self.psb = [self.es.enter_context(nc.psum_tensor(f"psb{i}", [128, 512], F32)) for i in range(8)]

    def view(self, off, nel, dt, shape=None):
        esz = 4 if dt == F32 else 2
        nb = nel * esz
        assert off % 4 == 0 and nb % 4 == 0, (off, nel)
        assert off + nb <= self.ARENA_BYTES, (off, nb)
        ap = self.arena[:, off // 4:(off + nb) // 4]
        if dt != F32:
            ap = ap.bitcast(dt)
        if shape is not None:
            if len(shape) == 2:
                ap = ap.rearrange("p (a b) -> p a b", a=shape[0], b=shape[1])
            elif len(shape) == 3:
                ap = ap.rearrange("p (a b c) -> p a b c", a=shape[0], b=shape[1], c=shape[2])
        return Buf(ap, "sb", off, esz, nel)

    def bank(self, i, lo=0, hi=512, dt=F32):
        ap = self.psb[i][:, lo:hi]
        nel = hi - lo
        esz = 4
        if dt != F32:
            ap = ap.bitcast(dt)
            nel *= 2
            esz = 2
        return Buf(ap, "ps", i * 2048 + lo * 4, esz, nel)

    def build(self):
        nc = self.nc
        dr = {}

        def din(name, shape):
            dr[name] = nc.dram_tensor(name, list(shape), F32, kind="ExternalInput").ap()

        def dout(name, shape):
            dr[name] = nc.dram_tensor(name, list(shape), F32, kind="ExternalOutput").ap()
        din("xs", [TS, D])
        din("xp", [TP, D])
        din("ck", [DEPTH, 256, 512])
        din("cv", [DEPTH, 256, 512])
        din("vecs", [128, NV])
        din("w_mod", [DEPTH, D, 3 * D])
        din("w_in", [DEPTH, D, 8192])
        din("w_br", [DEPTH, 3, 512, D])
        din("w_out", [DEPTH, D, D])
        din("lwa", [DEPTH, 2, 8, 64, 64])
        din("lwx", [DEPTH, 2, 8, 64, 64])
        din("tab", [DEPTH, 2, 8, 128, 1024])
        din("ident", [128, 128])
        dout("ys", [TS, D])
        dout("yp", [TP, D])
        dout("nk", [2, DEPTH, 256, 512])
        dout("nv", [2, DEPTH, 256, 512])
        dout("nh", [128, 64])
        self.dr = dr

        self.alloc_arena()
        o = 0
        self.XT = self.view(o, 8 * TMAX, F32, (8, TMAX)); o += 8 * TMAX * 4
        self.XM = self.view(o, 8 * TMAX, BF16, (8, TMAX)); o += 8 * TMAX * 2
        self.MG = self.view(o, 8 * TMAX, BF16, (8, TMAX)); o += 8 * TMAX * 2
        self.FT = self.view(o, 4 * TMAX, BF16, (4, TMAX)); o += 4 * TMAX * 2
        self.IDF = self.view(o, 128, F32); o += 512
        self.IDB = self.view(o, 128, BF16); o += 512
        self.ONESB = self.view(o, 128, BF16); o += 512
        self.VEC = self.view(o, 512, F32); o += 2048
        self.SC = self.view(o, 8, BF16); o += 512
        self.MODS = []
        for i in range(2):
            self.MODS.append({"MODV": self.view(o, 24, F32), "AV": self.view(o + 128, 8, F32), "CST": self.view(o + 192, 8, F32),
                              "HBV": self.view(o + 256, 16, F32)})
            o += 512
        self.modbank = None
        self.NH = self.view(o, 64, F32); o += 512
        self.RC = [self.view(o + i * 512, 1, F32) for i in range(2)]; o += 1024
        self.QZ = [self.view(o + i * 512, 256, BF16) for i in range(2)]; o += 1024
        self.BD = [[self.view(o + (d * 2 + g) * 512, 128, BF16) for g in range(2)] for d in range(2)]; o += 2048
        self.WS = [self.view(o + i * 2048, 1024, BF16, (8, 128)) for i in range(NW)]; o += NW * 2048
        self.wnext = 0
        self.SCR = o
        self.SCR_BYTES = self.ARENA_BYTES - o
        assert self.SCR_BYTES >= 41984, self.SCR_BYTES
        self.rot = {}

        V = self.VEC
        self.dma("sp", [(V.ap[:, 0:NV], dr["vecs"][:, :])], [], [V.r()])
        self.dma("sp", [(self.IDF.ap, dr["ident"][:, :])], [], [self.IDF.r()])
        self.dma("pool", [(self.IDB.ap, dr["ident"][:, :])], [], [self.IDB.r()])
        self.op("dve", lambda: nc.vector.memset(self.ONESB.ap, 1.0), [], [self.ONESB.r()])
        self.op("dve", lambda: nc.vector.memset(self.NH.ap, 0.0), [], [self.NH.r()])
        for i in range(2):
            self.op("dve", lambda i=i: nc.vector.memset(self.QZ[i].ap, 0.0), [], [self.QZ[i].r()])
        for d in range(2):
            for g in range(2):
                b = self.BD[d][g]
                self.op("dve", lambda b=b: nc.vector.memset(b.ap, 0.0), [], [b.r()])

        segs_s = [(0, 2048)]
        segs_p = [(0, 256), (256, 256)]
        if "S" in DBG["phases"]:
            self.phase("S", dr["xs"], dr["ys"], TS, segs_s, VOFF["cond_s"])
        if "P" in DBG["phases"]:
            self.phase("P", dr["xp"], dr["yp"], TP, segs_p, VOFF["cond_p"])
        self.dma("sp", [(dr["nh"][:, :], self.NH.ap)], [self.NH.r()], [])
        self.finish()
        return nc

    def vcol(self, name, idx, n=1):
        o = VOFF[name] + idx
        return self.VEC.ap[:, o:o + n]

    BANKMAPS = {
        "att": {"proj": [0, 1], "s": [2, 3], "po": [4, 5], "x": [6, 7]},
        "attcore": {"proj": [0, 1], "s": [0, 1, 2, 3], "po": [4, 5], "x": [6, 7]},
        "wide": {"proj": [0, 1, 2, 3, 4, 5], "x": [6, 7]},
        "lru": {"proj": [0, 1], "x": [2, 3, 4, 5, 6], "m": [7]},
        "norm": {"proj": [0, 1, 2, 3], "x": [4, 5, 6, 7]},
    }

    def psrot(self, cls):
        banks = self.BANKMAPS[getattr(self, "bmap", "norm")][cls]
        i = self.rot.get(cls, 0)
        self.rot[cls] = i + 1
        return banks[i % len(banks)]

    def wload(self, src, nk):
        w = self.WS[self.wnext]
        self.wnext = (self.wnext + 1) % NW
        self.dma("pool", [(w.ap[:, 0:nk, :], src)], [], [w.r(0, nk * 128)])
        return w

    def win(self, l, col):
        return self.dr["w_in"][l].rearrange("(k p) n -> p k n", p=128)[:, :, col:col + 128]

    def proj_fm(self, w, nk, src, t0, n, bank_i):
        nc = self.nc
        b = self.bank(bank_i, 0, n)
        fns = []
        reads = [w.r(0, nk * 128)]
        for k in range(nk):
            fns.append(lambda k=k: nc.tensor.matmul(b.ap, lhsT=w.ap[:, k, :], rhs=src.ap[:, k, t0:t0 + n],
                                                    start=(k == 0), stop=(k == nk - 1)))
            reads.append(src.r(k * TMAX + t0, k * TMAX + t0 + n))
        self.group("pe", fns, reads, [b.r()])
        return b

    def scr(self, off, nel, dt, shape=None):
        return self.view(self.SCR + off, nel, dt, shape)

    def phase(self, ph, xdram, ydram, T, segs, cond_off):
        nc = self.nc
        NB = T // 512
        NTT = T // 128
        self.T = T
        self.ph = ph
        self.bmap = "norm"
        XIN = [self.scr(i * 4096, 1024, F32) for i in range(2)]
        for tt in range(NTT):
            xin = XIN[tt % 2]
            self.dma("sp", [(xin.ap, xdram[tt * 128:(tt + 1) * 128, :])], [], [xin.r()])
            for half in range(2):
                bi = self.psrot("proj")
                b = self.bank(bi)
                fns = [lambda c=c, b=b, xin=xin: nc.tensor.transpose(
                    b.ap[:, (c % 4) * 128:(c % 4 + 1) * 128], xin.ap[:, c * 128:(c + 1) * 128], self.IDF.ap)
                    for c in range(half * 4, half * 4 + 4)]
                self.group("pe", fns, [xin.r(), self.IDF.r()], [b.r()])
                out = self.XT.ap[:, half * 4:half * 4 + 4, tt * 128:(tt + 1) * 128]
                in_ = b.ap.rearrange("p (c t) -> p c t", c=4)
                wr = [self.XT.r(c * TMAX + tt * 128, c * TMAX + (tt + 1) * 128) for c in range(half * 4, half * 4 + 4)]
                eng = "dve" if half == 0 else "act"
                if eng == "dve":
                    self.op("dve", lambda out=out, in_=in_: nc.vector.tensor_copy(out=out, in_=in_), [b.r()], wr)
                else:
                    self.op("act", lambda out=out, in_=in_: nc.scalar.copy(out=out, in_=in_), [b.r()], wr)
        for l in range(DBG["layers"]):
            self.layer(l, T, segs, cond_off)
        self.final(ydram, T)

    def rstd_block(self, blk, off):
        nc = self.nc
        SQ = [self.scr(off + i * 1024, 512, BF16) for i in range(2)] + [self.scr(off + 6144 + i * 1024, 512, BF16) for i in range(2)]
        TMP = self.scr(off + 2048, 512, F32)
        RS = self.scr(off + 4096, 512, F32)
        t0 = blk * 512
        bi = self.psrot("x")
        b = self.bank(bi)
        for c in range(8):
            sq = SQ[c % 4]
            xr = self.XT.r(c * TMAX + t0, c * TMAX + t0 + 512)
            if c % 2 == 0:
                self.op("act", lambda c=c, sq=sq: nc.scalar.activation(out=sq.ap, in_=self.XT.ap[:, c, t0:t0 + 512], func=AF.Square),
                        [xr], [sq.r()])
            else:
                self.op("pool", lambda c=c, sq=sq: nc.gpsimd.tensor_tensor(out=sq.ap, in0=self.XT.ap[:, c, t0:t0 + 512],
                                                                          in1=self.XT.ap[:, c, t0:t0 + 512], op=ALU.mult),
                        [xr], [sq.r()])
            self.group("pe", [lambda c=c, sq=sq: nc.tensor.matmul(b.ap, lhsT=self.ONESB.ap, rhs=sq.ap, start=(c == 0), stop=(c == 7))],
                       [sq.r(), self.ONESB.r()], [b.r()])
        self.op("act", lambda: nc.scalar.activation(out=TMP.ap, in_=b.ap, func=AF.Sqrt, scale=1.0 / D, bias=self.vcol("eps", 0)),
                [b.r(), self.VEC.r()], [TMP.r()])
        self.op("dve", lambda: nc.vector.reciprocal(out=RS.ap, in_=TMP.ap), [TMP.r()], [RS.r()])
        return RS

    def mod_part(self, key, cond_off, slot, part):
        nc = self.nc
        dr = self.dr
        V = self.VEC
        l = key[1]
        M = self.MODS[slot]
        if part == 0:
            self.op("act", lambda: nc.scalar.activation(out=self.SC.ap, in_=V.ap[:, cond_off:cond_off + 8], func=AF.Silu),
                    [V.r()], [self.SC.r()])
        bm = self.bank(7, 0, 24)
        wmod = dr["w_mod"][l].rearrange("(k p) n -> p k n", p=128)
        for j in range(part * 6, part * 6 + 6):
            w = self.wload(wmod[:, :, j * 128:(j + 1) * 128], 8)
            fns = [lambda k=k, w=w, j=j: nc.tensor.matmul(bm.ap[:, j:j + 1], lhsT=w.ap[:, k, :], rhs=self.SC.ap[:, k:k + 1],
                                                          start=(k == 0), stop=(k == 7)) for k in range(8)]
            self.group("pe", fns, [w.r(), self.SC.r()], [bm.r()])
        if part < 3:
            return
        MODV, AV, CST, HBV = M["MODV"], M["AV"], M["CST"], M["HBV"]
        ob = VOFF["b_mod"] + l * 24
        self.op("dve", lambda: nc.vector.tensor_tensor(out=MODV.ap, in0=bm.ap, in1=V.ap[:, ob:ob + 24], op=ALU.add),
                [bm.r(), V.r()], [MODV.r()])
        og = VOFF["norm_g"] + l * 8
        self.op("dve", lambda: nc.vector.scalar_tensor_tensor(out=AV.ap, in0=MODV.ap[:, 8:16], scalar=1.0,
                                                              in1=V.ap[:, og:og + 8], op0=ALU.add, op1=ALU.mult),
                [MODV.r(), V.r()], [AV.r()])
        self.op("dve", lambda: nc.vector.tensor_scalar(out=MODV.ap[:, 16:24], in0=MODV.ap[:, 16:24], scalar1=0.5, scalar2=None, op0=ALU.mult),
                [MODV.r()], [MODV.r()])
        ol = VOFF["lam"] + l * 8
        self.op("act", lambda: nc.scalar.activation(out=CST.ap, in_=V.ap[:, ol:ol + 8], func=AF.Exp, scale=-1.0),
                [V.r()], [CST.r()])
        self.op("act", lambda: nc.scalar.activation(out=CST.ap, in_=CST.ap, func=AF.Ln, bias=self.vcol("one", 0), scale=1.0),
                [CST.r(), V.r()], [CST.r()])
        self.op("dve", lambda: nc.vector.tensor_scalar(out=CST.ap, in0=CST.ap, scalar1=-4.0, scalar2=None, op0=ALU.mult),
                [CST.r()], [CST.r()])
        for gi, nm in enumerate(["lba", "lbx"]):
            ov = VOFF[nm] + l * 8
            self.op("dve", lambda gi=gi, ov=ov: nc.vector.tensor_scalar(out=HBV.ap[:, gi * 8:gi * 8 + 8], in0=V.ap[:, ov:ov + 8], scalar1=0.5,
                                                                       scalar2=None, op0=ALU.mult), [V.r()], [HBV.r()])
        self.mod_ready = key

    def layer(self, l, T, segs, cond_off):
        nc = self.nc
        dr = self.dr
        NB = T // 512
        NTT = T // 128
        V = self.VEC
        sample = (self.ph == "S")

        key = (self.ph, l)
        slot = l % 2
        if getattr(self, "mod_ready", None) != key:
            for part in range(4):
                self.mod_part(key, cond_off, slot, part)
        M = self.MODS[slot]
        self.M = M
        if l + 1 < DBG["layers"]:
            self.next_mod = ((self.ph, l + 1), cond_off, (l + 1) % 2)
        elif self.ph == "S" and "P" in DBG["phases"] and DBG["layers"] > 0:
            self.next_mod = (("P", 0), VOFF["cond_p"], 0)
        else:
            self.next_mod = None

        self.bmap = "norm"
        T1 = [self.scr(8192 + i * 2048, 512, F32) for i in range(4)]
        for blk in range(NB):
            RS = self.rstd_block(blk, 0)
            t0 = blk * 512
            for c in range(8):
                t1 = T1[c % 4]
                xr = self.XT.r(c * TMAX + t0, c * TMAX + t0 + 512)
                self.op("dve", lambda c=c, t1=t1: nc.vector.scalar_tensor_tensor(
                    out=t1.ap, in0=self.XT.ap[:, c, t0:t0 + 512], scalar=M["AV"].ap[:, c:c + 1], in1=RS.ap,
                    op0=ALU.mult, op1=ALU.mult), [xr, M["AV"].r(), RS.r()], [t1.r()])
                if c % 2 == 0:
                    self.op("act", lambda c=c, t1=t1: nc.scalar.activation(
                        out=self.XM.ap[:, c, t0:t0 + 512], in_=t1.ap, func=AF.Identity, bias=M["MODV"].ap[:, c:c + 1], scale=1.0),
                        [t1.r(), M["MODV"].r()], [self.XM.r(c * TMAX + t0, c * TMAX + t0 + 512)])
                else:
                    self.op("pool", lambda c=c, t1=t1: nc.gpsimd.tensor_scalar(
                        out=self.XM.ap[:, c, t0:t0 + 512], in0=t1.ap, scalar1=self.vcol("one", 0), scalar2=M["MODV"].ap[:, c:c + 1],
                        op0=ALU.mult, op1=ALU.add),
                        [t1.r(), M["MODV"].r(), V.r()], [self.XM.r(c * TMAX + t0, c * TMAX + t0 + 512)])

        st_ = DBG["stages"]
        self.mg_init = False
        if "A" in st_:
            self.bmap = "att"
            self.branch_attention(l, T, segs)
            self.merge(l, 0, T)
        if "B" in st_:
            self.bmap = "lru"
            self.branch_lru(l, T, segs)
            self.merge(l, 1, T)
        if "C" in st_:
            self.bmap = "wide"
            self.branch_conv(l, T, segs)
            self.merge(l, 2, T)
        if "O" not in st_:
            return
        self.bmap = "wide"
        wout = dr["w_out"][l].rearrange("(k p) n -> p k n", p=128)
        for j in range(8):
            w = self.wload(wout[:, :, j * 128:(j + 1) * 128], 8)
            for blk in range(NB):
                t0 = blk * 512
                b = self.proj_fm(w, 8, self.MG, t0, 512, self.psrot("proj"))
                xr = self.XT.r(j * TMAX + t0, j * TMAX + t0 + 512)
                self.op("dve", lambda j=j, b=b, t0=t0: nc.vector.scalar_tensor_tensor(
                    out=self.XT.ap[:, j, t0:t0 + 512], in0=b.ap, scalar=M["MODV"].ap[:, 16 + j:17 + j],
                    in1=self.XT.ap[:, j, t0:t0 + 512], op0=ALU.mult, op1=ALU.add),
                    [b.r(), M["MODV"].r(), xr], [xr])

    def merge(self, l, br, T):
        if not DBG["merge"]:
            return
        self.bmap = "wide"
        nc = self.nc
        dr = self.dr
        NB = T // 512
        SGM = [self.scr(i * 2048, 512, F32) for i in range(2)]
        TMPM = [self.scr(4096 + i * 2048, 512, F32) for i in range(2)]
        wbr = dr["w_br"][l, br].rearrange("(k p) n -> p k n", p=128)
        n = 0
        for j in range(8):
            w1 = self.wload(wbr[:, :, j * 128:(j + 1) * 128], 4)
            w2 = self.wload(self.win(l, 5120 + br * 1024 + j * 128), 8)
            for blk in range(NB):
                t0 = blk * 512
                b1 = self.proj_fm(w1, 4, self.FT, t0, 512, self.psrot("proj"))
                b2 = self.proj_fm(w2, 8, self.XM, t0, 512, self.psrot("proj"))
                sg = SGM[n % 2]
                tm = TMPM[n % 2]
                n += 1
                self.op("act", lambda sg=sg, b2=b2: nc.scalar.activation(out=sg.ap, in_=b2.ap, func=AF.Tanh, scale=0.5),
                        [b2.r()], [sg.r()])
                mr = self.MG.r(j * TMAX + t0, j * TMAX + t0 + 512)
                mg = self.MG.ap[:, j, t0:t0 + 512]
                if not self.mg_init:
                    self.op("dve", lambda sg=sg, b1=b1, mg=mg: nc.vector.scalar_tensor_tensor(out=mg, in0=sg.ap, scalar=1.0, in1=b1.ap,
                                                                                            op0=ALU.add, op1=ALU.mult),
                            [b1.r(), sg.r()], [mr])
                else:
                    self.op("dve", lambda sg=sg, b1=b1, tm=tm: nc.vector.scalar_tensor_tensor(out=tm.ap, in0=sg.ap, scalar=1.0, in1=b1.ap,
                                                                                            op0=ALU.add, op1=ALU.mult),
                            [b1.r(), sg.r()], [tm.r()])
                    self.op("dve", lambda tm=tm, mg=mg: nc.vector.tensor_tensor(out=mg, in0=mg, in1=tm.ap, op=ALU.add),
                            [tm.r(), mr], [mr])
        self.mg_init = True

    def branch_attention(self, l, T, segs):
        nc = self.nc
        dr = self.dr
        NB = T // 512
        NTT = T // 128
        sample = (self.ph == "S")
        o = 0
        QT = self.scr(o, TMAX, BF16); o += 4096
        KT = self.scr(o, TMAX, BF16); o += 4096
        SG = self.scr(o, TMAX, BF16); o += 4096
        OTOK = self.scr(o, 16 * 128, BF16, (16, 128)); o += 4096
        VAUG = self.scr(o, 16 * 130, BF16, (16, 2, 65)); o += 4608
        ES = [self.scr(o + i * 4096, 8 * 256, BF16, (8, 256)) for i in range(2)]
        CKS = self.scr(o, 2 * 512, F32, (2, 512))
        o += 8192
        TAB = self.scr(o, 2 * 2 * 1024, BF16, (2, 2, 1024)); o += 8192
        KCT = self.scr(o, 4 * 256, BF16, (4, 256)); o += 2048
        VC = self.scr(o, 2 * 8 * 65, BF16, (2, 8, 65)); o += 2560
        assert o <= self.SCR_BYTES, o
        KVO = [[self.scr(TAB.off - self.SCR + kv * 2048, 512, F32, (4, 128)) for kv in range(2)]]

        if sample:
            self.dma("sp", [(CKS.ap, dr["ck"][l].rearrange("(u p) f -> p u f", p=128))], [], [CKS.r()])
            for u in range(2):
                bi = self.psrot("proj")
                b = self.bank(bi)
                fns = [lambda hp=hp, u=u, b=b: nc.tensor.transpose(b.ap[:, hp * 128:(hp + 1) * 128],
                                                                   CKS.ap[:, u, hp * 128:(hp + 1) * 128], self.IDF.ap) for hp in range(4)]
                self.group("pe", fns, [CKS.r(), self.IDF.r()], [b.r()])
                self.op("dve", lambda u=u, b=b: nc.vector.tensor_copy(out=KCT.ap[:, :, u * 128:(u + 1) * 128],
                                                                      in_=b.ap.rearrange("p (h t) -> p h t", h=4)),
                        [b.r()], [KCT.r()])
            self.op("dve", lambda: nc.vector.memset(VC.ap[:, :, :, 64:65], 1.0), [], [VC.r()])
            self.dma("pool", [(VC.ap[:, u, :, 0:64], dr["cv"][l][u * 128:(u + 1) * 128, :].rearrange("p (h d) -> p h d", d=64)) for u in range(2)],
                     [], [VC.r()])
        self.op("dve", lambda: nc.vector.memset(VAUG.ap[:, :, :, 64:65], 1.0), [], [VAUG.r()])

        for hp in range(4):
            self.bmap = "att"
            wq = self.wload(self.win(l, hp * 128), 8)
            wk = self.wload(self.win(l, 512 + hp * 128), 8)
            wv = self.wload(self.win(l, 1024 + hp * 128), 8)
            wg = self.wload(self.win(l, 1536 + hp * 128), 8)
            if sample:
                self.dma("pool", [(TAB.ap[:, ty], dr["tab"][l, ty, 2 * hp:2 * hp + 2].rearrange("h p n -> p h n")) for ty in range(2)],
                         [], [TAB.r()])
            for blk in range(NB):
                t0 = blk * 512
                b = self.proj_fm(wq, 8, self.XM, t0, 512, self.psrot("proj"))
                self.op("act", lambda b=b, t0=t0: nc.scalar.activation(out=QT.ap[:, t0:t0 + 512], in_=b.ap, func=AF.Copy, scale=0.125),
                        [b.r()], [QT.r(t0, t0 + 512)])
                b = self.proj_fm(wk, 8, self.XM, t0, 512, self.psrot("proj"))
                self.op("dve", lambda b=b, t0=t0: nc.vector.tensor_copy(out=KT.ap[:, t0:t0 + 512], in_=b.ap),
                        [b.r()], [KT.r(t0, t0 + 512)])
                b = self.proj_fm(wg, 8, self.XM, t0, 512, self.psrot("proj"))
                self.op("act", lambda b=b, t0=t0: nc.scalar.activation(out=SG.ap[:, t0:t0 + 512], in_=b.ap, func=AF.Silu),
                        [b.r()], [SG.r(t0, t0 + 512)])
            for g4 in range(NTT // 4 if DBG["att"] >= 1 else 0):
                for kv in ([1] if sample else [0, 1]):
                    w = wv if kv == 1 else wk
                    bi = self.psrot("x")
                    b = self.bank(bi)
                    fns = []
                    reads = [w.r()]
                    for i in range(4):
                        tt = g4 * 4 + i
                        for k in range(8):
                            fns.append(lambda i=i, tt=tt, k=k, w=w, b=b: nc.tensor.matmul(
                                b.ap[:, i * 128:(i + 1) * 128], lhsT=self.XM.ap[:, k, tt * 128:(tt + 1) * 128], rhs=w.ap[:, k, :],
                                start=(k == 0), stop=(k == 7)))
                    for k in range(8):
                        reads.append(self.XM.r(k * TMAX + g4 * 512, k * TMAX + g4 * 512 + 512))
                    self.group("pe", fns, reads, [b.r()])
                    if kv == 1:
                        self.op("dve", lambda b=b, g4=g4: nc.vector.tensor_copy(
                            out=VAUG.ap[:, g4 * 4:g4 * 4 + 4, :, 0:64], in_=b.ap.rearrange("p (t h d) -> p t h d", t=4, h=2)),
                            [b.r()], [VAUG.r(g4 * 4 * 130, (g4 * 4 + 4) * 130)])
                    if not sample and DBG["kvout"]:
                        st = KVO[0][kv]
                        self.op("act", lambda b=b, st=st: nc.scalar.copy(out=st.ap, in_=b.ap.rearrange("p (t c) -> p t c", t=4)),
                                [b.r()], [st.r()])
                        dn = dr["nk" if kv == 0 else "nv"]
                        if DBG["kvout"] >= 2:
                            self.dma("sp", [(dn[s_, l, :, hp * 128:(hp + 1) * 128].rearrange("(u p) c -> p u c", p=128), st.ap[:, 2 * s_:2 * s_ + 2, :])
                                            for s_ in range(2)], [st.r()], [])
            items = []
            if sample:
                for a in range(8):
                    if a == 0:
                        krs = [0, 2, 4, 6]
                    elif a == 7:
                        krs = [24, 26, 28, 30]
                    else:
                        krs = list(range(4 * a - 4, 4 * a + 7, 2))
                    ty = 1 if a in (0, 7) else 0
                    chunks = [("loc", kr // 2, ty, 8 - (kr - 4 * a)) for kr in krs] + [("ctx", 0), ("ctx", 1)]
                    for hh in range(2):
                        items.append((a * 256, hh, chunks))
            else:
                for s in range(2):
                    chunks = [("loc", 2 * s, None, None), ("loc", 2 * s + 1, None, None)]
                    for hh in range(2):
                        items.append((s * 256, hh, chunks))

            def emit_qk(it, es):
                q0, hh, chunks = it
                hb = hh * 64
                nch = len(chunks)
                qz = self.QZ[hh]
                self.op("dve", lambda qz=qz, hb=hb, q0=q0: nc.vector.tensor_copy(out=qz.ap[hb:hb + 64, :], in_=QT.ap[hb:hb + 64, q0:q0 + 256]),
                        [QT.r(q0, q0 + 256)], [qz.r()])
                for pair in range(0, nch, 2):
                    n2 = min(2, nch - pair)
                    bi = self.psrot("s")
                    b = self.bank(bi, 0, n2 * 256)
                    fns = []
                    reads = [qz.r()]
                    for i in range(n2):
                        ch = chunks[pair + i]
                        oap = b.ap[:, i * 256:(i + 1) * 256]
                        if ch[0] == "loc":
                            tt = ch[1]
                            kap = KT.ap[:, tt * 128:(tt + 1) * 128]
                            reads.append(KT.r(tt * 128, (tt + 1) * 128))
                            hasb = ch[2] is not None
                        else:
                            kap = KCT.ap[:, hp, ch[1] * 128:(ch[1] + 1) * 128]
                            reads.append(KCT.r())
                            hasb = False
                        fns.append(lambda oap=oap, kap=kap, hasb=hasb: nc.tensor.matmul(
                            oap, lhsT=kap, rhs=qz.ap, start=True, stop=(not hasb)))
                        if hasb:
                            ty, s = ch[2], ch[3]
                            bap = TAB.ap[:, ty, hh, s * 64:s * 64 + 256]
                            reads.append(TAB.r())
                            fns.append(lambda oap=oap, bap=bap: nc.tensor.matmul(oap, lhsT=self.IDB.ap, rhs=bap, start=False, stop=True))
                    reads.append(self.IDB.r())
                    self.group("pe", fns, reads, [b.r()])
                    self.op("act", lambda b=b, pair=pair, n2=n2, es=es: nc.scalar.activation(
                        out=es.ap[:, pair:pair + n2, :], in_=b.ap.rearrange("p (c q) -> p c q", c=n2), func=AF.Exp),
                        [b.r()], [es.r(pair * 256, (pair + n2) * 256)])

            def emit_pv(it, es, idx):
                q0, hh, chunks = it
                hb = hh * 64
                nch = len(chunks)
                for half in range(2):
                    slot = self.rot.get("poslot", 0)
                    self.rot["poslot"] = slot + 1
                    po = self.bank(self.psrot("po"), 0, 65)
                    fns = []
                    reads = [es.r(0, nch * 256)]
                    for ci, ch in enumerate(chunks):
                        if ch[0] == "loc":
                            vap = VAUG.ap[:, ch[1], hh, :]
                            reads.append(VAUG.r(ch[1] * 130, (ch[1] + 1) * 130))
                        else:
                            vap = VC.ap[:, ch[1], 2 * hp + hh, :]
                            reads.append(VC.r())
                        fns.append(lambda ci=ci, vap=vap, po=po, half=half: nc.tensor.matmul(
                            po.ap, lhsT=es.ap[:, ci, half * 128:(half + 1) * 128], rhs=vap, start=(ci == 0), stop=(ci == nch - 1)))
                    self.group("pe", fns, reads, [po.r()])
                    rc = self.RC[slot % 2]
                    self.op("dve", lambda po=po, rc=rc: nc.vector.reciprocal(out=rc.ap, in_=po.ap[:, 64:65]), [po.r()], [rc.r()])
                    tt = (q0 + half * 128) // 128
                    self.op("dve", lambda po=po, rc=rc, tt=tt: nc.vector.tensor_scalar(
                        out=OTOK.ap[:, tt, hb:hb + 64], in0=po.ap[:, 0:64], scalar1=rc.ap, scalar2=None, op0=ALU.mult),
                        [po.r(), rc.r()], [OTOK.r(tt * 128, (tt + 1) * 128)])
                if hh == 1:
                    tt0 = q0 // 128
                    bt = self.bank(self.psrot("x"), 0, 128, BF16)
                    fns = [lambda i=i: nc.tensor.transpose(bt.ap[:, i * 128:(i + 1) * 128], OTOK.ap[:, tt0 + i, :], self.IDB.ap)
                           for i in range(2)]
                    self.group("pe", fns, [OTOK.r(tt0 * 128, (tt0 + 2) * 128), self.IDB.r()], [bt.r()])
                    self.op("dve", lambda: nc.vector.tensor_tensor(out=self.FT.ap[:, hp, q0:q0 + 256], in0=bt.ap,
                                                                   in1=SG.ap[:, q0:q0 + 256], op=ALU.mult),
                            [bt.r(), SG.r(q0, q0 + 256)], [self.FT.r(hp * TMAX + q0, hp * TMAX + q0 + 256)])

            if DBG["att"] < 2:
                continue
            self.bmap = "attcore"
            emit_qk(items[0], ES[0])
            for i in range(len(items) if DBG["att"] >= 3 else 0):
                if i + 1 < len(items):
                    emit_qk(items[i + 1], ES[(i + 1) % 2])
                emit_pv(items[i], ES[i % 2], i)

    def branch_lru(self, l, T, segs):
        nc = self.nc
        dr = self.dr
        NB = T // 512
        sample = (self.ph == "S")
        V = self.VEC
        M = self.M
        L = segs[0][1]
        SB = min(512, L)
        nsb = L // SB
        o = 0
        U = self.scr(o, len(segs) * (L + 6), BF16); o += 4608
        HS = self.scr(o, TMAX, F32); o += 8192
        SG = self.scr(o, TMAX, BF16); o += 4096
        sets = []
        for i in range(2):
            d_ = {}
            for nm in ["UC", "R", "I", "G"]:
                d_[nm] = self.scr(o, 512, F32); o += 2048
            d_["UCB"] = self.scr(o, 512, BF16); o += 1024
            sets.append(d_)
        DG = [[self.scr(o + (d * 4 + j) * 256, 128, BF16) for j in range(4)] for d in range(2)]; o += 2048
        CAR = self.RC
        assert o <= self.SCR_BYTES, o
        for c in range(4):
            wu = self.wload(self.win(l, 2048 + c * 128), 8)
            wg = self.wload(self.win(l, 2560 + c * 128), 8)
            for d in range(2):
                for g, nm in enumerate(["lwa", "lwx"]):
                    bd = self.BD[d][g]
                    self.dma("pool", [(bd.ap[0:64, 0:64], dr[nm][l, d, 2 * c]), (bd.ap[64:128, 64:128], dr[nm][l, d, 2 * c + 1])],
                             [], [bd.r()])
            for d in range(2):
                for j in range(4):
                    wc = VOFF["lcw"] + ((l * 2 + d) * 4 + j) * 4 + c
                    self.op("dve", lambda d=d, j=j, wc=wc: nc.vector.tensor_scalar(out=DG[d][j].ap, in0=self.IDB.ap, scalar1=V.ap[:, wc:wc + 1],
                                                                                 scalar2=None, op0=ALU.mult),
                            [self.IDB.r(), V.r()], [DG[d][j].r()])
            for si in range(len(segs)):
                base = si * (L + 6)
                self.op("dve", lambda base=base: nc.vector.memset(U.ap[:, base:base + 3], 0.0), [], [U.r(base, base + 3)])
                self.op("dve", lambda base=base: nc.vector.memset(U.ap[:, base + 3 + L:base + 6 + L], 0.0), [],
                        [U.r(base + 3 + L, base + 6 + L)])
            for blk in range(NB):
                t0 = blk * 512
                b = self.proj_fm(wu, 8, self.XM, t0, 512, self.psrot("proj"))
                for si, (s0, sl) in enumerate(segs):
                    lo = max(s0, t0)
                    hi = min(s0 + sl, t0 + 512)
                    if lo >= hi:
                        continue
                    uo = si * (L + 6) + 3 + (lo - s0)
                    self.op("dve", lambda b=b, lo=lo, hi=hi, uo=uo: nc.vector.tensor_copy(out=U.ap[:, uo:uo + hi - lo], in_=b.ap[:, lo - t0:hi - t0]),
                            [b.r()], [U.r(uo, uo + hi - lo)])
                b = self.proj_fm(wg, 8, self.XM, t0, 512, self.psrot("proj"))
                self.op("act", lambda b=b, t0=t0: nc.scalar.activation(out=SG.ap[:, t0:t0 + 512], in_=b.ap, func=AF.Silu),
                        [b.r()], [SG.r(t0, t0 + 512)])
            if self.next_mod is not None:
                self.mod_part(self.next_mod[0], self.next_mod[1], self.next_mod[2], c)
            for si, (s0, sl) in enumerate(segs):
                ubase = si * (L + 6) + 3
                written = [False] * nsb
                has_carry = [sample, sample]
                if sample:
                    for d in range(2):
                        so_ = VOFF["st"] + (l * 2 + d) * 4 + c
                        self.op("dve", lambda d=d, so_=so_: nc.vector.tensor_copy(out=CAR[d].ap, in_=V.ap[:, so_:so_ + 1]),
                                [V.r()], [CAR[d].r()])
                for i in range(nsb):
                    subs = [(0, i), (1, nsb - 1 - i)]
                    for d, sb in subs:
                        S = sets[d]
                        UC, R, I, G, UCB = S["UC"], S["R"], S["I"], S["G"], S["UCB"]
                        wof = VOFF["lcw"] + ((l * 2 + d) * 4) * 4 + c
                        bof = VOFF["lcb"] + (l * 2 + d) * 4 + c
                        hc = M["CST"].ap[:, d * 4 + c:d * 4 + c + 1]
                        tl = sb * SB
                        shift = -3 if d == 0 else 0
                        ur = U.r(ubase + tl - 3, ubase + tl + SB + 3)
                        u0 = ubase + tl + shift
                        bi = self.psrot("x")
                        bc = self.bank(bi, 0, SB)
                        fns = [lambda j=j, bc=bc, u0=u0, d=d: nc.tensor.matmul(bc.ap, lhsT=DG[d][j].ap, rhs=U.ap[:, u0 + j:u0 + j + SB],
                                                                              start=(j == 0), stop=(j == 3)) for j in range(4)]
                        self.group("pe", fns, [ur] + [DG[d][j].r() for j in range(4)], [bc.r()])
                        self.op("act", lambda bc=bc, UCB=UCB, bof=bof: nc.scalar.activation(out=UCB.ap[:, 0:SB], in_=bc.ap, func=AF.Identity,
                                                                                          bias=V.ap[:, bof:bof + 1], scale=1.0),
                                [bc.r(), V.r()], [UCB.r(0, SB)])
                        for g, dst in enumerate([R, I]):
                            bi = self.psrot("x")
                            b = self.bank(bi, 0, SB)
                            bd = self.BD[d][g]
                            hb = M["HBV"].ap[:, g * 8 + d * 4 + c:g * 8 + d * 4 + c + 1]
                            self.group("pe", [lambda b=b, bd=bd, UCB=UCB: nc.tensor.matmul(b.ap, lhsT=bd.ap, rhs=UCB.ap[:, 0:SB], start=True, stop=True)],
                                       [bd.r(), UCB.r(0, SB)], [b.r()])
                            self.op("act", lambda b=b, dst=dst, hb=hb: nc.scalar.activation(out=dst.ap[:, 0:SB], in_=b.ap, func=AF.Tanh,
                                                                                          bias=hb, scale=0.5),
                                    [b.r(), M["HBV"].r()], [dst.r(0, SB)])
                        self.op("act", lambda R=R, hc=hc: nc.scalar.activation(out=R.ap[:, 0:SB], in_=R.ap[:, 0:SB], func=AF.Exp, scale=hc, bias=hc),
                                [R.r(0, SB), M["CST"].r()], [R.r(0, SB)])
                        self.op("dve", lambda R=R, G=G: nc.vector.tensor_tensor(out=G.ap[:, 0:SB], in0=R.ap[:, 0:SB], in1=R.ap[:, 0:SB], op=ALU.mult),
                                [R.r(0, SB)], [G.r(0, SB)])
                    for d, sb in subs:
                        G = sets[d]["G"]
                        self.op("act", lambda G=G: nc.scalar.activation(out=G.ap[:, 0:SB], in_=G.ap[:, 0:SB], func=AF.Sqrt, scale=-0.25,
                                                                        bias=self.vcol("quarter", 0)),
                                [G.r(0, SB), V.r()], [G.r(0, SB)])
                    for d, sb in subs:
                        S = sets[d]
                        UC, R, I, G = S["UC"], S["R"], S["I"], S["G"]
                        H = UC
                        tl = sb * SB
                        tg = s0 + tl
                        UCB = S["UCB"]
                        self.op("dve", lambda I=I, UCB=UCB: nc.vector.scalar_tensor_tensor(out=I.ap[:, 0:SB], in0=I.ap[:, 0:SB], scalar=1.0,
                                                                                          in1=UCB.ap[:, 0:SB], op0=ALU.add, op1=ALU.mult),
                                [I.r(0, SB), UCB.r(0, SB)], [I.r(0, SB)])
                        self.op("dve", lambda I=I, G=G: nc.vector.tensor_tensor(out=I.ap[:, 0:SB], in0=I.ap[:, 0:SB], in1=G.ap[:, 0:SB], op=ALU.mult),
                                [I.r(0, SB), G.r(0, SB)], [I.r(0, SB)])
                        if d == 0:
                            out_ap, a_ap, b_ap = H.ap[:, 0:SB], R.ap[:, 0:SB], I.ap[:, 0:SB]
                            last_col = H.ap[:, SB - 1:SB]
                        else:
                            out_ap, a_ap, b_ap = H.ap[:, 0:SB][:, ::-1], R.ap[:, 0:SB][:, ::-1], I.ap[:, 0:SB][:, ::-1]
                            last_col = H.ap[:, 0:1]
                        init = CAR[d].ap if has_carry[d] else 0.0
                        rds = [R.r(0, SB), I.r(0, SB)] + ([CAR[d].r()] if has_carry[d] else [])
                        self.op("dve", lambda out_ap=out_ap, a_ap=a_ap, b_ap=b_ap, init=init: nc.vector.tensor_tensor_scan(
                            out=out_ap, data0=a_ap, data1=b_ap, initial=init, op0=ALU.mult, op1=ALU.add), rds, [H.r(0, SB)])
                        is_last = (i == nsb - 1)
                        if not is_last:
                            self.op("dve", lambda d=d, last_col=last_col: nc.vector.tensor_copy(out=CAR[d].ap, in_=last_col),
                                    [H.r(0, SB)], [CAR[d].r()])
                            has_carry[d] = True
                        elif not sample:
                            col = ((si * DEPTH + l) * 2 + d) * 4 + c
                            self.op("dve", lambda col=col, last_col=last_col: nc.vector.tensor_copy(out=self.NH.ap[:, col:col + 1], in_=last_col),
                                    [H.r(0, SB)], [self.NH.r(col, col + 1)])
                        if not written[sb]:
                            written[sb] = True
                            self.op("act", lambda H=H, tg=tg: nc.scalar.copy(out=HS.ap[:, tg:tg + SB], in_=H.ap[:, 0:SB]),
                                    [H.r(0, SB)], [HS.r(tg, tg + SB)])
                        else:
                            self.op("dve", lambda H=H, G=G, tg=tg: nc.vector.tensor_tensor(out=G.ap[:, 0:SB], in0=H.ap[:, 0:SB], in1=HS.ap[:, tg:tg + SB], op=ALU.add),
                                    [H.r(0, SB), HS.r(tg, tg + SB)], [G.r(0, SB)])
                            self.op("dve", lambda G=G, tg=tg, c=c: nc.vector.tensor_tensor(out=self.FT.ap[:, c, tg:tg + SB], in0=G.ap[:, 0:SB],
                                                                                         in1=SG.ap[:, tg:tg + SB], op=ALU.mult),
                                    [G.r(0, SB), SG.r(tg, tg + SB)], [self.FT.r(c * TMAX + tg, c * TMAX + tg + SB)])

    def branch_conv(self, l, T, segs):
        nc = self.nc
        NB = T // 512
        V = self.VEC
        L = segs[0][1]
        SB = min(512, L)
        nsb = L // SB
        o = 0
        PR = self.scr(o, len(segs) * (L + 2), F32); o += 8704
        CB = self.scr(o, TMAX, F32); o += 8192
        SG = self.scr(o, TMAX, BF16); o += 4096
        CCT = [self.scr(o + i * 2048, 512, F32) for i in range(2)]; o += 4096
        CV = [self.scr(o + i * 2048, 512, F32) for i in range(2)]; o += 4096
        assert o <= self.SCR_BYTES
        n = 0
        for c in range(4):
            wcb = self.wload(self.win(l, 3072 + c * 128), 8)
            wcc = self.wload(self.win(l, 3584 + c * 128), 8)
            wch = self.wload(self.win(l, 4096 + c * 128), 8)
            wgc = self.wload(self.win(l, 4608 + c * 128), 8)
            for si in range(len(segs)):
                base = si * (L + 2)
                self.op("dve", lambda base=base: nc.vector.memset(PR.ap[:, base:base + 1], 0.0), [], [PR.r(base, base + 1)])
                self.op("dve", lambda base=base: nc.vector.memset(PR.ap[:, base + 1 + L:base + 2 + L], 0.0), [], [PR.r(base + 1 + L, base + 2 + L)])
            for blk in range(NB):
                t0 = blk * 512
                b1 = self.proj_fm(wcc, 8, self.XM, t0, 512, self.psrot("proj"))
                cct = CCT[blk % 2]
                self.op("act", lambda b1=b1, cct=cct: nc.scalar.copy(out=cct.ap, in_=b1.ap), [b1.r()], [cct.r()])
                b2 = self.proj_fm(wch, 8, self.XM, t0, 512, self.psrot("proj"))
                for si, (s0, sl) in enumerate(segs):
                    lo = max(s0, t0)
                    hi = min(s0 + sl, t0 + 512)
                    if lo >= hi:
                        continue
                    po_ = si * (L + 2) + 1 + (lo - s0)
                    self.op("dve", lambda b2=b2, cct=cct, lo=lo, hi=hi, po_=po_: nc.vector.tensor_tensor(
                        out=PR.ap[:, po_:po_ + hi - lo], in0=b2.ap[:, lo - t0:hi - t0], in1=cct.ap[:, lo - t0:hi - t0], op=ALU.mult),
                        [b2.r(), cct.r()], [PR.r(po_, po_ + hi - lo)])
                b3 = self.proj_fm(wcb, 8, self.XM, t0, 512, self.psrot("proj"))
                self.op("act", lambda b3=b3, t0=t0: nc.scalar.copy(out=CB.ap[:, t0:t0 + 512], in_=b3.ap), [b3.r()], [CB.r(t0, t0 + 512)])
                b4 = self.proj_fm(wgc, 8, self.XM, t0, 512, self.psrot("proj"))
                self.op("act", lambda b4=b4, t0=t0: nc.scalar.activation(out=SG.ap[:, t0:t0 + 512], in_=b4.ap, func=AF.Silu),
                        [b4.r()], [SG.r(t0, t0 + 512)])
            wof = VOFF["cw"] + (l * 3) * 4 + c
            for si, (s0, sl) in enumerate(segs):
                pbase = si * (L + 2) + 1
                for sb in range(nsb):
                    cv = CV[n % 2]
                    n += 1
                    tl = sb * SB
                    tg = s0 + tl
                    p0 = pbase + tl - 1
                    pr = PR.r(p0, p0 + SB + 2)
                    self.op("dve", lambda p0=p0, cv=cv: nc.vector.tensor_scalar(
                        out=cv.ap[:, 0:SB], in0=PR.ap[:, p0:p0 + SB], scalar1=V.ap[:, wof:wof + 1], scalar2=None, op0=ALU.mult),
                        [pr, V.r()], [cv.r(0, SB)])
                    for j in range(1, 3):
                        self.op("dve", lambda p0=p0, cv=cv, j=j: nc.vector.scalar_tensor_tensor(
                            out=cv.ap[:, 0:SB], in0=PR.ap[:, p0 + j:p0 + j + SB], scalar=V.ap[:, wof + 4 * j:wof + 4 * j + 1],
                            in1=cv.ap[:, 0:SB], op0=ALU.mult, op1=ALU.add), [pr, V.r(), cv.r(0, SB)], [cv.r(0, SB)])
                    self.op("dve", lambda cv=cv, tg=tg: nc.vector.tensor_tensor(out=cv.ap[:, 0:SB], in0=cv.ap[:, 0:SB], in1=CB.ap[:, tg:tg + SB], op=ALU.mult),
                            [cv.r(0, SB), CB.r(tg, tg + SB)], [cv.r(0, SB)])
                    self.op("dve", lambda cv=cv, tg=tg, c=c: nc.vector.tensor_tensor(out=self.FT.ap[:, c, tg:tg + SB], in0=cv.ap[:, 0:SB],
                                                                                   in1=SG.ap[:, tg:tg + SB], op=ALU.mult),
                            [cv.r(0, SB), SG.r(tg, tg + SB)], [self.FT.r(c * TMAX + tg, c * TMAX + tg + SB)])

    def final(self, ydram, T):
        nc = self.nc
        self.bmap = "norm"
        NB = T // 512
        V = self.VEC
        YF = self.scr(8192, 8 * 512, F32, (8, 512))
        YO = [self.scr(8192 + 16384 + i * 4096, 1024, F32) for i in range(2)]
        n = 0
        for blk in range(NB):
            RS = self.rstd_block(blk, 0)
            t0 = blk * 512
            for c in range(8):
                fg = VOFF["final_g"] + c
                self.op("dve", lambda c=c, fg=fg: nc.vector.scalar_tensor_tensor(
                    out=YF.ap[:, c, :], in0=self.XT.ap[:, c, t0:t0 + 512], scalar=V.ap[:, fg:fg + 1], in1=RS.ap, op0=ALU.mult, op1=ALU.mult),
                    [self.XT.r(c * TMAX + t0, c * TMAX + t0 + 512), V.r(), RS.r()], [YF.r(c * 512, (c + 1) * 512)])
            for ti in range(4):
                yo = YO[n % 2]
                n += 1
                for half in range(2):
                    bi = self.psrot("proj")
                    b = self.bank(bi)
                    fns = [lambda c=c, b=b, ti=ti: nc.tensor.transpose(b.ap[:, (c % 4) * 128:(c % 4 + 1) * 128],
                                                                       YF.ap[:, c, ti * 128:(ti + 1) * 128], self.IDF.ap)
                           for c in range(half * 4, half * 4 + 4)]
                    self.group("pe", fns, [YF.r(), self.IDF.r()], [b.r()])
                    if half == 0:
                        self.op("dve", lambda b=b, yo=yo: nc.vector.tensor_copy(out=yo.ap[:, 0:512], in_=b.ap), [b.r()], [yo.r(0, 512)])
                    else:
                        self.op("act", lambda b=b, yo=yo: nc.scalar.copy(out=yo.ap[:, 512:1024], in_=b.ap), [b.r()], [yo.r(512, 1024)])
                tok = t0 + ti * 128
                self.dma("sp", [(ydram[tok:tok + 128, :], yo.ap)], [yo.r()], [])


def _pvec(v):
    v = np.asarray(v, np.float32)
    return np.ascontiguousarray(v.reshape(-1, 128).T)


def _build_tab(rpb):
    H = rpb.shape[0]
    qc = np.arange(64)
    kc = np.arange(64)
    ws = np.clip(qc - 8, 0, 48)
    valid = (kc[:, None] >= ws[None, :]) & (kc[:, None] < ws[None, :] + 16)
    cidx = np.clip(kc[:, None] - qc[None, :] + 15, 0, 30)
    out = np.full((2, H, 128, 16, 64), NEG, np.float32)
    for ty in range(2):
        for krl in range(2):
            for j in range(16):
                e = j - 1 - krl
                if e < 0 or e > 14:
                    continue
                if ty == 0 and not (4 <= e <= 11):
                    continue
                d = 14 - e
                g = rpb[:, d, :][:, cidx]
                out[ty, :, krl * 64:(krl + 1) * 64, j, :] = np.where(valid[None], g, np.float32(NEG))
    return out.reshape(2, H, 128, 1024)


_NC_CACHE = {}


def kernel(x_prompt, x_sample, cache_k, cache_v, state_lru, c, c_ctx, norm_g, w_mod, b_mod,
           w_in, na_rpb, lru_conv_w, lru_conv_b, lru_wa, lru_ba, lru_wx, lru_bx, lru_lam,
           conv_w, w_br_na, w_br_lru, w_br_conv, w_out, final_g):
    f32 = lambda a: np.ascontiguousarray(np.asarray(a, dtype=np.float32))
    x_prompt, x_sample, cache_k, cache_v = f32(x_prompt), f32(x_sample), f32(cache_k), f32(cache_v)
    state_lru, c, c_ctx = f32(state_lru), f32(c), f32(c_ctx)
    w_mod, w_in, w_out = f32(w_mod), f32(w_in), f32(w_out)
    w_br = np.ascontiguousarray(np.stack([f32(w_br_na), f32(w_br_lru), f32(w_br_conv)], axis=1))
    lwa, lwx = f32(lru_wa), f32(lru_wx)
    na_rpb = f32(na_rpb)
    tab = np.ascontiguousarray(np.stack([_build_tab(na_rpb[l]) for l in range(DEPTH)], axis=0))
    ident = np.eye(128, dtype=np.float32)

    base = np.zeros((128, NV), np.float32)
    base[:, VOFF["cond_p"]:VOFF["cond_p"] + 8] = _pvec(c_ctx)
    base[:, VOFF["final_g"]:VOFF["final_g"] + 8] = _pvec(final_g)
    for l in range(DEPTH):
        base[:, VOFF["norm_g"] + l * 8:VOFF["norm_g"] + l * 8 + 8] = _pvec(norm_g[l])
        base[:, VOFF["b_mod"] + l * 24:VOFF["b_mod"] + l * 24 + 24] = _pvec(b_mod[l])
        for d in range(2):
            for j in range(4):
                o = VOFF["lcw"] + ((l * 2 + d) * 4 + j) * 4
                base[:, o:o + 4] = _pvec(np.asarray(lru_conv_w)[l, d, j])
            for nm, arr in [("lcb", lru_conv_b), ("lba", lru_ba), ("lbx", lru_bx), ("lam", lru_lam)]:
                o = VOFF[nm] + (l * 2 + d) * 4
                base[:, o:o + 4] = _pvec(np.asarray(arr)[l, d])
        for j in range(3):
            o = VOFF["cw"] + (l * 3 + j) * 4
            base[:, o:o + 4] = _pvec(np.asarray(conv_w)[l, j])
    base[:, VOFF["eps"]] = 1e-6
    base[:, VOFF["one"]] = 1.0
    base[:, VOFF["quarter"]] = 0.25

    in_maps = []
    for core in range(8):
        b = core % 2
        vec = base.copy()
        vec[:, VOFF["cond_s"]:VOFF["cond_s"] + 8] = _pvec(c[b])
        for l in range(DEPTH):
            for d in range(2):
                o = VOFF["st"] + (l * 2 + d) * 4
                vec[:, o:o + 4] = _pvec(state_lru[b, l, d])
        in_maps.append({
            "xs": x_sample[b],
            "xp": np.ascontiguousarray(x_prompt[2 * core:2 * core + 2].reshape(TP, D)),
            "ck": np.ascontiguousarray(cache_k[b].reshape(DEPTH, 256, 512)),
            "cv": np.ascontiguousarray(cache_v[b].reshape(DEPTH, 256, 512)),
            "vecs": vec,
            "w_mod": w_mod, "w_in": w_in, "w_br": w_br, "w_out": w_out,
            "lwa": lwa, "lwx": lwx, "tab": tab, "ident": ident,
        })
    if "nc" not in _NC_CACHE:
        _NC_CACHE["nc"] = Builder().build()
    nc = _NC_CACHE["nc"]
    res = run_bass_kernel_spmd(nc, in_maps, core_ids=list(range(8)))
    rs = res.results
    y_prompt = np.concatenate([rs[i]["yp"].reshape(2, 256, D) for i in range(8)], axis=0)
    y_sample = np.stack([rs[0]["ys"], rs[1]["ys"]], axis=0)
    nk = np.concatenate([rs[i]["nk"].reshape(2, DEPTH, 256, 8, 64) for i in range(8)], axis=0)
    nv = np.concatenate([rs[i]["nv"].reshape(2, DEPTH, 256, 8, 64) for i in range(8)], axis=0)
    nhs = []
    for i in range(8):
        h = rs[i]["nh"].reshape(128, 2, DEPTH, 2, 4)
        nhs.append(np.transpose(h, (1, 2, 3, 4, 0)).reshape(2, DEPTH, 2, 512))
    nh = np.concatenate(nhs, axis=0)
    return (y_prompt.astype(np.float32), y_sample.astype(np.float32), nk.astype(np.float32),
            nv.astype(np.float32), nh.astype(np.float32))
```

```python
import numpy as np
import concourse.bass as bass
import concourse.mybir as mybir
from concourse.bass_utils import run_bass_kernel_spmd
from contextlib import ExitStack

F32 = mybir.dt.float32
BF16 = mybir.dt.bfloat16
AF = mybir.ActivationFunctionType
ALU = mybir.AluOpType

DEPTH = 4
D = 1024
TS = 2048
TP = 512
TMAX = 2048
NEG = -30000.0
PAGE = 512
NDS = 40
NW = 6
DBG = {"phases": "SP", "layers": DEPTH, "stages": "MNABCO", "att": 9, "merge": 1, "kvout": 2}

VOFF = {}
_o = 0
for _n, _sz in [("cond_s", 8), ("cond_p", 8), ("final_g", 8), ("norm_g", 32), ("b_mod", 96),
                ("lcw", 128), ("lcb", 32), ("lba", 32), ("lbx", 32), ("lam", 32), ("cw", 48),
                ("st", 32), ("eps", 1), ("one", 1), ("quarter", 1)]:
    VOFF[_n] = _o
    _o += _sz
NV = _o


class Buf:
    def __init__(self, ap, space, off, esz, nel):
        self.ap = ap
        self.space = space
        self.off = off
        self.esz = esz
        self.nel = nel

    def r(self, lo=0, hi=None):
        if hi is None:
            hi = self.nel
        return (self.space, self.off + lo * self.esz, self.off + hi * self.esz)


class Builder:
    def __init__(self):
        self.nc = bass.Bass("TRN2", target_bir_lowering=False)
        self.es = ExitStack()
        nc = self.nc
        self.engs = {"pe": nc.tensor, "act": nc.scalar, "dve": nc.vector, "pool": nc.gpsimd, "sp": nc.sync}
        self.sems = []
        self.semval = []
        self.esem = {}
        for e in self.engs:
            self.esem[e] = self._newsem("e_" + e)
        self.dsems = {q: [self._newsem(f"d{q}{i}") for i in range(NDS // 2)] for q in ("sp", "pool")}
        self.dnext = {"sp": 0, "pool": 0}
        self.waited = {e: {} for e in self.engs}
        self.pages = {}
        self.ninstr = 0

    def _newsem(self, name):
        s = self.es.enter_context(self.nc.semaphore(name))
        self.sems.append(s)
        self.semval.append(0)
        return len(self.sems) - 1

    @staticmethod
    def _pg(reg):
        sp, lo, hi = reg
        if sp == "ps":
            return [(sp, p) for p in range(lo // 2048, (hi - 1) // 2048 + 1)]
        return [(sp, p) for p in range(lo // PAGE, (hi - 1) // PAGE + 1)]

    @staticmethod
    def _split(reads, writes):
        r2 = [r for r in reads if r[0] != "ps"]
        w2 = list(writes) + [r for r in reads if r[0] == "ps"]
        return r2, w2

    def _waits(self, engine, reads, writes, extra=()):
        need = {}

        def add(sv):
            if sv is None:
                return
            s, v = sv
            if need.get(s, 0) < v:
                need[s] = v
        for r in reads:
            for p in self._pg(r):
                st = self.pages.get(p)
                if st:
                    add(st[0])
        for w in writes:
            for p in self._pg(w):
                st = self.pages.get(p)
                if st:
                    add(st[0])
                    for s, v in st[1].items():
                        add((s, v))
        for sv in extra:
            add(sv)
        wd = self.waited[engine]
        own = self.esem[engine]
        eng = self.engs[engine]
        for s, v in need.items():
            if engine == "pe" and s == own:
                continue
            if wd.get(s, 0) >= v:
                continue
            eng.wait_ge(self.sems[s], v)
            wd[s] = v
            self.ninstr += 1

    def _update(self, s, v, reads, writes):
        for r in reads:
            for p in self._pg(r):
                st = self.pages.get(p)
                if st is None:
                    st = [None, {}]
                    self.pages[p] = st
                st[1][s] = v
        for w in writes:
            for p in self._pg(w):
                self.pages[p] = [(s, v), {}]

    def op(self, engine, fn, reads, writes):
        reads, writes = self._split(reads, writes)
        self._waits(engine, reads, writes)
        ins = fn()
        s = self.esem[engine]
        self.semval[s] += 1
        ins.then_inc(self.sems[s], 1)
        self._update(s, self.semval[s], reads, writes)
        self.ninstr += 1

    def group(self, engine, fns, reads, writes):
        reads, writes = self._split(reads, writes)
        self._waits(engine, reads, writes)
        ins = None
        for fn in fns:
            ins = fn()
            self.ninstr += 1
        s = self.esem[engine]
        self.semval[s] += 1
        ins.then_inc(self.sems[s], 1)
        self._update(s, self.semval[s], reads, writes)

    def dma(self, queue, pairs, reads, writes):
        d = self.dsems[queue][self.dnext[queue]]
        self.dnext[queue] = (self.dnext[queue] + 1) % (NDS // 2)
        extra = [(d, self.semval[d])] if self.semval[d] > 0 else []
        self._waits(queue, reads, writes, extra)
        eng = self.engs[queue]
        for out, in_ in pairs:
            ins = eng.dma_start(out=out, in_=in_)
            self.semval[d] += 16
            ins.then_inc(self.sems[d], 16)
            self.ninstr += 1
        self._update(d, self.semval[d], reads, writes)

    def finish(self):
        sp = self.engs["sp"]
        for s in range(len(self.sems)):
            if self.semval[s] > 0 and self.waited["sp"].get(s, 0) < self.semval[s]:
                sp.wait_ge(self.sems[s], self.semval[s])

    def alloc_arena(self):
        nc = self.nc
        self.ARENA_BYTES = 211968
        self.arena = self.es.enter_context(nc.sbuf_tensor("arena", [128, self.ARENA_BYTES // 4], F32))
        self.psb = [self.es.enter_context(nc.psum_tensor(f"psb{i}", [128, 512], F32)) for i in range(8)]

    def view(self, off, nel, dt, shape=None):
        esz = 4 if dt == F32 else 2
        nb = nel * esz
        assert off % 4 == 0 and nb % 4 == 0, (off, nel)
        assert off + nb <= self.ARENA_BYTES, (off, nb)
        ap = self.arena[:, off // 4:(off + nb) // 4]
        if dt != F32:
            ap = ap.bitcast(dt)
        if shape is not None:
            if len(shape) == 2:
                ap = ap.rearrange("p (a b) -> p a b", a=shape[0], b=shape[1])
            elif len(shape) == 3:
                ap = ap.rearrange("p (a b c) -> p a b c", a=shape[0], b=shape[1], c=shape[2])
        return Buf(ap, "sb", off, esz, nel)

    def bank(self, i, lo=0, hi=512, dt=F32):
        ap = self.psb[i][:, lo:hi]
        nel = hi - lo
        esz = 4
        if dt != F32:
            ap = ap.bitcast(dt)
            nel *= 2
            esz = 2
        return Buf(ap, "ps", i * 2048 + lo * 4, esz, nel)

    def build(self):
        nc = self.nc
        dr = {}

        def din(name, shape):
            dr[name] = nc.dram_tensor(name, list(shape), F32, kind="ExternalInput").ap()

        def dout(name, shape):
            dr[name] = nc.dram_tensor(name, list(shape), F32, kind="ExternalOutput").ap()
        din("xs", [TS, D])
        din("xp", [TP, D])
        din("ck", [DEPTH, 256, 512])
        din("cv", [DEPTH, 256, 512])
        din("vecs", [128, NV])
        din("w_mod", [DEPTH, D, 3 * D])
        din("w_in", [DEPTH, D, 8192])
        din("w_br", [DEPTH, 3, 512, D])
        din("w_out", [DEPTH, D, D])
        din("lwa", [DEPTH, 2, 8, 64, 64])
        din("lwx", [DEPTH, 2, 8, 64, 64])
        din("tab", [DEPTH, 2, 8, 128, 1024])
        din("ident", [128, 128])
        dout("ys", [TS, D])
        dout("yp", [TP, D])
        dout("nk", [2, DEPTH, 256, 512])
        dout("nv", [2, DEPTH, 256, 512])
        dout("nh", [128, 64])
        self.dr = dr

        self.alloc_arena()
        o = 0
        self.XT = self.view(o, 8 * TMAX, F32, (8, TMAX)); o += 8 * TMAX * 4
        self.XM = self.view(o, 8 * TMAX, BF16, (8, TMAX)); o += 8 * TMAX * 2
        self.MG = self.view(o, 8 * TMAX, BF16, (8, TMAX)); o += 8 * TMAX * 2
        self.FT = self.view(o, 4 * TMAX, BF16, (4, TMAX)); o += 4 * TMAX * 2
        self.IDF = self.view(o, 128, F32); o += 512
        self.IDB = self.view(o, 128, BF16); o += 512
        self.ONESB = self.view(o, 128, BF16); o += 512
        self.VEC = self.view(o, 512, F32); o += 2048
        self.SC = self.view(o, 8, BF16); o += 512
        self.MODS = []
        for i in range(2):
            self.MODS.append({"MODV": self.view(o, 24, F32), "AV": self.view(o + 128, 8, F32), "CST": self.view(o + 192, 8, F32),
                              "HBV": self.view(o + 256, 16, F32)})
            o += 512
        self.modbank = None
        self.NH = self.view(o, 64, F32); o += 512
        self.RC = [self.view(o + i * 512, 1, F32) for i in range(2)]; o += 1024
        self.QZ = [self.view(o + i * 512, 256, BF16) for i in range(2)]; o += 1024
        self.BD = [[self.view(o + (d * 2 + g) * 512, 128, BF16) for g in range(2)] for d in range(2)]; o += 2048
        self.WS = [self.view(o + i * 2048, 1024, BF16, (8, 128)) for i in range(NW)]; o += NW * 2048
        self.wnext = 0
        self.SCR = o
        self.SCR_BYTES = self.ARENA_BYTES - o
        assert self.SCR_BYTES >= 41984, self.SCR_BYTES
        self.rot = {}

        V = self.VEC
        self.dma("sp", [(V.ap[:, 0:NV], dr["vecs"][:, :])], [], [V.r()])
        self.dma("sp", [(self.IDF.ap, dr["ident"][:, :])], [], [self.IDF.r()])
        self.dma("pool", [(self.IDB.ap, dr["ident"][:, :])], [], [self.IDB.r()])
        self.op("dve", lambda: nc.vector.memset(self.ONESB.ap, 1.0), [], [self.ONESB.r()])
        self.op("dve", lambda: nc.vector.memset(self.NH.ap, 0.0), [], [self.NH.r()])
        for i in range(2):
            self.op("dve", lambda i=i: nc.vector.memset(self.QZ[i].ap, 0.0), [], [self.QZ[i].r()])
        for d in range(2):
            for g in range(2):
                b = self.BD[d][g]
                self.op("dve", lambda b=b: nc.vector.memset(b.ap, 0.0), [], [b.r()])

        segs_s = [(0, 2048)]
        segs_p = [(0, 256), (256, 256)]
        if "S" in DBG["phases"]:
            self.phase("S", dr["xs"], dr["ys"], TS, segs_s, VOFF["cond_s"])
        if "P" in DBG["phases"]:
            self.phase("P", dr["xp"], dr["yp"], TP, segs_p, VOFF["cond_p"])
        self.dma("sp", [(dr["nh"][:, :], self.NH.ap)], [self.NH.r()], [])
        self.finish()
        return nc

    def vcol(self, name, idx, n=1):
        o = VOFF[name] + idx
        return self.VEC.ap[:, o:o + n]

    BANKMAPS = {
        "att": {"proj": [0, 1], "s": [2, 3], "po": [4, 5], "x": [6, 7]},
        "attcore": {"proj": [0, 1], "s": [0, 1, 2, 3], "po": [4, 5], "x": [6, 7]},
        "wide": {"proj": [0, 1, 2, 3, 4, 5], "x": [6, 7]},
        "lru": {"proj": [0, 1], "x": [2, 3, 4, 5, 6], "m": [7]},
        "norm": {"proj": [0, 1, 2, 3], "x": [4, 5, 6, 7]},
    }

    def psrot(self, cls):
        banks = self.BANKMAPS[getattr(self, "bmap", "norm")][cls]
        i = self.rot.get(cls, 0)
        self.rot[cls] = i + 1
        return banks[i % len(banks)]

    def wload(self, src, nk):
        w = self.WS[self.wnext]
        self.wnext = (self.wnext + 1) % NW
        self.dma("pool", [(w.ap[:, 0:nk, :], src)], [], [w.r(0, nk * 128)])
        return w

    def win(self, l, col):
        return self.dr["w_in"][l].rearrange("(k p) n -> p k n", p=128)[:, :, col:col + 128]

    def proj_fm(self, w, nk, src, t0, n, bank_i):
        nc = self.nc
        b = self.bank(bank_i, 0, n)
        fns = []
        reads = [w.r(0, nk * 128)]
        for k in range(nk):
            fns.append(lambda k=k: nc.tensor.matmul(b.ap, lhsT=w.ap[:, k, :], rhs=src.ap[:, k, t0:t0 + n],
                                                    start=(k == 0), stop=(k == nk - 1)))
            reads.append(src.r(k * TMAX + t0, k * TMAX + t0 + n))
        self.group("pe", fns, reads, [b.r()])
        return b

    def scr(self, off, nel, dt, shape=None):
        return self.view(self.SCR + off, nel, dt, shape)

    def phase(self, ph, xdram, ydram, T, segs, cond_off):
        nc = self.nc
        NB = T // 512
        NTT = T // 128
        self.T = T
        self.ph = ph
        self.bmap = "norm"
        XIN = [self.scr(i * 4096, 1024, F32) for i in range(2)]
        for tt in range(NTT):
            xin = XIN[tt % 2]
            self.dma("sp", [(xin.ap, xdram[tt * 128:(tt + 1) * 128, :])], [], [xin.r()])
            for half in range(2):
                bi = self.psrot("proj")
                b = self.bank(bi)
                fns = [lambda c=c, b=b, xin=xin: nc.tensor.transpose(
                    b.ap[:, (c % 4) * 128:(c % 4 + 1) * 128], xin.ap[:, c * 128:(c + 1) * 128], self.IDF.ap)
                    for c in range(half * 4, half * 4 + 4)]
                self.group("pe", fns, [xin.r(), self.IDF.r()], [b.r()])
                out = self.XT.ap[:, half * 4:half * 4 + 4, tt * 128:(tt + 1) * 128]
                in_ = b.ap.rearrange("p (c t) -> p c t", c=4)
                wr = [self.XT.r(c * TMAX + tt * 128, c * TMAX + (tt + 1) * 128) for c in range(half * 4, half * 4 + 4)]
                eng = "dve" if half == 0 else "act"
                if eng == "dve":
                    self.op("dve", lambda out=out, in_=in_: nc.vector.tensor_copy(out=out, in_=in_), [b.r()], wr)
                else:
                    self.op("act", lambda out=out, in_=in_: nc.scalar.copy(out=out, in_=in_), [b.r()], wr)
        for l in range(DBG["layers"]):
            self.layer(l, T, segs, cond_off)
        self.final(ydram, T)

    def rstd_block(self, blk, off):
        nc = self.nc
        SQ = [self.scr(off + i * 1024, 512, BF16) for i in range(2)] + [self.scr(off + 6144 + i * 1024, 512, BF16) for i in range(2)]
        TMP = self.scr(off + 2048, 512, F32)
        RS = self.scr(off + 4096, 512, F32)
        t0 = blk * 512
        bi = self.psrot("x")
        b = self.bank(bi)
        for c in range(8):
            sq = SQ[c % 4]
            xr = self.XT.r(c * TMAX + t0, c * TMAX + t0 + 512)
            if c not in (1, 3, 5):
                self.op("act", lambda c=c, sq=sq: nc.scalar.activation(out=sq.ap, in_=self.XT.ap[:, c, t0:t0 + 512], func=AF.Square),
                        [xr], [sq.r()])
            else:
                self.op("pool", lambda c=c, sq=sq: nc.gpsimd.tensor_tensor(out=sq.ap, in0=self.XT.ap[:, c, t0:t0 + 512],
                                                                          in1=self.XT.ap[:, c, t0:t0 + 512], op=ALU.mult),
                        [xr], [sq.r()])
            self.group("pe", [lambda c=c, sq=sq: nc.tensor.matmul(b.ap, lhsT=self.ONESB.ap, rhs=sq.ap, start=(c == 0), stop=(c == 7))],
                       [sq.r(), self.ONESB.r()], [b.r()])
        self.op("act", lambda: nc.scalar.activation(out=TMP.ap, in_=b.ap, func=AF.Sqrt, scale=1.0 / D, bias=self.vcol("eps", 0)),
                [b.r(), self.VEC.r()], [TMP.r()])
        self.op("dve", lambda: nc.vector.reciprocal(out=RS.ap, in_=TMP.ap), [TMP.r()], [RS.r()])
        return RS

    def mod_part(self, key, cond_off, slot, part):
        nc = self.nc
        dr = self.dr
        V = self.VEC
        l = key[1]
        M = self.MODS[slot]
        if part == 0:
            self.op("act", lambda: nc.scalar.activation(out=self.SC.ap, in_=V.ap[:, cond_off:cond_off + 8], func=AF.Silu),
                    [V.r()], [self.SC.r()])
        bm = self.bank(7, 0, 24)
        wmod = dr["w_mod"][l].rearrange("(k p) n -> p k n", p=128)
        for j in range(part * 6, part * 6 + 6):
            w = self.wload(wmod[:, :, j * 128:(j + 1) * 128], 8)
            fns = [lambda k=k, w=w, j=j: nc.tensor.matmul(bm.ap[:, j:j + 1], lhsT=w.ap[:, k, :], rhs=self.SC.ap[:, k:k + 1],
                                                          start=(k == 0), stop=(k == 7)) for k in range(8)]
            self.group("pe", fns, [w.r(), self.SC.r()], [bm.r()])
        if part < 3:
            return
        MODV, AV, CST, HBV = M["MODV"], M["AV"], M["CST"], M["HBV"]
        ob = VOFF["b_mod"] + l * 24
        self.op("dve", lambda: nc.vector.tensor_tensor(out=MODV.ap, in0=bm.ap, in1=V.ap[:, ob:ob + 24], op=ALU.add),
                [bm.r(), V.r()], [MODV.r()])
        og = VOFF["norm_g"] + l * 8
        self.op("dve", lambda: nc.vector.scalar_tensor_tensor(out=AV.ap, in0=MODV.ap[:, 8:16], scalar=1.0,
                                                              in1=V.ap[:, og:og + 8], op0=ALU.add, op1=ALU.mult),
                [MODV.r(), V.r()], [AV.r()])
        self.op("dve", lambda: nc.vector.tensor_scalar(out=MODV.ap[:, 16:24], in0=MODV.ap[:, 16:24], scalar1=0.5, scalar2=None, op0=ALU.mult),
                [MODV.r()], [MODV.r()])
        ol = VOFF["lam"] + l * 8
        self.op("act", lambda: nc.scalar.activation(out=CST.ap, in_=V.ap[:, ol:ol + 8], func=AF.Exp, scale=-1.0),
                [V.r()], [CST.r()])
        self.op("act", lambda: nc.scalar.activation(out=CST.ap, in_=CST.ap, func=AF.Ln, bias=self.vcol("one", 0), scale=1.0),
                [CST.r(), V.r()], [CST.r()])
        self.op("dve", lambda: nc.vector.tensor_scalar(out=CST.ap, in0=CST.ap, scalar1=-4.0, scalar2=None, op0=ALU.mult),
                [CST.r()], [CST.r()])
        for gi, nm in enumerate(["lba", "lbx"]):
            ov = VOFF[nm] + l * 8
            self.op("dve", lambda gi=gi, ov=ov: nc.vector.tensor_scalar(out=HBV.ap[:, gi * 8:gi * 8 + 8], in0=V.ap[:, ov:ov + 8], scalar1=0.5,
                                                                       scalar2=None, op0=ALU.mult), [V.r()], [HBV.r()])
        self.mod_ready = key

    def layer(self, l, T, segs, cond_off):
        nc = self.nc
        dr = self.dr
        NB = T // 512
        NTT = T // 128
        V = self.VEC
        sample = (self.ph == "S")

        key = (self.ph, l)
        slot = l % 2
        if getattr(self, "mod_ready", None) != key:
            for part in range(4):
                self.mod_part(key, cond_off, slot, part)
        M = self.MODS[slot]
        self.M = M
        if l + 1 < DBG["layers"]:
            self.next_mod = ((self.ph, l + 1), cond_off, (l + 1) % 2)
        elif self.ph == "S" and "P" in DBG["phases"] and DBG["layers"] > 0:
            self.next_mod = (("P", 0), VOFF["cond_p"], 0)
        else:
            self.next_mod = None

        self.bmap = "norm"
        T1 = [self.scr(8192 + i * 2048, 512, F32) for i in range(4)]
        for blk in range(NB):
            RS = self.rstd_block(blk, 0)
            t0 = blk * 512
            for c in range(8):
                t1 = T1[c % 4]
                xr = self.XT.r(c * TMAX + t0, c * TMAX + t0 + 512)
                self.op("dve", lambda c=c, t1=t1: nc.vector.scalar_tensor_tensor(
                    out=t1.ap, in0=self.XT.ap[:, c, t0:t0 + 512], scalar=M["AV"].ap[:, c:c + 1], in1=RS.ap,
                    op0=ALU.mult, op1=ALU.mult), [xr, M["AV"].r(), RS.r()], [t1.r()])
                if c not in (1, 3, 5):
                    self.op("act", lambda c=c, t1=t1: nc.scalar.activation(
                        out=self.XM.ap[:, c, t0:t0 + 512], in_=t1.ap, func=AF.Identity, bias=M["MODV"].ap[:, c:c + 1], scale=1.0),
                        [t1.r(), M["MODV"].r()], [self.XM.r(c * TMAX + t0, c * TMAX + t0 + 512)])
                else:
                    self.op("pool", lambda c=c, t1=t1: nc.gpsimd.tensor_scalar(
                        out=self.XM.ap[:, c, t0:t0 + 512], in0=t1.ap, scalar1=self.vcol("one", 0), scalar2=M["MODV"].ap[:, c:c + 1],
                        op0=ALU.mult, op1=ALU.add),
                        [t1.r(), M["MODV"].r(), V.r()], [self.XM.r(c * TMAX + t0, c * TMAX + t0 + 512)])

        st_ = DBG["stages"]
        self.mg_init = False
        if "A" in st_:
            self.bmap = "att"
            self.branch_attention(l, T, segs)
            self.merge(l, 0, T)
        if "B" in st_:
            self.bmap = "lru"
            self.branch_lru(l, T, segs)
            self.merge(l, 1, T)
        if "C" in st_:
            self.bmap = "wide"
            self.branch_conv(l, T, segs)
            self.merge(l, 2, T)
        if "O" not in st_:
            return
        self.bmap = "wide"
        wout = dr["w_out"][l].rearrange("(k p) n -> p k n", p=128)
        for j in range(8):
            w = self.wload(wout[:, :, j * 128:(j + 1) * 128], 8)
            for blk in range(NB):
                t0 = blk * 512
                b = self.proj_fm(w, 8, self.MG, t0, 512, self.psrot("proj"))
                xr = self.XT.r(j * TMAX + t0, j * TMAX + t0 + 512)
                self.op("dve", lambda j=j, b=b, t0=t0: nc.vector.scalar_tensor_tensor(
                    out=self.XT.ap[:, j, t0:t0 + 512], in0=b.ap, scalar=M["MODV"].ap[:, 16 + j:17 + j],
                    in1=self.XT.ap[:, j, t0:t0 + 512], op0=ALU.mult, op1=ALU.add),
                    [b.r(), M["MODV"].r(), xr], [xr])

    def merge(self, l, br, T):
        if not DBG["merge"]:
            return
        self.bmap = "wide"
        nc = self.nc
        dr = self.dr
        NB = T // 512
        SGM = [self.scr(i * 2048, 512, F32) for i in range(2)]
        TMPM = [self.scr(4096 + i * 2048, 512, F32) for i in range(2)]
        wbr = dr["w_br"][l, br].rearrange("(k p) n -> p k n", p=128)
        n = 0
        for j in range(8):
            w1 = self.wload(wbr[:, :, j * 128:(j + 1) * 128], 4)
            w2 = self.wload(self.win(l, 5120 + br * 1024 + j * 128), 8)
            for blk in range(NB):
                t0 = blk * 512
                b1 = self.proj_fm(w1, 4, self.FT, t0, 512, self.psrot("proj"))
                b2 = self.proj_fm(w2, 8, self.XM, t0, 512, self.psrot("proj"))
                sg = SGM[n % 2]
                tm = TMPM[n % 2]
                n += 1
                self.op("act", lambda sg=sg, b2=b2: nc.scalar.activation(out=sg.ap, in_=b2.ap, func=AF.Tanh, scale=0.5),
                        [b2.r()], [sg.r()])
                mr = self.MG.r(j * TMAX + t0, j * TMAX + t0 + 512)
                mg = self.MG.ap[:, j, t0:t0 + 512]
                if not self.mg_init:
                    self.op("dve", lambda sg=sg, b1=b1, mg=mg: nc.vector.scalar_tensor_tensor(out=mg, in0=sg.ap, scalar=1.0, in1=b1.ap,
                                                                                            op0=ALU.add, op1=ALU.mult),
                            [b1.r(), sg.r()], [mr])
                else:
                    self.op("dve", lambda sg=sg, b1=b1, tm=tm: nc.vector.scalar_tensor_tensor(out=tm.ap, in0=sg.ap, scalar=1.0, in1=b1.ap,
                                                                                            op0=ALU.add, op1=ALU.mult),
                            [b1.r(), sg.r()], [tm.r()])
                    self.op("dve", lambda tm=tm, mg=mg: nc.vector.tensor_tensor(out=mg, in0=mg, in1=tm.ap, op=ALU.add),
                            [tm.r(), mr], [mr])
        self.mg_init = True

    def branch_attention(self, l, T, segs):
        nc = self.nc
        dr = self.dr
        NB = T // 512
        NTT = T // 128
        sample = (self.ph == "S")
        o = 0
        QT = self.scr(o, TMAX, BF16); o += 4096
        KT = self.scr(o, TMAX, BF16); o += 4096
        SG = self.scr(o, TMAX, BF16); o += 4096
        OTOK = self.scr(o, 16 * 128, BF16, (16, 128)); o += 4096
        VAUG = self.scr(o, 16 * 130, BF16, (16, 2, 65)); o += 4608
        ES = [self.scr(o + i * 4096, 8 * 256, BF16, (8, 256)) for i in range(2)]
        CKS = self.scr(o, 2 * 512, F32, (2, 512))
        o += 8192
        TAB = self.scr(o, 2 * 2 * 1024, BF16, (2, 2, 1024)); o += 8192
        KCT = self.scr(o, 4 * 256, BF16, (4, 256)); o += 2048
        VC = self.scr(o, 2 * 8 * 65, BF16, (2, 8, 65)); o += 2560
        assert o <= self.SCR_BYTES, o
        KVO = [[self.scr(TAB.off - self.SCR + kv * 2048, 512, F32, (4, 128)) for kv in range(2)]]

        if sample:
            self.dma("sp", [(CKS.ap, dr["ck"][l].rearrange("(u p) f -> p u f", p=128))], [], [CKS.r()])
            for u in range(2):
                bi = self.psrot("proj")
                b = self.bank(bi)
                fns = [lambda hp=hp, u=u, b=b: nc.tensor.transpose(b.ap[:, hp * 128:(hp + 1) * 128],
                                                                   CKS.ap[:, u, hp * 128:(hp + 1) * 128], self.IDF.ap) for hp in range(4)]
                self.group("pe", fns, [CKS.r(), self.IDF.r()], [b.r()])
                self.op("dve", lambda u=u, b=b: nc.vector.tensor_copy(out=KCT.ap[:, :, u * 128:(u + 1) * 128],
                                                                      in_=b.ap.rearrange("p (h t) -> p h t", h=4)),
                        [b.r()], [KCT.r()])
            self.op("dve", lambda: nc.vector.memset(VC.ap[:, :, :, 64:65], 1.0), [], [VC.r()])
            self.dma("pool", [(VC.ap[:, u, :, 0:64], dr["cv"][l][u * 128:(u + 1) * 128, :].rearrange("p (h d) -> p h d", d=64)) for u in range(2)],
                     [], [VC.r()])
        self.op("dve", lambda: nc.vector.memset(VAUG.ap[:, :, :, 64:65], 1.0), [], [VAUG.r()])

        for hp in range(4):
            self.bmap = "att"
            wq = self.wload(self.win(l, hp * 128), 8)
            wk = self.wload(self.win(l, 512 + hp * 128), 8)
            wv = self.wload(self.win(l, 1024 + hp * 128), 8)
            wg = self.wload(self.win(l, 1536 + hp * 128), 8)
            if sample:
                self.dma("pool", [(TAB.ap[:, ty], dr["tab"][l, ty, 2 * hp:2 * hp + 2].rearrange("h p n -> p h n")) for ty in range(2)],
                         [], [TAB.r()])
            for blk in range(NB):
                t0 = blk * 512
                b = self.proj_fm(wq, 8, self.XM, t0, 512, self.psrot("proj"))
                self.op("act", lambda b=b, t0=t0: nc.scalar.activation(out=QT.ap[:, t0:t0 + 512], in_=b.ap, func=AF.Copy, scale=0.125),
                        [b.r()], [QT.r(t0, t0 + 512)])
                b = self.proj_fm(wk, 8, self.XM, t0, 512, self.psrot("proj"))
                self.op("dve", lambda b=b, t0=t0: nc.vector.tensor_copy(out=KT.ap[:, t0:t0 + 512], in_=b.ap),
                        [b.r()], [KT.r(t0, t0 + 512)])
                b = self.proj_fm(wg, 8, self.XM, t0, 512, self.psrot("proj"))
                self.op("act", lambda b=b, t0=t0: nc.scalar.activation(out=SG.ap[:, t0:t0 + 512], in_=b.ap, func=AF.Silu),
                        [b.r()], [SG.r(t0, t0 + 512)])
            for g4 in range(NTT // 4 if DBG["att"] >= 1 else 0):
                for kv in ([1] if sample else [0, 1]):
                    w = wv if kv == 1 else wk
                    bi = self.psrot("x")
                    b = self.bank(bi)
                    fns = []
                    reads = [w.r()]
                    for i in range(4):
                        tt = g4 * 4 + i
                        for k in range(8):
                            fns.append(lambda i=i, tt=tt, k=k, w=w, b=b: nc.tensor.matmul(
                                b.ap[:, i * 128:(i + 1) * 128], lhsT=self.XM.ap[:, k, tt * 128:(tt + 1) * 128], rhs=w.ap[:, k, :],
                                start=(k == 0), stop=(k == 7)))
                    for k in range(8):
                        reads.append(self.XM.r(k * TMAX + g4 * 512, k * TMAX + g4 * 512 + 512))
                    self.group("pe", fns, reads, [b.r()])
                    if kv == 1:
                        self.op("dve", lambda b=b, g4=g4: nc.vector.tensor_copy(
                            out=VAUG.ap[:, g4 * 4:g4 * 4 + 4, :, 0:64], in_=b.ap.rearrange("p (t h d) -> p t h d", t=4, h=2)),
                            [b.r()], [VAUG.r(g4 * 4 * 130, (g4 * 4 + 4) * 130)])
                    if not sample and DBG["kvout"]:
                        st = KVO[0][kv]
                        self.op("act", lambda b=b, st=st: nc.scalar.copy(out=st.ap, in_=b.ap.rearrange("p (t c) -> p t c", t=4)),
                                [b.r()], [st.r()])
                        dn = dr["nk" if kv == 0 else "nv"]
                        if DBG["kvout"] >= 2:
                            self.dma("sp", [(dn[s_, l, :, hp * 128:(hp + 1) * 128].rearrange("(u p) c -> p u c", p=128), st.ap[:, 2 * s_:2 * s_ + 2, :])
                                            for s_ in range(2)], [st.r()], [])
            items = []
            if sample:
                for a in range(8):
                    if a == 0:
                        krs = [0, 2, 4, 6]
                    elif a == 7:
                        krs = [24, 26, 28, 30]
                    else:
                        krs = list(range(4 * a - 4, 4 * a + 7, 2))
                    ty = 1 if a in (0, 7) else 0
                    chunks = [("loc", kr // 2, ty, 8 - (kr - 4 * a)) for kr in krs] + [("ctx", 0), ("ctx", 1)]
                    for hh in range(2):
                        items.append((a * 256, hh, chunks))
            else:
                for s in range(2):
                    chunks = [("loc", 2 * s, None, None), ("loc", 2 * s + 1, None, None)]
                    for hh in range(2):
                        items.append((s * 256, hh, chunks))

            def emit_qk(it, es):
                q0, hh, chunks = it
                hb = hh * 64
                nch = len(chunks)
                qz = self.QZ[hh]
                self.op("dve", lambda qz=qz, hb=hb, q0=q0: nc.vector.tensor_copy(out=qz.ap[hb:hb + 64, :], in_=QT.ap[hb:hb + 64, q0:q0 + 256]),
                        [QT.r(q0, q0 + 256)], [qz.r()])
                for pair in range(0, nch, 2):
                    n2 = min(2, nch - pair)
                    bi = self.psrot("s")
                    b = self.bank(bi, 0, n2 * 256)
                    fns = []
                    reads = [qz.r()]
                    for i in range(n2):
                        ch = chunks[pair + i]
                        oap = b.ap[:, i * 256:(i + 1) * 256]
                        if ch[0] == "loc":
                            tt = ch[1]
                            kap = KT.ap[:, tt * 128:(tt + 1) * 128]
                            reads.append(KT.r(tt * 128, (tt + 1) * 128))
                            hasb = ch[2] is not None
                        else:
                            kap = KCT.ap[:, hp, ch[1] * 128:(ch[1] + 1) * 128]
                            reads.append(KCT.r())
                            hasb = False
                        fns.append(lambda oap=oap, kap=kap, hasb=hasb: nc.tensor.matmul(
                            oap, lhsT=kap, rhs=qz.ap, start=True, stop=(not hasb)))
                        if hasb:
                            ty, s = ch[2], ch[3]
                            bap = TAB.ap[:, ty, hh, s * 64:s * 64 + 256]
                            reads.append(TAB.r())
                            fns.append(lambda oap=oap, bap=bap: nc.tensor.matmul(oap, lhsT=self.IDB.ap, rhs=bap, start=False, stop=True))
                    reads.append(self.IDB.r())
                    self.group("pe", fns, reads, [b.r()])
                    self.op("act", lambda b=b, pair=pair, n2=n2, es=es: nc.scalar.activation(
                        out=es.ap[:, pair:pair + n2, :], in_=b.ap.rearrange("p (c q) -> p c q", c=n2), func=AF.Exp),
                        [b.r()], [es.r(pair * 256, (pair + n2) * 256)])

            def emit_pv(it, es, idx):
                q0, hh, chunks = it
                hb = hh * 64
                nch = len(chunks)
                for half in range(2):
                    slot = self.rot.get("poslot", 0)
                    self.rot["poslot"] = slot + 1
                    po = self.bank(self.psrot("po"), 0, 65)
                    fns = []
                    reads = [es.r(0, nch * 256)]
                    for ci, ch in enumerate(chunks):
                        if ch[0] == "loc":
                            vap = VAUG.ap[:, ch[1], hh, :]
                            reads.append(VAUG.r(ch[1] * 130, (ch[1] + 1) * 130))
                        else:
                            vap = VC.ap[:, ch[1], 2 * hp + hh, :]
                            reads.append(VC.r())
                        fns.append(lambda ci=ci, vap=vap, po=po, half=half: nc.tensor.matmul(
                            po.ap, lhsT=es.ap[:, ci, half * 128:(half + 1) * 128], rhs=vap, start=(ci == 0), stop=(ci == nch - 1)))
                    self.group("pe", fns, reads, [po.r()])
                    rc = self.RC[slot % 2]
                    self.op("dve", lambda po=po, rc=rc: nc.vector.reciprocal(out=rc.ap, in_=po.ap[:, 64:65]), [po.r()], [rc.r()])
                    tt = (q0 + half * 128) // 128
                    self.op("dve", lambda po=po, rc=rc, tt=tt: nc.vector.tensor_scalar(
                        out=OTOK.ap[:, tt, hb:hb + 64], in0=po.ap[:, 0:64], scalar1=rc.ap, scalar2=None, op0=ALU.mult),
                        [po.r(), rc.r()], [OTOK.r(tt * 128, (tt + 1) * 128)])
                if hh == 1:
                    tt0 = q0 // 128
                    bt = self.bank(self.psrot("x"), 0, 128, BF16)
                    fns = [lambda i=i: nc.tensor.transpose(bt.ap[:, i * 128:(i + 1) * 128], OTOK.ap[:, tt0 + i, :], self.IDB.ap)
                           for i in range(2)]
                    self.group("pe", fns, [OTOK.r(tt0 * 128, (tt0 + 2) * 128), self.IDB.r()], [bt.r()])
                    self.op("dve", lambda: nc.vector.tensor_tensor(out=self.FT.ap[:, hp, q0:q0 + 256], in0=bt.ap,
                                                                   in1=SG.ap[:, q0:q0 + 256], op=ALU.mult),
                            [bt.r(), SG.r(q0, q0 + 256)], [self.FT.r(hp * TMAX + q0, hp * TMAX + q0 + 256)])

            if DBG["att"] < 2:
                continue
            self.bmap = "attcore"
            emit_qk(items[0], ES[0])
            for i in range(len(items) if DBG["att"] >= 3 else 0):
                if i + 1 < len(items):
                    emit_qk(items[i + 1], ES[(i + 1) % 2])
                emit_pv(items[i], ES[i % 2], i)

    def branch_lru(self, l, T, segs):
        nc = self.nc
        dr = self.dr
        NB = T // 512
        sample = (self.ph == "S")
        V = self.VEC
        M = self.M
        L = segs[0][1]
        SB = min(512, L)
        nsb = L // SB
        o = 0
        U = self.scr(o, len(segs) * (L + 6), BF16); o += 4608
        HS = self.scr(o, TMAX, F32); o += 8192
        SG = self.scr(o, TMAX, BF16); o += 4096
        sets = []
        for i in range(2):
            d_ = {}
            for nm in ["UC", "R", "I", "G"]:
                d_[nm] = self.scr(o, 512, F32); o += 2048
            d_["UCB"] = self.scr(o, 512, BF16); o += 1024
            sets.append(d_)
        DG = [[self.scr(o + (d * 4 + j) * 256, 128, BF16) for j in range(4)] for d in range(2)]; o += 2048
        CAR = self.RC
        assert o <= self.SCR_BYTES, o
        for c in range(4):
            wu = self.wload(self.win(l, 2048 + c * 128), 8)
            wg = self.wload(self.win(l, 2560 + c * 128), 8)
            for d in range(2):
                for g, nm in enumerate(["lwa", "lwx"]):
                    bd = self.BD[d][g]
                    self.dma("pool", [(bd.ap[0:64, 0:64], dr[nm][l, d, 2 * c]), (bd.ap[64:128, 64:128], dr[nm][l, d, 2 * c + 1])],
                             [], [bd.r()])
            for d in range(2):
                for j in range(4):
                    wc = VOFF["lcw"] + ((l * 2 + d) * 4 + j) * 4 + c
                    self.op("dve", lambda d=d, j=j, wc=wc: nc.vector.tensor_scalar(out=DG[d][j].ap, in0=self.IDB.ap, scalar1=V.ap[:, wc:wc + 1],
                                                                                 scalar2=None, op0=ALU.mult),
                            [self.IDB.r(), V.r()], [DG[d][j].r()])
            for si in range(len(segs)):
                base = si * (L + 6)
                self.op("dve", lambda base=base: nc.vector.memset(U.ap[:, base:base + 3], 0.0), [], [U.r(base, base + 3)])
                self.op("dve", lambda base=base: nc.vector.memset(U.ap[:, base + 3 + L:base + 6 + L], 0.0), [],
                        [U.r(base + 3 + L, base + 6 + L)])
            for blk in range(NB):
                t0 = blk * 512
                b = self.proj_fm(wu, 8, self.XM, t0, 512, self.psrot("proj"))
                for si, (s0, sl) in enumerate(segs):
                    lo = max(s0, t0)
                    hi = min(s0 + sl, t0 + 512)
                    if lo >= hi:
                        continue
                    uo = si * (L + 6) + 3 + (lo - s0)
                    self.op("dve", lambda b=b, lo=lo, hi=hi, uo=uo: nc.vector.tensor_copy(out=U.ap[:, uo:uo + hi - lo], in_=b.ap[:, lo - t0:hi - t0]),
                            [b.r()], [U.r(uo, uo + hi - lo)])
                b = self.proj_fm(wg, 8, self.XM, t0, 512, self.psrot("proj"))
                self.op("act", lambda b=b, t0=t0: nc.scalar.activation(out=SG.ap[:, t0:t0 + 512], in_=b.ap, func=AF.Silu),
                        [b.r()], [SG.r(t0, t0 + 512)])
            if self.next_mod is not None:
                self.mod_part(self.next_mod[0], self.next_mod[1], self.next_mod[2], c)
            for si, (s0, sl) in enumerate(segs):
                ubase = si * (L + 6) + 3
                written = [False] * nsb
                has_carry = [sample, sample]
                if sample:
                    for d in range(2):
                        so_ = VOFF["st"] + (l * 2 + d) * 4 + c
                        self.op("dve", lambda d=d, so_=so_: nc.vector.tensor_copy(out=CAR[d].ap, in_=V.ap[:, so_:so_ + 1]),
                                [V.r()], [CAR[d].r()])
                for i in range(nsb):
                    subs = [(0, i), (1, nsb - 1 - i)]
                    for d, sb in subs:
                        S = sets[d]
                        UC, R, I, G, UCB = S["UC"], S["R"], S["I"], S["G"], S["UCB"]
                        wof = VOFF["lcw"] + ((l * 2 + d) * 4) * 4 + c
                        bof = VOFF["lcb"] + (l * 2 + d) * 4 + c
                        hc = M["CST"].ap[:, d * 4 + c:d * 4 + c + 1]
                        tl = sb * SB
                        shift = -3 if d == 0 else 0
                        ur = U.r(ubase + tl - 3, ubase + tl + SB + 3)
                        u0 = ubase + tl + shift
                        bi = self.psrot("x")
                        bc = self.bank(bi, 0, SB)
                        fns = [lambda j=j, bc=bc, u0=u0, d=d: nc.tensor.matmul(bc.ap, lhsT=DG[d][j].ap, rhs=U.ap[:, u0 + j:u0 + j + SB],
                                                                              start=(j == 0), stop=(j == 3)) for j in range(4)]
                        self.group("pe", fns, [ur] + [DG[d][j].r() for j in range(4)], [bc.r()])
                        self.op("act", lambda bc=bc, UCB=UCB, bof=bof: nc.scalar.activation(out=UCB.ap[:, 0:SB], in_=bc.ap, func=AF.Identity,
                                                                                          bias=V.ap[:, bof:bof + 1], scale=1.0),
                                [bc.r(), V.r()], [UCB.r(0, SB)])
                        for g, dst in enumerate([R, I]):
                            bi = self.psrot("x")
                            b = self.bank(bi, 0, SB)
                            bd = self.BD[d][g]
                            hb = M["HBV"].ap[:, g * 8 + d * 4 + c:g * 8 + d * 4 + c + 1]
                            self.group("pe", [lambda b=b, bd=bd, UCB=UCB: nc.tensor.matmul(b.ap, lhsT=bd.ap, rhs=UCB.ap[:, 0:SB], start=True, stop=True)],
                                       [bd.r(), UCB.r(0, SB)], [b.r()])
                            self.op("act", lambda b=b, dst=dst, hb=hb: nc.scalar.activation(out=dst.ap[:, 0:SB], in_=b.ap, func=AF.Tanh,
                                                                                          bias=hb, scale=0.5),
                                    [b.r(), M["HBV"].r()], [dst.r(0, SB)])
                        self.op("act", lambda R=R, hc=hc: nc.scalar.activation(out=R.ap[:, 0:SB], in_=R.ap[:, 0:SB], func=AF.Exp, scale=hc, bias=hc),
                                [R.r(0, SB), M["CST"].r()], [R.r(0, SB)])
                        self.op("dve", lambda R=R, G=G: nc.vector.tensor_tensor(out=G.ap[:, 0:SB], in0=R.ap[:, 0:SB], in1=R.ap[:, 0:SB], op=ALU.mult),
                                [R.r(0, SB)], [G.r(0, SB)])
                    for d, sb in subs:
                        G = sets[d]["G"]
                        self.op("act", lambda G=G: nc.scalar.activation(out=G.ap[:, 0:SB], in_=G.ap[:, 0:SB], func=AF.Sqrt, scale=-0.25,
                                                                        bias=self.vcol("quarter", 0)),
                                [G.r(0, SB), V.r()], [G.r(0, SB)])
                    for d, sb in subs:
                        S = sets[d]
                        UC, R, I, G = S["UC"], S["R"], S["I"], S["G"]
                        H = UC
                        tl = sb * SB
                        tg = s0 + tl
                        UCB = S["UCB"]
                        self.op("dve", lambda I=I, UCB=UCB: nc.vector.scalar_tensor_tensor(out=I.ap[:, 0:SB], in0=I.ap[:, 0:SB], scalar=1.0,
                                                                                          in1=UCB.ap[:, 0:SB], op0=ALU.add, op1=ALU.mult),
                                [I.r(0, SB), UCB.r(0, SB)], [I.r(0, SB)])
                        self.op("dve", lambda I=I, G=G: nc.vector.tensor_tensor(out=I.ap[:, 0:SB], in0=I.ap[:, 0:SB], in1=G.ap[:, 0:SB], op=ALU.mult),
                                [I.r(0, SB), G.r(0, SB)], [I.r(0, SB)])
                        if d == 0:
                            out_ap, a_ap, b_ap = H.ap[:, 0:SB], R.ap[:, 0:SB], I.ap[:, 0:SB]
                            last_col = H.ap[:, SB - 1:SB]
                        else:
                            out_ap, a_ap, b_ap = H.ap[:, 0:SB][:, ::-1], R.ap[:, 0:SB][:, ::-1], I.ap[:, 0:SB][:, ::-1]
                            last_col = H.ap[:, 0:1]
                        init = CAR[d].ap if has_carry[d] else 0.0
                        rds = [R.r(0, SB), I.r(0, SB)] + ([CAR[d].r()] if has_carry[d] else [])
                        self.op("dve", lambda out_ap=out_ap, a_ap=a_ap, b_ap=b_ap, init=init: nc.vector.tensor_tensor_scan(
                            out=out_ap, data0=a_ap, data1=b_ap, initial=init, op0=ALU.mult, op1=ALU.add), rds, [H.r(0, SB)])
                        is_last = (i == nsb - 1)
                        if not is_last:
                            self.op("dve", lambda d=d, last_col=last_col: nc.vector.tensor_copy(out=CAR[d].ap, in_=last_col),
                                    [H.r(0, SB)], [CAR[d].r()])
                            has_carry[d] = True
                        elif not sample:
                            col = ((si * DEPTH + l) * 2 + d) * 4 + c
                            self.op("dve", lambda col=col, last_col=last_col: nc.vector.tensor_copy(out=self.NH.ap[:, col:col + 1], in_=last_col),
                                    [H.r(0, SB)], [self.NH.r(col, col + 1)])
                        if not written[sb]:
                            written[sb] = True
                            self.op("act", lambda H=H, tg=tg: nc.scalar.copy(out=HS.ap[:, tg:tg + SB], in_=H.ap[:, 0:SB]),
                                    [H.r(0, SB)], [HS.r(tg, tg + SB)])
                        else:
                            self.op("dve", lambda H=H, G=G, tg=tg: nc.vector.tensor_tensor(out=G.ap[:, 0:SB], in0=H.ap[:, 0:SB], in1=HS.ap[:, tg:tg + SB], op=ALU.add),
                                    [H.r(0, SB), HS.r(tg, tg + SB)], [G.r(0, SB)])
                            self.op("dve", lambda G=G, tg=tg, c=c: nc.vector.tensor_tensor(out=self.FT.ap[:, c, tg:tg + SB], in0=G.ap[:, 0:SB],
                                                                                         in1=SG.ap[:, tg:tg + SB], op=ALU.mult),
                                    [G.r(0, SB), SG.r(tg, tg + SB)], [self.FT.r(c * TMAX + tg, c * TMAX + tg + SB)])

    def branch_conv(self, l, T, segs):
        nc = self.nc
        NB = T // 512
        V = self.VEC
        L = segs[0][1]
        SB = min(512, L)
        nsb = L // SB
        o = 0
        PR = self.scr(o, len(segs) * (L + 2), F32); o += 8704
        CB = self.scr(o, TMAX, F32); o += 8192
        SG = self.scr(o, TMAX, BF16); o += 4096
        CCT = [self.scr(o + i * 2048, 512, F32) for i in range(2)]; o += 4096
        CV = [self.scr(o + i * 2048, 512, F32) for i in range(2)]; o += 4096
        assert o <= self.SCR_BYTES
        n = 0
        for c in range(4):
            wcb = self.wload(self.win(l, 3072 + c * 128), 8)
            wcc = self.wload(self.win(l, 3584 + c * 128), 8)
            wch = self.wload(self.win(l, 4096 + c * 128), 8)
            wgc = self.wload(self.win(l, 4608 + c * 128), 8)
            for si in range(len(segs)):
                base = si * (L + 2)
                self.op("dve", lambda base=base: nc.vector.memset(PR.ap[:, base:base + 1], 0.0), [], [PR.r(base, base + 1)])
                self.op("dve", lambda base=base: nc.vector.memset(PR.ap[:, base + 1 + L:base + 2 + L], 0.0), [], [PR.r(base + 1 + L, base + 2 + L)])
            for blk in range(NB):
                t0 = blk * 512
                b1 = self.proj_fm(wcc, 8, self.XM, t0, 512, self.psrot("proj"))
                cct = CCT[blk % 2]
                self.op("act", lambda b1=b1, cct=cct: nc.scalar.copy(out=cct.ap, in_=b1.ap), [b1.r()], [cct.r()])
                b2 = self.proj_fm(wch, 8, self.XM, t0, 512, self.psrot("proj"))
                for si, (s0, sl) in enumerate(segs):
                    lo = max(s0, t0)
                    hi = min(s0 + sl, t0 + 512)
                    if lo >= hi:
                        continue
                    po_ = si * (L + 2) + 1 + (lo - s0)
                    self.op("dve", lambda b2=b2, cct=cct, lo=lo, hi=hi, po_=po_: nc.vector.tensor_tensor(
                        out=PR.ap[:, po_:po_ + hi - lo], in0=b2.ap[:, lo - t0:hi - t0], in1=cct.ap[:, lo - t0:hi - t0], op=ALU.mult),
                        [b2.r(), cct.r()], [PR.r(po_, po_ + hi - lo)])
                b3 = self.proj_fm(wcb, 8, self.XM, t0, 512, self.psrot("proj"))
                self.op("act", lambda b3=b3, t0=t0: nc.scalar.copy(out=CB.ap[:, t0:t0 + 512], in_=b3.ap), [b3.r()], [CB.r(t0, t0 + 512)])
                b4 = self.proj_fm(wgc, 8, self.XM, t0, 512, self.psrot("proj"))
                self.op("act", lambda b4=b4, t0=t0: nc.scalar.activation(out=SG.ap[:, t0:t0 + 512], in_=b4.ap, func=AF.Silu),
                        [b4.r()], [SG.r(t0, t0 + 512)])
            wof = VOFF["cw"] + (l * 3) * 4 + c
            for si, (s0, sl) in enumerate(segs):
                pbase = si * (L + 2) + 1
                for sb in range(nsb):
                    cv = CV[n % 2]
                    n += 1
                    tl = sb * SB
                    tg = s0 + tl
                    p0 = pbase + tl - 1
                    pr = PR.r(p0, p0 + SB + 2)
                    self.op("dve", lambda p0=p0, cv=cv: nc.vector.tensor_scalar(
                        out=cv.ap[:, 0:SB], in0=PR.ap[:, p0:p0 + SB], scalar1=V.ap[:, wof:wof + 1], scalar2=None, op0=ALU.mult),
                        [pr, V.r()], [cv.r(0, SB)])
                    for j in range(1, 3):
                        self.op("dve", lambda p0=p0, cv=cv, j=j: nc.vector.scalar_tensor_tensor(
                            out=cv.ap[:, 0:SB], in0=PR.ap[:, p0 + j:p0 + j + SB], scalar=V.ap[:, wof + 4 * j:wof + 4 * j + 1],
                            in1=cv.ap[:, 0:SB], op0=ALU.mult, op1=ALU.add), [pr, V.r(), cv.r(0, SB)], [cv.r(0, SB)])
                    self.op("dve", lambda cv=cv, tg=tg: nc.vector.tensor_tensor(out=cv.ap[:, 0:SB], in0=cv.ap[:, 0:SB], in1=CB.ap[:, tg:tg + SB], op=ALU.mult),
                            [cv.r(0, SB), CB.r(tg, tg + SB)], [cv.r(0, SB)])
                    self.op("dve", lambda cv=cv, tg=tg, c=c: nc.vector.tensor_tensor(out=self.FT.ap[:, c, tg:tg + SB], in0=cv.ap[:, 0:SB],
                                                                                   in1=SG.ap[:, tg:tg + SB], op=ALU.mult),
                            [cv.r(0, SB), SG.r(tg, tg + SB)], [self.FT.r(c * TMAX + tg, c * TMAX + tg + SB)])

    def final(self, ydram, T):
        nc = self.nc
        self.bmap = "norm"
        NB = T // 512
        V = self.VEC
        YF = self.scr(8192, 8 * 512, F32, (8, 512))
        YO = [self.scr(8192 + 16384 + i * 4096, 1024, F32) for i in range(2)]
        n = 0
        for blk in range(NB):
            RS = self.rstd_block(blk, 0)
            t0 = blk * 512
            for c in range(8):
                fg = VOFF["final_g"] + c
                self.op("dve", lambda c=c, fg=fg: nc.vector.scalar_tensor_tensor(
                    out=YF.ap[:, c, :], in0=self.XT.ap[:, c, t0:t0 + 512], scalar=V.ap[:, fg:fg + 1], in1=RS.ap, op0=ALU.mult, op1=ALU.mult),
                    [self.XT.r(c * TMAX + t0, c * TMAX + t0 + 512), V.r(), RS.r()], [YF.r(c * 512, (c + 1) * 512)])
            for ti in range(4):
                yo = YO[n % 2]
                n += 1
                for half in range(2):
                    bi = self.psrot("proj")
                    b = self.bank(bi)
                    fns = [lambda c=c, b=b, ti=ti: nc.tensor.transpose(b.ap[:, (c % 4) * 128:(c % 4 + 1) * 128],
                                                                       YF.ap[:, c, ti * 128:(ti + 1) * 128], self.IDF.ap)
                           for c in range(half * 4, half * 4 + 4)]
                    self.group("pe", fns, [YF.r(), self.IDF.r()], [b.r()])
                    if half == 0:
                        self.op("dve", lambda b=b, yo=yo: nc.vector.tensor_copy(out=yo.ap[:, 0:512], in_=b.ap), [b.r()], [yo.r(0, 512)])
                    else:
                        self.op("act", lambda b=b, yo=yo: nc.scalar.copy(out=yo.ap[:, 512:1024], in_=b.ap), [b.r()], [yo.r(512, 1024)])
                tok = t0 + ti * 128
                self.dma("sp", [(ydram[tok:tok + 128, :], yo.ap)], [yo.r()], [])


def _pvec(v):
    v = np.asarray(v, np.float32)
    return np.ascontiguousarray(v.reshape(-1, 128).T)


def _build_tab(rpb):
    H = rpb.shape[0]
    qc = np.arange(64)
    kc = np.arange(64)
    ws = np.clip(qc - 8, 0, 48)
    valid = (kc[:, None] >= ws[None, :]) & (kc[:, None] < ws[None, :] + 16)
    cidx = np.clip(kc[:, None] - qc[None, :] + 15, 0, 30)
    out = np.full((2, H, 128, 16, 64), NEG, np.float32)
    for ty in range(2):
        for krl in range(2):
            for j in range(16):
                e = j - 1 - krl
                if e < 0 or e > 14:
                    continue
                if ty == 0 and not (4 <= e <= 11):
                    continue
                d = 14 - e
                g = rpb[:, d, :][:, cidx]
                out[ty, :, krl * 64:(krl + 1) * 64, j, :] = np.where(valid[None], g, np.float32(NEG))
    return out.reshape(2, H, 128, 1024)


_NC_CACHE = {}


def kernel(x_prompt, x_sample, cache_k, cache_v, state_lru, c, c_ctx, norm_g, w_mod, b_mod,
           w_in, na_rpb, lru_conv_w, lru_conv_b, lru_wa, lru_ba, lru_wx, lru_bx, lru_lam,
           conv_w, w_br_na, w_br_lru, w_br_conv, w_out, final_g):
    f32 = lambda a: np.ascontiguousarray(np.asarray(a, dtype=np.float32))
    x_prompt, x_sample, cache_k, cache_v = f32(x_prompt), f32(x_sample), f32(cache_k), f32(cache_v)
    state_lru, c, c_ctx = f32(state_lru), f32(c), f32(c_ctx)
    w_mod, w_in, w_out = f32(w_mod), f32(w_in), f32(w_out)
    w_br = np.ascontiguousarray(np.stack([f32(w_br_na), f32(w_br_lru), f32(w_br_conv)], axis=1))
    lwa, lwx = f32(lru_wa), f32(lru_wx)
    na_rpb = f32(na_rpb)
    tab = np.ascontiguousarray(np.stack([_build_tab(na_rpb[l]) for l in range(DEPTH)], axis=0))
    ident = np.eye(128, dtype=np.float32)

    base = np.zeros((128, NV), np.float32)
    base[:, VOFF["cond_p"]:VOFF["cond_p"] + 8] = _pvec(c_ctx)
    base[:, VOFF["final_g"]:VOFF["final_g"] + 8] = _pvec(final_g)
    for l in range(DEPTH):
        base[:, VOFF["norm_g"] + l * 8:VOFF["norm_g"] + l * 8 + 8] = _pvec(norm_g[l])
        base[:, VOFF["b_mod"] + l * 24:VOFF["b_mod"] + l * 24 + 24] = _pvec(b_mod[l])
        for d in range(2):
            for j in range(4):
                o = VOFF["lcw"] + ((l * 2 + d) * 4 + j) * 4
                base[:, o:o + 4] = _pvec(np.asarray(lru_conv_w)[l, d, j])
            for nm, arr in [("lcb", lru_conv_b), ("lba", lru_ba), ("lbx", lru_bx), ("lam", lru_lam)]:
                o = VOFF[nm] + (l * 2 + d) * 4
                base[:, o:o + 4] = _pvec(np.asarray(arr)[l, d])
        for j in range(3):
            o = VOFF["cw"] + (l * 3 + j) * 4
            base[:, o:o + 4] = _pvec(np.asarray(conv_w)[l, j])
    base[:, VOFF["eps"]] = 1e-6
    base[:, VOFF["one"]] = 1.0
    base[:, VOFF["quarter"]] = 0.25

    in_maps = []
    for core in range(8):
        b = core % 2
        vec = base.copy()
        vec[:, VOFF["cond_s"]:VOFF["cond_s"] + 8] = _pvec(c[b])
        for l in range(DEPTH):
            for d in range(2):
                o = VOFF["st"] + (l * 2 + d) * 4
                vec[:, o:o + 4] = _pvec(state_lru[b, l, d])
        in_maps.append({
            "xs": x_sample[b],
            "xp": np.ascontiguousarray(x_prompt[2 * core:2 * core + 2].reshape(TP, D)),
            "ck": np.ascontiguousarray(cache_k[b].reshape(DEPTH, 256, 512)),
            "cv": np.ascontiguousarray(cache_v[b].reshape(DEPTH, 256, 512)),
            "vecs": vec,
            "w_mod": w_mod, "w_in": w_in, "w_br": w_br, "w_out": w_out,
            "lwa": lwa, "lwx": lwx, "tab": tab, "ident": ident,
        })
    if "nc" not in _NC_CACHE:
        _NC_CACHE["nc"] = Builder().build()
    nc = _NC_CACHE["nc"]
    res = run_bass_kernel_spmd(nc, in_maps, core_ids=list(range(8)))
    rs = res.results
    y_prompt = np.concatenate([rs[i]["yp"].reshape(2, 256, D) for i in range(8)], axis=0)
    y_sample = np.stack([rs[0]["ys"], rs[1]["ys"]], axis=0)
    nk = np.concatenate([rs[i]["nk"].reshape(2, DEPTH, 256, 8, 64) for i in range(8)], axis=0)
    nv = np.concatenate([rs[i]["nv"].reshape(2, DEPTH, 256, 8, 64) for i in range(8)], axis=0)
    nhs = []
    for i in range(8):
        h = rs[i]["nh"].reshape(128, 2, DEPTH, 2, 4)
        nhs.append(np.transpose(h, (1, 2, 3, 4, 0)).reshape(2, DEPTH, 2, 512))
    nh = np.concatenate(nhs, axis=0)
    return (y_prompt.astype(np.float32), y_sample.astype(np.float32), nk.astype(np.float32),
            nv.astype(np.float32), nh.astype(np.float32))
```
